# Optimizing a Trainium2 kernel written in Bass

```python
import math
import jax, jax.numpy as jnp
from jax import lax
import numpy as np

D_MODEL = 1024
BATCH = 32
SEQ = 2048
DEPTH = 2
DEC_BATCH = 16
DEC_SEQ = 32
PAST_LEN = 4096

CHUNK = 64
N_MIXERS = 2
N_A_LAYERS = (DEPTH + 1) // 2
N_B_LAYERS = DEPTH // 2
LRU_WIDTH = D_MODEL
LRU_HEADS = 8
LRU_BLOCK = LRU_WIDTH // LRU_HEADS
LRU_CONV_WIDTH = 4
LRU_C = 8.0
CF_CONV_WIDTH = 31
PEER_HEADS = 8
PEER_NKEYS = 128
PEER_NEXPERTS = PEER_NKEYS * PEER_NKEYS
PEER_DK = 256
PEER_TOPK = 16
PEER_BLOCK = 256
EPS = 1e-6

kernel_name = "hybrid_rglru_conformer_peer_stream"


def rmsnorm(x, g):
    xf = x.astype(jnp.float32)
    y = xf * lax.rsqrt(jnp.mean(xf * xf, axis=-1, keepdims=True) + EPS)
    return (y * g.astype(jnp.float32)).astype(x.dtype)


def layernorm(x, g, b):
    xf = x.astype(jnp.float32)
    mu = jnp.mean(xf, axis=-1, keepdims=True)
    var = jnp.mean(jnp.square(xf - mu), axis=-1, keepdims=True)
    y = (xf - mu) * lax.rsqrt(var + EPS)
    return (y * g.astype(jnp.float32) + b.astype(jnp.float32)).astype(x.dtype)


def modulate(h, shift, scale):
    return h * (1 + scale[:, None, :]) + shift[:, None, :]


def causal_dwconv(xp, w, b):
    c = xp.shape[-1]
    y = lax.conv_general_dilated(xp, w[:, None, :].astype(xp.dtype), window_strides=(1,), padding="VALID",
                                 dimension_numbers=("NWC", "WIO", "NWC"), feature_group_count=c)
    return y + b


def _lin_combine(c1, c2):
    a1, b1 = c1
    a2, b2 = c2
    return a1 * a2, a2 * b1 + b2


def rglru_block(h, h0, conv_buf, w_in, b_in, conv_w, conv_b, ga_w, ga_b, gx_w, gx_b, lam, w_out, b_out):
    bsz, t, _ = h.shape
    proj = h @ w_in + b_in
    gate_branch, rec_branch = jnp.split(proj, 2, axis=-1)
    gate = jax.nn.gelu(gate_branch)
    xp = jnp.concatenate([conv_buf.astype(rec_branch.dtype), rec_branch], axis=1)
    new_buf = xp[:, -(LRU_CONV_WIDTH - 1):]
    xc = causal_dwconv(xp, conv_w, conv_b)
    xh = xc.reshape(bsz, t, LRU_HEADS, LRU_BLOCK)
    r = jax.nn.sigmoid(jnp.einsum("bthi,hij->bthj", xh, ga_w).reshape(bsz, t, LRU_WIDTH) + ga_b)
    i = jax.nn.sigmoid(jnp.einsum("bthi,hij->bthj", xh, gx_w).reshape(bsz, t, LRU_WIDTH) + gx_b)
    log_a = -LRU_C * r.astype(jnp.float32) * jax.nn.softplus(-lam.astype(jnp.float32))
    a = jnp.exp(log_a)
    mult = jnp.sqrt(-jnp.expm1(2.0 * log_a))
    bvals = mult * (i.astype(jnp.float32) * xc.astype(jnp.float32))
    bvals = bvals.at[:, 0].add(a[:, 0] * h0.astype(jnp.float32))
    _, hs = lax.associative_scan(_lin_combine, (a, bvals), axis=1)
    new_h = hs[:, -1].astype(h0.dtype)
    out = (hs.astype(h.dtype) * gate) @ w_out + b_out
    return out, new_h, new_buf


def conformer_conv_block(h, buf, w_pw1, b_pw1, dw_w, dw_b, ln_g, ln_b, w_pw2, b_pw2):
    proj = h @ w_pw1 + b_pw1
    val, gt = jnp.split(proj, 2, axis=-1)
    glu = val * jax.nn.sigmoid(gt)
    xp = jnp.concatenate([buf.astype(glu.dtype), glu], axis=1)
    new_buf = xp[:, -(CF_CONV_WIDTH - 1):]
    d = causal_dwconv(xp, dw_w, dw_b)
    d = jax.nn.silu(layernorm(d, ln_g, ln_b))
    out = d @ w_pw2 + b_pw2
    return out, new_buf


def peer(h, w_q, k1, k2, u, v):
    n = h.shape[0]
    nb = -(-n // PEER_BLOCK)
    hp = jnp.pad(h, ((0, nb * PEER_BLOCK - n), (0, 0))).reshape(nb, PEER_BLOCK, D_MODEL)
    half = PEER_DK // 2

    def block(hb):
        q = (hb @ w_q).reshape(PEER_BLOCK, PEER_HEADS, PEER_DK).astype(jnp.float32)
        s1 = jnp.einsum("thd,kd->thk", q[..., :half], k1.astype(jnp.float32))
        s2 = jnp.einsum("thd,kd->thk", q[..., half:], k2.astype(jnp.float32))
        s1v, s1i = lax.top_k(s1, PEER_TOPK)
        s2v, s2i = lax.top_k(s2, PEER_TOPK)
        cand = (s1v[..., :, None] + s2v[..., None, :]).reshape(PEER_BLOCK, PEER_HEADS, PEER_TOPK * PEER_TOPK)
        sv, si = lax.top_k(cand, PEER_TOPK)
        idx1 = jnp.take_along_axis(s1i, si // PEER_TOPK, axis=-1)
        idx2 = jnp.take_along_axis(s2i, si % PEER_TOPK, axis=-1)
        e = idx1 * PEER_NKEYS + idx2
        g = jax.nn.softmax(sv, axis=-1).astype(hb.dtype)
        ue = jnp.take(u, e, axis=0)
        ve = jnp.take(v, e, axis=0)
        z = jax.nn.gelu(jnp.einsum("thkd,td->thk", ue, hb))
        return jnp.einsum("thk,thkd->td", g * z, ve)

    return lax.map(block, hp).reshape(nb * PEER_BLOCK, D_MODEL)[:n]


def trunk(x, c, lru_h, lru_conv, dwconv, p):
    new_h, new_conv, new_dw = [], [], []
    for l in range(DEPTH):
        mod = jax.nn.silu(c) @ p["ada_w"][l] + p["ada_b"][l]
        sh1, sc1, g1, sh2, sc2, g2 = jnp.split(mod, 6, axis=-1)
        h = modulate(rmsnorm(x, p["norm_mix"][l]), sh1, sc1)
        j = l // N_MIXERS
        if l % N_MIXERS == 0:
            out, hn, cn = rglru_block(h, lru_h[j], lru_conv[j], p["lru_w_in"][j], p["lru_b_in"][j],
                                      p["lru_conv_w"][j], p["lru_conv_b"][j], p["lru_gate_a_w"][j],
                                      p["lru_gate_a_b"][j], p["lru_gate_x_w"][j], p["lru_gate_x_b"][j],
                                      p["lru_lambda"][j], p["lru_w_out"][j], p["lru_b_out"][j])
            new_h.append(hn)
            new_conv.append(cn)
        else:
            out, dn = conformer_conv_block(h, dwconv[j], p["cf_w_pw1"][j], p["cf_b_pw1"][j], p["cf_dw_w"][j],
                                           p["cf_dw_b"][j], p["cf_ln_g"][j], p["cf_ln_b"][j],
                                           p["cf_w_pw2"][j], p["cf_b_pw2"][j])
            new_dw.append(dn)
        x = x + g1[:, None, :] * out
        h = modulate(rmsnorm(x, p["norm_ffn"][l]), sh2, sc2)
        bsz, t, _ = h.shape
        ffn = peer(h.reshape(bsz * t, D_MODEL), p["peer_w_q"][l], p["peer_k1"][l], p["peer_k2"][l],
                   p["peer_u"][l], p["peer_v"][l]).reshape(bsz, t, D_MODEL)
        x = x + g2[:, None, :] * ffn
    y = rmsnorm(x, p["norm_final"])
    return y, jnp.stack(new_h), jnp.stack(new_conv), jnp.stack(new_dw)


def setup_inputs(seed: int = 0) -> dict:
    key = jax.random.key(seed)
    ks = iter(jax.random.split(key, 40))

    def nrm(shape, scale):
        return jax.random.normal(next(ks), shape, jnp.float32) * scale

    d, w = D_MODEL, LRU_WIDTH
    a_c = jax.random.uniform(next(ks), (N_A_LAYERS, w), jnp.float32, 0.9, 0.999)
    a0 = a_c ** (1.0 / LRU_C)
    lam = jnp.log(a0) - jnp.log1p(-a0)
    return {
        "x_prompt": nrm((BATCH, SEQ, d), 1.0),
        "x_sample": nrm((DEC_BATCH, DEC_SEQ, d), 1.0),
        "c_prompt": nrm((BATCH, d), 1.0),
        "c_sample": nrm((DEC_BATCH, d), 1.0),
        "state_lru_h": nrm((N_A_LAYERS, DEC_BATCH, w), 0.5),
        "state_lru_conv": nrm((N_A_LAYERS, DEC_BATCH, LRU_CONV_WIDTH - 1, w), 1.0),
        "state_dwconv": nrm((N_B_LAYERS, DEC_BATCH, CF_CONV_WIDTH - 1, d), 1.0),
        "ada_w": nrm((DEPTH, d, 6 * d), 0.5 * d ** -0.5),
        "ada_b": nrm((DEPTH, 6 * d), 0.01),
        "norm_mix": 1.0 + nrm((DEPTH, d), 0.02),
        "norm_ffn": 1.0 + nrm((DEPTH, d), 0.02),
        "norm_final": 1.0 + nrm((d,), 0.02),
        "lru_w_in": nrm((N_A_LAYERS, d, 2 * w), d ** -0.5),
        "lru_b_in": nrm((N_A_LAYERS, 2 * w), 0.01),
        "lru_conv_w": nrm((N_A_LAYERS, LRU_CONV_WIDTH, w), LRU_CONV_WIDTH ** -0.5),
        "lru_conv_b": nrm((N_A_LAYERS, w), 0.01),
        "lru_gate_a_w": nrm((N_A_LAYERS, LRU_HEADS, LRU_BLOCK, LRU_BLOCK), LRU_BLOCK ** -0.5),
        "lru_gate_a_b": nrm((N_A_LAYERS, w), 0.01),
        "lru_gate_x_w": nrm((N_A_LAYERS, LRU_HEADS, LRU_BLOCK, LRU_BLOCK), LRU_BLOCK ** -0.5),
        "lru_gate_x_b": nrm((N_A_LAYERS, w), 0.01),
        "lru_lambda": lam,
        "lru_w_out": nrm((N_A_LAYERS, w, d), w ** -0.5),
        "lru_b_out": nrm((N_A_LAYERS, d), 0.01),
        "cf_w_pw1": nrm((N_B_LAYERS, d, 2 * d), d ** -0.5),
        "cf_b_pw1": nrm((N_B_LAYERS, 2 * d), 0.01),
        "cf_dw_w": nrm((N_B_LAYERS, CF_CONV_WIDTH, d), CF_CONV_WIDTH ** -0.5),
        "cf_dw_b": nrm((N_B_LAYERS, d), 0.01),
        "cf_ln_g": 1.0 + nrm((N_B_LAYERS, d), 0.02),
        "cf_ln_b": nrm((N_B_LAYERS, d), 0.01),
        "cf_w_pw2": nrm((N_B_LAYERS, d, d), d ** -0.5),
        "cf_b_pw2": nrm((N_B_LAYERS, d), 0.01),
        "peer_w_q": nrm((DEPTH, d, PEER_HEADS * PEER_DK), d ** -0.5),
        "peer_k1": nrm((DEPTH, PEER_NKEYS, PEER_DK // 2), (PEER_DK // 2) ** -0.5),
        "peer_k2": nrm((DEPTH, PEER_NKEYS, PEER_DK // 2), (PEER_DK // 2) ** -0.5),
        "peer_u": nrm((DEPTH, PEER_NEXPERTS, d), d ** -0.5),
        "peer_v": nrm((DEPTH, PEER_NEXPERTS, d), PEER_HEADS ** -0.5),
    }


def reference(x_prompt, x_sample, c_prompt, c_sample, state_lru_h, state_lru_conv, state_dwconv,
              ada_w, ada_b, norm_mix, norm_ffn, norm_final,
              lru_w_in, lru_b_in, lru_conv_w, lru_conv_b, lru_gate_a_w, lru_gate_a_b,
              lru_gate_x_w, lru_gate_x_b, lru_lambda, lru_w_out, lru_b_out,
              cf_w_pw1, cf_b_pw1, cf_dw_w, cf_dw_b, cf_ln_g, cf_ln_b, cf_w_pw2, cf_b_pw2,
              peer_w_q, peer_k1, peer_k2, peer_u, peer_v):
    p = dict(ada_w=ada_w, ada_b=ada_b, norm_mix=norm_mix, norm_ffn=norm_ffn, norm_final=norm_final,
             lru_w_in=lru_w_in, lru_b_in=lru_b_in, lru_conv_w=lru_conv_w, lru_conv_b=lru_conv_b,
             lru_gate_a_w=lru_gate_a_w, lru_gate_a_b=lru_gate_a_b, lru_gate_x_w=lru_gate_x_w,
             lru_gate_x_b=lru_gate_x_b, lru_lambda=lru_lambda, lru_w_out=lru_w_out, lru_b_out=lru_b_out,
             cf_w_pw1=cf_w_pw1, cf_b_pw1=cf_b_pw1, cf_dw_w=cf_dw_w, cf_dw_b=cf_dw_b, cf_ln_g=cf_ln_g,
             cf_ln_b=cf_ln_b, cf_w_pw2=cf_w_pw2, cf_b_pw2=cf_b_pw2, peer_w_q=peer_w_q, peer_k1=peer_k1,
             peer_k2=peer_k2, peer_u=peer_u, peer_v=peer_v)
    bp = x_prompt.shape[0]
    dt = x_prompt.dtype
    h0 = jnp.zeros((N_A_LAYERS, bp, LRU_WIDTH), dt)
    conv0 = jnp.zeros((N_A_LAYERS, bp, LRU_CONV_WIDTH - 1, LRU_WIDTH), dt)
    dw0 = jnp.zeros((N_B_LAYERS, bp, CF_CONV_WIDTH - 1, D_MODEL), dt)
    y_prompt, lru_h_p, lru_conv_p, dwconv_p = trunk(x_prompt, c_prompt, h0, conv0, dw0, p)
    y_sample, lru_h_s, lru_conv_s, dwconv_s = trunk(x_sample, c_sample, state_lru_h, state_lru_conv,
                                                    state_dwconv, p)
    return (y_prompt, y_sample, lru_h_p, lru_conv_p, dwconv_p, lru_h_s, lru_conv_s, dwconv_s)
```

```python
import numpy as np
from contextlib import ExitStack
import concourse.bass as bass
import concourse.mybir as mybir
from concourse.bass_utils import run_bass_kernel_spmd

F32 = mybir.dt.float32
BF16 = mybir.dt.bfloat16
U32 = mybir.dt.uint32
ALU = mybir.AluOpType
AF = mybir.ActivationFunctionType
AX = mybir.AxisListType

NCORES = 8
D = 1024
KC = 8
TMAX = 128
SEQ = 2048
NPS = 4
NSS = 2
DSEQ = 32
NTOK = NPS * SEQ + NSS * DSEQ
EPS = 1e-6
G_UV = 2
SG = 8
NEG = -1.0e30
import os
ONLY_SAMPLE = bool(int(os.environ.get("PEER_ONLY_SAMPLE", "0")))
DBG_TILES = int(os.environ.get("PEER_DBG_TILES", "0"))

VEC_SPEC = [("nm0", 8), ("nm1", 8), ("nf0", 8), ("nf1", 8), ("nfin", 8), ("b_in", 16),
            ("cw0", 8), ("cw1", 8), ("cw2", 8), ("cw3", 8), ("cb", 8), ("gab", 8), ("gxb", 8),
            ("lam", 8), ("bout", 8), ("bpw1", 16)] + [("dw%d" % k, 8) for k in range(31)] + \
           [("dwb", 8), ("lng", 8), ("lnb", 8), ("bpw2", 8)]
VOFF = {}
_o = 0
for _n, _w in VEC_SPEC:
    VOFF[_n] = _o
    _o += _w
NV = _o
PIECE = {"w_in": 0, "w_out": 8, "pw1": 12, "pw2": 20, "wq0": 24, "wq1": 32}
NPIECE = 40


class Tok:
    __slots__ = ("sem", "val")

    def __init__(self, sem, val):
        self.sem = sem
        self.val = val


class Buf:
    def __init__(self, name):
        self.name = name
        self.w = None
        self.r = {}


class Eng:
    def __init__(self, name):
        self.name = name
        self.sem = None
        self.cnt = 0
        self.seen = {}
        self.ops = []


class Sched:
    def __init__(self):
        self.pe = Eng("pe")
        self.act = Eng("act")
        self.dve = Eng("dve")
        self.pool = Eng("pool")
        self.sp = Eng("sp")
        self.engs = [self.pe, self.act, self.dve, self.pool, self.sp]
        self.dsem = {}
        self.semlist = []

    def _waits(self, E, reads, writes):
        waits = {}

        def need(tok, same_ok):
            if tok is None:
                return
            if tok.sem is E.sem and (same_ok or E is self.pe):
                return
            k = id(tok.sem)
            if k not in waits or waits[k][1] < tok.val:
                waits[k] = (tok.sem, tok.val)

        for b in reads:
            need(b.w, False)
        for b in writes:
            need(b.w, True)
            for sem, val in b.r.values():
                need(Tok(sem, val), True)
        wl = []
        for k, (sem, val) in waits.items():
            if E.seen.get(k, 0) < val:
                E.seen[k] = val
                wl.append((sem, val))
        return wl

    def _commit(self, tok, reads, writes):
        k = id(tok.sem)
        for b in reads:
            if k not in b.r or b.r[k][1] < tok.val:
                b.r[k] = (tok.sem, tok.val)
        for b in writes:
            b.w = tok
            b.r = {}

    def op(self, E, name, kw, reads=(), writes=()):
        if name == "activation" and "bias" not in kw:
            kw["bias"] = self.zero[:kw["in_"].shape[0]]
        return self.multi(E, [(name, kw)], reads, writes)

    def multi(self, E, insts, reads=(), writes=()):
        wl = self._waits(E, reads, writes)
        E.cnt += 1
        tok = Tok(E.sem, E.cnt)
        E.ops.append((wl, insts, E.sem, 1))
        self._commit(tok, reads, writes)
        return tok

    def dma(self, Q, key, out, in_, reads=(), writes=()):
        wl = self._waits(Q, reads, writes)
        ent = self.dsem[key]
        ent[1] += 16
        tok = Tok(ent[0], ent[1])
        Q.ops.append((wl, [("dma_start", dict(out=out, in_=in_))], ent[0], 16))
        self._commit(tok, reads, writes)
        return tok

    def fence(self, E, toks):
        wl = []
        for tok in toks:
            k = id(tok.sem)
            if E.seen.get(k, 0) < tok.val:
                E.seen[k] = tok.val
                wl.append((tok.sem, tok.val))
        if wl:
            E.ops.append((wl, [], None, 0))

    def replay(self, E, e):
        for wl, insts, sem, inc in E.ops:
            for s, v in wl:
                e.wait_ge(s, v)
            last = None
            for name, kw in insts:
                last = getattr(e, name)(**kw)
            if last is not None and sem is not None:
                last.then_inc(sem, inc)


def cap(full, off, dims, nparts=128):
    pstep = full.ap[0][0]
    return bass.AP(tensor=full.tensor, offset=off, ap=[[pstep, nparts]] + [list(d) for d in dims])


def dap(t, off, dims):
    return bass.AP(tensor=t.tensor, offset=off, ap=[list(d) for d in dims])


def build_program():
    nc = bass.Bass("TRN2", target_bir_lowering=False)
    K = Sched()
    PE, ACT, DVE, POOL, SP = K.pe, K.act, K.dve, K.pool, K.sp

    def din(name, shape, dt=F32):
        return nc.dram_tensor(name, list(shape), dt, kind="ExternalInput").ap()

    def dout(name, shape, dt=F32):
        return nc.dram_tensor(name, list(shape), dt, kind="ExternalOutput").ap()

    xin = din("xin", [128, KC, NTOK])
    cT = din("cT", [128, KC, 6])
    st_h = din("st_h", [128, NSS, KC])
    st_conv = din("st_conv", [128, KC, NSS, 3])
    st_dw = din("st_dw", [128, KC, NSS, 30])
    consts = din("consts", [128, 4, 128])
    vec_d = din("vec", [128, NV])
    adab_d = din("adab", [128, 2, 48])
    adaw_d = din("adaw", [2, 12, 128, KC * 512])
    gw_d = din("gw", [128, 2 * 8 * 128])
    kT_d = din("kT", [128, 4 * 128])
    wd_d = din("wd", [NPIECE * 128 * KC * 256 // 2048, 2048])
    u_d = din("u_arr", [2 * 128 * 128 * 1024 // 2048, 2048])
    v_d = din("v_arr", [2 * 128 * 128 * 1024 // 2048, 2048])
    wd_b = nc.dram_tensor("wd_b", [NPIECE * 128 * KC * 256 // 2048, 2048], BF16, kind="Internal").ap()
    u_b = nc.dram_tensor("u_b", [2 * 128 * 128 * 1024 // 2048, 2048], BF16, kind="Internal").ap()
    v_b = nc.dram_tensor("v_b", [2 * 128 * 128 * 1024 // 2048, 2048], BF16, kind="Internal").ap()
    yout = dout("yout", [128, KC, NTOK])
    ho = dout("ho", [128, 6, KC])
    convo = dout("convo", [128, KC, 6, 3])
    dwo = dout("dwo", [128, KC, 6, 30])

    es = ExitStack()
    with es:
        sb_off = [20480]

        def SBt(name, shape, dt, at=None):
            esz = 2 if dt == BF16 else 4
            n = 1
            for s in shape[1:]:
                n *= s
            nbytes = (n * esz + 63) // 64 * 64
            if at is None:
                at = sb_off[0]
                sb_off[0] += nbytes
            t = nc.alloc_sbuf_tensor_at(name, list(shape), dt, offset=at)
            return t.ap()

        T = TMAX
        cst = SBt("cst", [128, 4, 128], F32)
        ZERO = cst[:, 3, 0:1]
        ONE = cst[:, 3, 1:2]
        EPSA = cst[:, 3, 2:3]
        K.zero = ZERO
        ident = cst[:, 0, :]
        ones = cst[:, 1, :]
        iota_f = cst[:, 2, :]
        iota_b = SBt("iota_b", [128, 128], BF16)
        vec = SBt("vec", [128, NV], F32)
        adab = SBt("adab", [128, 2, 48], F32)
        modv = SBt("modv", [128, 2, 48, 6], F32)
        G1 = SBt("G1", [128, 2, 8, 6], F32)
        GB1 = SBt("GB1", [128, 2, 8, 6], F32)
        G2 = SBt("G2", [128, 2, 8, 6], F32)
        cA = SBt("cA", [128, 8], F32)
        c2A = SBt("c2A", [128, 8], F32)
        ptmp = SBt("ptmp", [128, 4, 8], F32)
        cTs = SBt("cTs", [128, KC, 6], F32)
        csil = SBt("csil", [128, KC, 6], F32)
        gwb = SBt("gwb", [128, 2, 8, 128], BF16)
        kTb = SBt("kTb", [128, 4, 128], BF16)
        hstate = SBt("hstate", [128, 8], F32)
        xp = SBt("xp", [128, KC, 3 + T], F32)
        xpc = SBt("xpc", [128, KC, 30 + T], F32)
        x = SBt("x", [128, KC, T], F32)
        hT = SBt("hT", [128, KC, T], BF16)
        ybuf = SBt("ybuf", [128, KC, T], F32)
        sq = SBt("sq", [128, KC, T], F32)
        sdt = SBt("sdt", [128, T], F32)
        rstd = SBt("rstd", [128, T], F32)
        tmpk = [SBt("tmpk%d" % i, [128, T], F32) for i in range(2)]
        wslot = [SBt("wslot%d" % i, [128, KC, 256], BF16) for i in range(3)]
        ubuf = [SBt("ubuf%d" % i, [128, G_UV, 1024], BF16) for i in range(2)]
        vbuf = [SBt("vbuf%d" % i, [128, G_UV, 1024], BF16) for i in range(2)]
        gate = SBt("gate", [128, KC, T], BF16)
        xc = SBt("xc", [128, KC, T], F32)
        xcb = SBt("xcb", [128, KC, T], BF16)
        rr = SBt("rr", [128, KC, T], F32)
        ii = SBt("ii", [128, KC, T], F32)
        aa = SBt("aa", [128, KC, T], F32)
        mm_ = SBt("mm", [128, KC, T], F32)
        yin = SBt("yin", [128, KC, T], BF16)
        rtmp = [SBt("rtmp%d" % i, [128, T], F32) for i in range(2)]
        sgm = rr
        dd = xc
        mu = SBt("mu", [128, T], F32)
        musq = SBt("musq", [128, T], F32)
        var = SBt("var", [128, T], F32)
        sd2 = SBt("sd2", [128, T], F32)
        rs2 = SBt("rs2", [128, T], F32)
        t1 = [SBt("t1_%d" % i, [128, T], F32) for i in range(2)]
        dnb = SBt("dnb", [128, KC, T], BF16)
        qT = SBt("qT", [128, 16, T], BF16)
        s_off = sb_off[0]
        s_sb = SBt("s_sb", [128, 16, 128], F32)
        s2_off = sb_off[0]
        s2 = SBt("s2", [128, 16, 128], F32)
        eqt = SBt("eqt", [128, 8, 16, 16], F32, at=s_off)
        prod = SBt("prod", [128, 8, 16, 16], F32, at=s2_off)
        v1 = SBt("v1", [128, 16, 16], F32)
        ix = SBt("ix", [128, 16, 16], U32)
        ixf = SBt("ixf", [128, 16, 16], F32)
        cand_off = sb_off[0]
        cand = SBt("cand", [128, 8, 256], F32)
        cand2_off = sb_off[0]
        cand2 = SBt("cand2", [128, 8, 256], F32)
        cv = SBt("cv", [128, 8, 16], F32)
        ci = SBt("ci", [128, 8, 16], U32)
        cab = SBt("cab", [128, 2, 128], U32)
        cabf = SBt("cabf", [128, 2, 128], F32)
        ge = SBt("ge", [128, 8, 16], F32)
        gs = SBt("gs", [128, 8], F32)
        gs2 = SBt("gs2", [128, 8], F32)
        idxg = SBt("idxg", [128, 3, 128], F32)
        idxT = SBt("idxT", [128, 3, T], BF16)
        oh = [[SBt("oh%d_%d" % (s, i), [128, SG, 128], BF16) for i in range(3)] for s in range(2)]
        wsb_off = sb_off[0]
        Wsb = SBt("Wsb", [128, 128, T], BF16)
        gz = [SBt("gz%d" % i, [128, T], BF16) for i in range(3)]
        wz = [SBt("wz%d" % i, [128, T], BF16) for i in range(3)]
        assert sb_off[0] <= 224 * 1024 - 2048, sb_off[0]
        adas = [SBt("adas%d" % i, [128, KC, 512], F32, at=wsb_off + i * 16384) for i in range(2)]
        gws = SBt("gws", [128, 2 * 8 * 128], F32, at=cand_off)
        kTs = SBt("kTs", [128, 4 * 128], F32, at=cand2_off)

        PS = [es.enter_context(nc.psum_tensor("ps%d" % i, [128, 512], F32)) for i in range(8)]
        PSB = [Buf("ps%d" % i) for i in range(8)]
        for E in K.engs:
            E.sem = es.enter_context(nc.semaphore("sem_" + E.name))

        def dsem(key):
            K.dsem[key] = [es.enter_context(nc.semaphore("d_" + key)), 0]

        for key in ["ld_misc", "ld_x", "ld_w0", "ld_w1", "ld_w2", "ld_u0", "ld_u1", "ld_v0", "ld_v1",
                    "st_y", "st_c", "st_h", "st_d", "cast_w", "cast_u", "cast_v", "ld_a0", "ld_a1", "ld_st"]:
            dsem(key)

        rot = {"bank": 0, "w": 0, "uv": 0, "tmpk": 0, "rtmp": 0, "t1": 0, "gz": 0, "oh": 0, "cp": 0}

        def nbank():
            b = rot["bank"]
            rot["bank"] = (b + 1) % 4
            return b

        def nrot(key, n):
            v = rot[key]
            rot[key] = (v + 1) % n
            return v

        B = lambda n: Buf(n)
        cstB, vecB, adabB, modvB, GB_, cAB, csilB, gwbB, kTbB = B("cst"), B("vec"), B("adab"), B("modv"), B("G"), B("cA"), B("csil"), B("gwb"), B("kTb")
        iotabB, ptmpB, cTsB, gwsB, kTsB = B("iotab"), B("ptmp"), B("cTs"), B("gws"), B("kTs")
        hstB = [B("hst%d" % j) for j in range(8)]
        xpB = [B("xp%d" % j) for j in range(8)]
        xpcB = [B("xpc%d" % j) for j in range(8)]
        xB = [B("x%d" % j) for j in range(8)]
        hTB = [B("hT%d" % j) for j in range(8)]
        yB = B("ybuf")
        sqB = [B("sq%d" % j) for j in range(8)]
        sdtB, rstdB = B("sdt"), B("rstd")
        tmpkB = [B("tmpk0"), B("tmpk1")]
        wslotB = [B("ws%d" % i) for i in range(3)]
        ubufB = [B("ub0"), B("ub1")]
        vbufB = [B("vb0"), B("vb1")]
        gateB = [B("gate%d" % j) for j in range(8)]
        xcB = [B("xc%d" % j) for j in range(8)]
        xcbB = [B("xcb%d" % j) for j in range(8)]
        rrB = [B("rr%d" % j) for j in range(8)]
        iiB = [B("ii%d" % j) for j in range(8)]
        aaB = [B("aa%d" % j) for j in range(8)]
        mmB = [B("mm%d" % j) for j in range(8)]
        yinB = [B("yin%d" % j) for j in range(8)]
        rtmpB = [B("rtmp0"), B("rtmp1")]
        muB, musqB, varB, sd2B, rs2B = B("mu"), B("musq"), B("var"), B("sd2"), B("rs2")
        t1B = [B("t1_0"), B("t1_1")]
        dnbB = [B("dnb%d" % j) for j in range(8)]
        qTB = [B("qT%d" % j) for j in range(16)]
        ssbB = [B("ssb%d" % j) for j in range(4)]
        s2B = [B("s2_%d" % j) for j in range(4)]
        v1B = [B("v1_%d" % j) for j in range(16)]
        ixB = [B("ix_%d" % j) for j in range(16)]
        ixfB, candB, cvB, ciB, cabB, cabfB, geB, gsB, gs2B, idxgB = B("ixf"), [B("cand%d" % h) for h in range(8)], [B("cv%d" % h) for h in range(8)], [B("ci%d" % h) for h in range(8)], B("cab"), B("cabf"), B("ge"), B("gs"), B("gs2"), B("idxg")
        cand2B = [B("cand2_%d" % h) for h in range(8)]
        idxTB = B("idxT")
        ohB = [[B("oh%d_%d" % (s, i)) for i in range(3)] for s in range(2)]
        WsbB = B("Wsb")
        gzB = [B("gz%d" % i) for i in range(3)]
        wzB = [B("wz%d" % i) for i in range(3)]
        adasB = [B("adas0"), B("adas1")]
        wdbB, ubB, vbB = B("wd_b"), B("u_b"), B("v_b")
        store_toks = []

        def V(name, j=0):
            o = VOFF[name] + j
            return vec[:, o:o + 1]

        K.dma(SP, "ld_misc", cst[:], consts[:, :, :], writes=[cstB])
        K.dma(SP, "ld_misc", vec[:], vec_d[:, :], writes=[vecB])
        K.dma(SP, "ld_misc", adab[:], adab_d[:, :, :], writes=[adabB])
        K.dma(SP, "ld_misc", cTs[:], cT[:, :, :], writes=[cTsB])
        K.dma(SP, "ld_misc", gws[:], gw_d[:, :], writes=[gwsB])
        tk_misc = K.dma(SP, "ld_misc", kTs[:], kT_d[:, :], writes=[kTsB])
        for bb in (cstB, vecB, adabB, cTsB, gwsB, kTsB):
            bb.w = tk_misc
        R = 2048
        nrow = wd_d.shape[0]
        r0 = 0
        while r0 < nrow:
            n = min(R, nrow - r0)
            K.dma(POOL, "cast_w", wd_b[r0:r0 + n, :], wd_d[r0:r0 + n, :], writes=[wdbB])
            r0 += n
        for (src, dst, bb, ck) in ((u_d, u_b, ubB, "cast_u"), (v_d, v_b, vbB, "cast_v")):
            for r0 in range(0, src.shape[0], R):
                K.dma(POOL, ck, dst[r0:r0 + R, :], src[r0:r0 + R, :], writes=[bb])
        K.op(DVE, "tensor_copy", dict(out=iota_b[:], in_=iota_f), reads=[cstB], writes=[iotabB])
        K.op(DVE, "tensor_copy", dict(out=gwb[:].rearrange("p a h j -> p (a h j)"), in_=gws[:]), reads=[gwsB], writes=[gwbB])
        K.op(DVE, "tensor_copy", dict(out=kTb[:].rearrange("p a k -> p (a k)"), in_=kTs[:]), reads=[kTsB], writes=[kTbB])
        K.op(ACT, "activation", dict(out=csil[:], in_=cTs[:], func=AF.Silu), reads=[cTsB], writes=[csilB])
        lam = vec[:, VOFF["lam"]:VOFF["lam"] + 8]
        K.op(DVE, "tensor_scalar", dict(out=ptmp[:, 3, :], in0=lam, scalar1=-1.0, scalar2=None, op0=ALU.mult), reads=[vecB], writes=[ptmpB])
        K.op(DVE, "tensor_tensor", dict(out=ptmp[:, 0, :], in0=lam, in1=ptmp[:, 3, :], op=ALU.max), reads=[vecB, ptmpB], writes=[ptmpB])
        K.op(ACT, "activation", dict(out=ptmp[:, 1, :], in_=ptmp[:, 0, :], func=AF.Exp, scale=-1.0), reads=[ptmpB], writes=[ptmpB])
        K.op(ACT, "activation", dict(out=ptmp[:, 2, :], in_=ptmp[:, 1, :], func=AF.Ln, bias=ONE), reads=[ptmpB], writes=[ptmpB])
        K.op(DVE, "tensor_scalar", dict(out=ptmp[:, 3, :], in0=lam, scalar1=-1.0, scalar2=0.0, op0=ALU.mult, op1=ALU.max), reads=[vecB, ptmpB], writes=[ptmpB])
        K.op(DVE, "tensor_tensor", dict(out=ptmp[:, 3, :], in0=ptmp[:, 3, :], in1=ptmp[:, 2, :], op=ALU.add), reads=[ptmpB], writes=[ptmpB])
        K.op(DVE, "tensor_scalar", dict(out=cA[:], in0=ptmp[:, 3, :], scalar1=-8.0, scalar2=None, op0=ALU.mult), reads=[ptmpB], writes=[cAB])
        K.op(DVE, "tensor_scalar", dict(out=c2A[:], in0=ptmp[:, 3, :], scalar1=-16.0, scalar2=None, op0=ALU.mult), reads=[ptmpB], writes=[cAB])
        for l in range(2):
            pb = nbank()
            for pc in range(12):
                sl = pc % 2
                K.dma(SP, "ld_a%d" % sl, adas[sl][:].rearrange("p k o -> p (k o)"), adaw_d[l, pc, :, :], writes=[adasB[sl]])
                for o4 in range(4):
                    oc = pc * 4 + o4
                    insts = [("matmul", dict(out=PS[pb][:, oc * 6:(oc + 1) * 6], lhsT=adas[sl][:, k, o4 * 128:(o4 + 1) * 128],
                                             rhs=csil[:, k, :], start=(k == 0), stop=(k == KC - 1))) for k in range(KC)]
                    K.multi(PE, insts, reads=[adasB[sl], csilB], writes=[PSB[pb]])
            K.op(DVE, "tensor_tensor", dict(out=modv[:, l, :, :], in0=PS[pb][:, 0:288].rearrange("p (o s) -> p o s", s=6),
                                            in1=cap(adab, l * 48, [[1, 48], [0, 6]]), op=ALU.add),
                 reads=[PSB[pb], adabB], writes=[modvB])
            nmn = "nm%d" % l
            nfn = "nf%d" % l
            bon = "bout" if l == 0 else "bpw2"
            K.op(DVE, "scalar_tensor_tensor", dict(out=G1[:, l, :, :], in0=modv[:, l, 8:16, :], scalar=1.0,
                                                   in1=cap(vec, VOFF[nmn], [[1, 8], [0, 6]]), op0=ALU.add, op1=ALU.mult),
                 reads=[modvB, vecB], writes=[GB_])
            K.op(DVE, "scalar_tensor_tensor", dict(out=G2[:, l, :, :], in0=modv[:, l, 32:40, :], scalar=1.0,
                                                   in1=cap(vec, VOFF[nfn], [[1, 8], [0, 6]]), op0=ALU.add, op1=ALU.mult),
                 reads=[modvB, vecB], writes=[GB_])
            K.op(DVE, "tensor_tensor", dict(out=GB1[:, l, :, :], in0=modv[:, l, 16:24, :],
                                            in1=cap(vec, VOFF[bon], [[1, 8], [0, 6]]), op=ALU.mult),
                 reads=[modvB, vecB], writes=[GB_])

        def MOD(l, oc, s):
            return modv[:, l, oc, s:s + 1]

        def load_piece(pc):
            sl = nrot("w", 3)
            src = dap(wd_b, pc * 128 * KC * 256, [[KC * 256, 128], [1, KC * 256]])
            K.dma(SP, "ld_w%d" % sl, wslot[sl][:].rearrange("p k o -> p (k o)"), src, reads=[wdbB], writes=[wslotB[sl]])
            return sl

        def stats(T_):
            for k in range(KC):
                K.op(ACT, "activation", dict(out=sq[:, k, :T_], in_=x[:, k, :T_], func=AF.Square), reads=[xB[k]], writes=[sqB[k]])
            pb = nbank()
            insts = [("matmul", dict(out=PS[pb][:, :T_], lhsT=ones, rhs=sq[:, k, :T_], start=(k == 0), stop=(k == KC - 1))) for k in range(KC)]
            K.multi(PE, insts, reads=sqB + [cstB], writes=[PSB[pb]])
            K.op(ACT, "activation", dict(out=sdt[:, :T_], in_=PS[pb][:, :T_], func=AF.Sqrt, bias=EPSA, scale=1.0 / D), reads=[PSB[pb]], writes=[sdtB])
            K.op(DVE, "reciprocal", dict(out=rstd[:, :T_], in_=sdt[:, :T_]), reads=[sdtB], writes=[rstdB])

        def modnorm(T_, Gt, l, shoc, s):
            for k in range(KC):
                i = nrot("tmpk", 2)
                K.op(DVE, "scalar_tensor_tensor", dict(out=tmpk[i][:, :T_], in0=x[:, k, :T_], scalar=Gt[:, l, k, s:s + 1], in1=rstd[:, :T_],
                                                       op0=ALU.mult, op1=ALU.mult), reads=[xB[k], GB_, rstdB], writes=[tmpkB[i]])
                K.op(ACT, "activation", dict(out=hT[:, k, :T_], in_=tmpk[i][:, :T_], func=AF.Identity, bias=MOD(l, shoc + k, s)),
                     reads=[tmpkB[i], modvB], writes=[hTB[k]])

        def proj(T_, pc, oo, rhs_t, rhsB):
            sl = proj.cur
            pb = nbank()
            insts = [("matmul", dict(out=PS[pb][:, :T_], lhsT=wslot[sl][:, k, oo * 128:(oo + 1) * 128], rhs=rhs_t[:, k, :T_],
                                     start=(k == 0), stop=(k == KC - 1))) for k in range(KC)]
            K.multi(PE, insts, reads=[wslotB[sl]] + rhsB, writes=[PSB[pb]])
            return pb

        def resid_add(T_, pb, l, dc, s):
            i = nrot("rtmp", 2)
            K.op(ACT, "activation", dict(out=rtmp[i][:, :T_], in_=PS[pb][:, :T_], func=AF.Identity, bias=GB1[:, l, dc, s:s + 1], scale=MOD(l, 16 + dc, s)),
                 reads=[PSB[pb], GB_, modvB], writes=[rtmpB[i]])
            K.op(POOL, "tensor_tensor", dict(out=x[:, dc, :T_], in0=x[:, dc, :T_], in1=rtmp[i][:, :T_], op=ALU.add), reads=[rtmpB[i], xB[dc]], writes=[xB[dc]])

        def lru_stage(T_, s, first, last):
            if first:
                if s < NPS:
                    for j in range(KC):
                        K.op(POOL, "memset", dict(ap=xp[:, j, 0:3], constant=0.0), writes=[xpB[j]])
                        K.op(POOL, "memset", dict(ap=hstate[:, j:j + 1], constant=0.0), writes=[hstB[j]])
                else:
                    K.dma(SP, "ld_st", xp[:, :, 0:3], st_conv[:, :, s - NPS, :], writes=xpB)
                    tk_st = K.dma(SP, "ld_st", hstate[:], st_h[:, s - NPS, :], writes=hstB)
                    for bb in xpB:
                        bb.w = tk_st
            stats(T_)
            modnorm(T_, G1, 0, 0, s)
            for pc in range(8):
                proj.cur = load_piece(PIECE["w_in"] + pc)
                for oo in range(2):
                    oc = pc * 2 + oo
                    pb = proj(T_, pc, oo, hT, hTB)
                    if oc < 8:
                        K.op(ACT, "activation", dict(out=gate[:, oc, :T_], in_=PS[pb][:, :T_], func=AF.Gelu_apprx_tanh, bias=V("b_in", oc)),
                             reads=[PSB[pb], vecB], writes=[gateB[oc]])
                    else:
                        j = oc - 8
                        K.op(ACT, "activation", dict(out=xp[:, j, 3:3 + T_], in_=PS[pb][:, :T_], func=AF.Identity, bias=V("b_in", oc)),
                             reads=[PSB[pb], vecB], writes=[xpB[j]])
            for j in range(KC):
                K.op(DVE, "tensor_scalar", dict(out=xc[:, j, :T_], in0=xp[:, j, 0:T_], scalar1=V("cw0", j), scalar2=V("cb", j), op0=ALU.mult, op1=ALU.add),
                     reads=[xpB[j], vecB], writes=[xcB[j]])
            for k in range(1, 4):
                for j in range(KC):
                    K.op(DVE, "scalar_tensor_tensor", dict(out=xc[:, j, :T_], in0=xp[:, j, k:k + T_], scalar=V("cw%d" % k, j), in1=xc[:, j, :T_],
                                                           op0=ALU.mult, op1=ALU.add), reads=[xpB[j], vecB, xcB[j]], writes=[xcB[j]])
            for j in range(KC):
                K.op(POOL, "tensor_copy", dict(out=xcb[:, j, :T_], in_=xc[:, j, :T_]), reads=[xcB[j]], writes=[xcbB[j]])
                if last:
                    pass
            if last:
                store_toks.append(K.dma(SP, "st_c", convo[:, :, s, :], xp[:, :, T_:T_ + 3], reads=xpB))
            for j in range(KC):
                K.op(POOL, "tensor_copy", dict(out=xp[:, j, 0:3], in_=xp[:, j, T_:T_ + 3]), reads=[xpB[j]], writes=[xpB[j]])
            for j in range(KC):
                for a_, dst, dstB, bn in ((0, rr, rrB, "gab"), (1, ii, iiB, "gxb")):
                    pb = nbank()
                    K.multi(PE, [("matmul", dict(out=PS[pb][:, :T_], lhsT=gwb[:, a_, j, :], rhs=xcb[:, j, :T_], start=True, stop=True))],
                            reads=[gwbB, xcbB[j]], writes=[PSB[pb]])
                    K.op(ACT, "activation", dict(out=dst[:, j, :T_], in_=PS[pb][:, :T_], func=AF.Sigmoid, bias=V(bn, j)),
                         reads=[PSB[pb], vecB], writes=[dstB[j]])
            for j in range(KC):
                K.op(ACT, "activation", dict(out=aa[:, j, :T_], in_=rr[:, j, :T_], func=AF.Exp, scale=cA[:, j:j + 1]), reads=[rrB[j], cAB], writes=[aaB[j]])
                K.op(ACT, "activation", dict(out=mm_[:, j, :T_], in_=rr[:, j, :T_], func=AF.Exp, scale=c2A[:, j:j + 1]), reads=[rrB[j], cAB], writes=[mmB[j]])
            for j in range(KC):
                K.op(ACT, "activation", dict(out=mm_[:, j, :T_], in_=mm_[:, j, :T_], func=AF.Sqrt, bias=ONE, scale=-1.0), reads=[mmB[j]], writes=[mmB[j]])
            for j in range(KC):
                K.op(DVE, "tensor_tensor", dict(out=mm_[:, j, :T_], in0=mm_[:, j, :T_], in1=ii[:, j, :T_], op=ALU.mult), reads=[mmB[j], iiB[j]], writes=[mmB[j]])
            for j in range(KC):
                K.op(DVE, "tensor_tensor", dict(out=mm_[:, j, :T_], in0=mm_[:, j, :T_], in1=xc[:, j, :T_], op=ALU.mult), reads=[mmB[j], xcB[j]], writes=[mmB[j]])
            for j in range(KC):
                K.op(DVE, "tensor_tensor_scan", dict(out=rr[:, j, :T_], data0=aa[:, j, :T_], data1=mm_[:, j, :T_], initial=hstate[:, j:j + 1],
                                                     op0=ALU.mult, op1=ALU.add), reads=[aaB[j], mmB[j], hstB[j]], writes=[rrB[j]])
            for j in range(KC):
                K.op(POOL, "tensor_copy", dict(out=hstate[:, j:j + 1], in_=rr[:, j, T_ - 1:T_]), reads=[rrB[j]], writes=[hstB[j]])
                K.op(DVE, "tensor_tensor", dict(out=yin[:, j, :T_], in0=rr[:, j, :T_], in1=gate[:, j, :T_], op=ALU.mult), reads=[rrB[j], gateB[j]], writes=[yinB[j]])
            if last:
                store_toks.append(K.dma(SP, "st_h", ho[:, s, :], hstate[:], reads=hstB))
            for pc in range(4):
                proj.cur = load_piece(PIECE["w_out"] + pc)
                for oo in range(2):
                    dc = pc * 2 + oo
                    pb = proj(T_, pc, oo, yin, yinB)
                    resid_add(T_, pb, 0, dc, s)

        def conf_stage(T_, s, first, last):
            if first:
                if s < NPS:
                    for j in range(KC):
                        K.op(POOL, "memset", dict(ap=xpc[:, j, 0:30], constant=0.0), writes=[xpcB[j]])
                else:
                    K.dma(SP, "ld_st", xpc[:, :, 0:30], st_dw[:, :, s - NPS, :], writes=xpcB)
            stats(T_)
            modnorm(T_, G1, 1, 0, s)
            for pc in (4, 5, 6, 7, 0, 1, 2, 3):
                proj.cur = load_piece(PIECE["pw1"] + pc)
                for oo in range(2):
                    oc = pc * 2 + oo
                    pb = proj(T_, pc, oo, hT, hTB)
                    if oc >= 8:
                        j = oc - 8
                        K.op(ACT, "activation", dict(out=sgm[:, j, :T_], in_=PS[pb][:, :T_], func=AF.Sigmoid, bias=V("bpw1", oc)),
                             reads=[PSB[pb], vecB], writes=[rrB[j]])
                    else:
                        j = oc
                        K.op(DVE, "scalar_tensor_tensor", dict(out=xpc[:, j, 30:30 + T_], in0=PS[pb][:, :T_], scalar=V("bpw1", j), in1=sgm[:, j, :T_],
                                                               op0=ALU.add, op1=ALU.mult), reads=[PSB[pb], vecB, rrB[j]], writes=[xpcB[j]])
            for j in range(KC):
                K.op(DVE, "tensor_scalar", dict(out=dd[:, j, :T_], in0=xpc[:, j, 0:T_], scalar1=V("dw0", j), scalar2=V("dwb", j), op0=ALU.mult, op1=ALU.add),
                     reads=[xpcB[j], vecB], writes=[xcB[j]])
            for k in range(1, 31):
                for j in range(KC):
                    K.op(DVE, "scalar_tensor_tensor", dict(out=dd[:, j, :T_], in0=xpc[:, j, k:k + T_], scalar=V("dw%d" % k, j), in1=dd[:, j, :T_],
                                                           op0=ALU.mult, op1=ALU.add), reads=[xpcB[j], vecB, xcB[j]], writes=[xcB[j]])
            if last:
                store_toks.append(K.dma(SP, "st_d", dwo[:, :, s, :], xpc[:, :, T_:T_ + 30], reads=xpcB))
            for j in range(KC):
                K.op(POOL, "tensor_copy", dict(out=xpc[:, j, 0:30], in_=xpc[:, j, T_:T_ + 30]), reads=[xpcB[j]], writes=[xpcB[j]])
            for j in range(KC):
                K.op(ACT, "activation", dict(out=sq[:, j, :T_], in_=dd[:, j, :T_], func=AF.Square), reads=[xcB[j]], writes=[sqB[j]])
            pa = nbank()
            K.multi(PE, [("matmul", dict(out=PS[pa][:, :T_], lhsT=ones, rhs=dd[:, j, :T_], start=(j == 0), stop=(j == KC - 1))) for j in range(KC)],
                    reads=xcB + [cstB], writes=[PSB[pa]])
            pb2 = nbank()
            K.multi(PE, [("matmul", dict(out=PS[pb2][:, :T_], lhsT=ones, rhs=sq[:, j, :T_], start=(j == 0), stop=(j == KC - 1))) for j in range(KC)],
                    reads=sqB + [cstB], writes=[PSB[pb2]])
            K.op(ACT, "activation", dict(out=mu[:, :T_], in_=PS[pa][:, :T_], func=AF.Identity, scale=1.0 / D), reads=[PSB[pa]], writes=[muB])
            K.op(DVE, "tensor_tensor", dict(out=musq[:, :T_], in0=mu[:, :T_], in1=mu[:, :T_], op=ALU.mult), reads=[muB], writes=[musqB])
            K.op(DVE, "scalar_tensor_tensor", dict(out=var[:, :T_], in0=PS[pb2][:, :T_], scalar=1.0 / D, in1=musq[:, :T_], op0=ALU.mult, op1=ALU.subtract),
                 reads=[PSB[pb2], musqB], writes=[varB])
            K.op(ACT, "activation", dict(out=sd2[:, :T_], in_=var[:, :T_], func=AF.Sqrt, bias=EPSA), reads=[varB], writes=[sd2B])
            K.op(DVE, "reciprocal", dict(out=rs2[:, :T_], in_=sd2[:, :T_]), reads=[sd2B], writes=[rs2B])
            for j in range(KC):
                i = nrot("t1", 2)
                K.op(POOL, "tensor_tensor", dict(out=t1[i][:, :T_], in0=dd[:, j, :T_], in1=mu[:, :T_], op=ALU.subtract), reads=[xcB[j], muB], writes=[t1B[i]])
                K.op(DVE, "tensor_tensor", dict(out=t1[i][:, :T_], in0=t1[i][:, :T_], in1=rs2[:, :T_], op=ALU.mult), reads=[t1B[i], rs2B], writes=[t1B[i]])
                K.op(ACT, "activation", dict(out=dnb[:, j, :T_], in_=t1[i][:, :T_], func=AF.Silu, bias=V("lnb", j), scale=V("lng", j)),
                     reads=[t1B[i], vecB], writes=[dnbB[j]])
            for pc in range(4):
                proj.cur = load_piece(PIECE["pw2"] + pc)
                for oo in range(2):
                    dc = pc * 2 + oo
                    pb = proj(T_, pc, oo, dnb, dnbB)
                    resid_add(T_, pb, 1, dc, s)

        def peer_stage(T_, s, l):
            stats(T_)
            modnorm(T_, G2, l, 24, s)
            for pc in range(8):
                proj.cur = load_piece(PIECE["wq%d" % l] + pc)
                for oo in range(2):
                    qc = pc * 2 + oo
                    pb = proj(T_, pc, oo, hT, hTB)
                    if qc % 2 == 0:
                        K.op(ACT, "activation", dict(out=qT[:, qc, :T_], in_=PS[pb][:, :T_], func=AF.Identity), reads=[PSB[pb]], writes=[qTB[qc]])
                    else:
                        K.op(DVE, "tensor_copy", dict(out=qT[:, qc, :T_], in_=PS[pb][:, :T_]), reads=[PSB[pb]], writes=[qTB[qc]])
            ngr = (T_ + 127) // 128
            for g in range(ngr):
                g0 = g * 128
                gt = min(128, T_ - g0)
                for i4 in range(4):
                    pb = nbank()
                    insts = []
                    for i in range(4):
                        qc = i4 * 4 + i
                        insts.append(("matmul", dict(out=PS[pb][:gt, i * 128:(i + 1) * 128], lhsT=qT[:, qc, g0:g0 + gt], rhs=kTb[:, l * 2 + (qc % 2), :],
                                                     start=True, stop=True)))
                    K.multi(PE, insts, reads=[qTB[i4 * 4 + i] for i in range(4)] + [kTbB], writes=[PSB[pb]])
                    K.op(ACT, "activation", dict(out=s_sb[:gt, i4 * 4:(i4 + 1) * 4, :], in_=PS[pb][:gt, :].rearrange("p (a b) -> p a b", b=128), func=AF.Identity),
                         reads=[PSB[pb]], writes=[ssbB[i4]])
                for qc in range(16):
                    K.op(DVE, "max", dict(out=v1[:gt, qc, 0:8], in_=s_sb[:gt, qc, :]), reads=[ssbB[qc // 4]], writes=[v1B[qc]])
                for qc in range(16):
                    K.op(DVE, "max_index", dict(out=ix[:gt, qc, 0:8], in_max=v1[:gt, qc, 0:8], in_values=s_sb[:gt, qc, :]), reads=[ssbB[qc // 4], v1B[qc]], writes=[ixB[qc]])
                for qc in range(16):
                    K.op(DVE, "match_replace", dict(out=s2[:gt, qc, :], in_to_replace=v1[:gt, qc, 0:8], in_values=s_sb[:gt, qc, :], imm_value=NEG),
                         reads=[ssbB[qc // 4], v1B[qc]], writes=[s2B[qc // 4]])
                for qc in range(16):
                    K.op(DVE, "max", dict(out=v1[:gt, qc, 8:16], in_=s2[:gt, qc, :]), reads=[s2B[qc // 4]], writes=[v1B[qc]])
                for qc in range(16):
                    K.op(DVE, "max_index", dict(out=ix[:gt, qc, 8:16], in_max=v1[:gt, qc, 8:16], in_values=s2[:gt, qc, :]), reads=[s2B[qc // 4], v1B[qc]], writes=[ixB[qc]])
                K.op(DVE, "tensor_copy", dict(out=ixf[:gt], in_=ix[:gt]), reads=ixB, writes=[ixfB])
                K.op(DVE, "tensor_tensor", dict(out=cand[:gt].rearrange("p h (a b) -> p h a b", b=16),
                                                in0=cap(v1, 0, [[32, 8], [1, 16], [0, 16]], gt), in1=cap(v1, 16, [[32, 8], [0, 16], [1, 16]], gt), op=ALU.add),
                     reads=v1B, writes=candB)
                for h in range(8):
                    K.op(DVE, "max", dict(out=cv[:gt, h, 0:8], in_=cand[:gt, h, :]), reads=[candB[h]], writes=[cvB[h]])
                for h in range(8):
                    K.op(DVE, "max_index", dict(out=ci[:gt, h, 0:8], in_max=cv[:gt, h, 0:8], in_values=cand[:gt, h, :]), reads=[candB[h], cvB[h]], writes=[ciB[h]])
                for h in range(8):
                    K.op(DVE, "match_replace", dict(out=cand2[:gt, h, :], in_to_replace=cv[:gt, h, 0:8], in_values=cand[:gt, h, :], imm_value=NEG),
                         reads=[candB[h], cvB[h]], writes=[cand2B[h]])
                for h in range(8):
                    K.op(DVE, "max", dict(out=cv[:gt, h, 8:16], in_=cand2[:gt, h, :]), reads=[cand2B[h]], writes=[cvB[h]])
                for h in range(8):
                    K.op(DVE, "max_index", dict(out=ci[:gt, h, 8:16], in_max=cv[:gt, h, 8:16], in_values=cand2[:gt, h, :]), reads=[cand2B[h], cvB[h]], writes=[ciB[h]])
                K.op(DVE, "tensor_tensor", dict(out=ge[:gt], in0=cv[:gt], in1=cap(cv, 0, [[16, 8], [0, 16]], gt), op=ALU.subtract), reads=cvB, writes=[geB])
                K.op(ACT, "activation", dict(out=ge[:gt], in_=ge[:gt], func=AF.Exp), reads=[geB], writes=[geB])
                K.op(DVE, "tensor_reduce", dict(out=gs[:gt], in_=ge[:gt], axis=AX.X, op=ALU.add), reads=[geB], writes=[gsB])
                K.op(DVE, "reciprocal", dict(out=gs2[:gt], in_=gs[:gt]), reads=[gsB], writes=[gs2B])
                K.op(DVE, "tensor_tensor", dict(out=idxg[:gt, 2, :].rearrange("p (h k) -> p h k", k=16), in0=ge[:gt], in1=cap(gs2, 0, [[1, 8], [0, 16]], gt), op=ALU.mult),
                     reads=[geB, gs2B], writes=[idxgB])
                K.op(DVE, "tensor_scalar", dict(out=cab[:gt, 0, :], in0=ci[:gt].rearrange("p h k -> p (h k)"), scalar1=4, scalar2=None, op0=ALU.logical_shift_right), reads=ciB, writes=[cabB])
                K.op(DVE, "tensor_scalar", dict(out=cab[:gt, 1, :], in0=ci[:gt].rearrange("p h k -> p (h k)"), scalar1=15, scalar2=None, op0=ALU.bitwise_and), reads=ciB, writes=[cabB])
                K.op(DVE, "tensor_copy", dict(out=cabf[:gt], in_=cab[:gt]), reads=[cabB], writes=[cabfB])
                for hf in range(2):
                    K.op(DVE, "tensor_tensor", dict(out=eqt[:gt], in0=cap(cabf, hf * 128, [[16, 8], [1, 16], [0, 16]], gt),
                                                    in1=cap(cst, 2 * 128, [[0, 8], [0, 16], [1, 16]], gt), op=ALU.is_equal),
                         reads=[cabfB, cstB], writes=ssbB)
                    K.op(DVE, "tensor_tensor", dict(out=prod[:gt], in0=eqt[:gt], in1=cap(ixf, hf * 16, [[32, 8], [0, 16], [1, 16]], gt), op=ALU.mult),
                         reads=ssbB + [ixfB], writes=s2B)
                    K.op(DVE, "tensor_reduce", dict(out=idxg[:gt, hf, :], in_=prod[:gt].rearrange("p h a b -> p (h a) b"), axis=AX.X, op=ALU.add),
                         reads=s2B, writes=[idxgB])
                pb = nbank()
                insts = [("transpose", dict(out=PS[pb][:, i * 128:i * 128 + gt], in_=idxg[:gt, i, :], identity=cst[:gt, 0, :gt])) for i in range(3)]
                K.multi(PE, insts, reads=[idxgB, cstB], writes=[PSB[pb]])
                K.op(ACT, "activation", dict(out=idxT[:, :, g0:g0 + gt], in_=PS[pb][:, 0:384].rearrange("p (a b) -> p a b", b=128)[:, :, :gt], func=AF.Identity),
                     reads=[PSB[pb]], writes=[idxTB])
                for t0 in range(g0, g0 + gt, SG):
                    st = nrot("oh", 2)
                    eqA, Aoh, Boh = oh[st]
                    eqAB, AohB, BohB = ohB[st]
                    io = cap(iota_b, 0, [[0, SG], [1, 128]])
                    K.op(DVE, "tensor_tensor", dict(out=eqA[:], in0=io, in1=cap(idxT, 0 * T + t0, [[1, SG], [0, 128]]), op=ALU.is_equal),
                         reads=[iotabB, idxTB], writes=[eqAB])
                    K.op(POOL, "tensor_tensor", dict(out=Aoh[:], in0=eqA[:], in1=cap(idxT, 2 * T + t0, [[1, SG], [0, 128]]), op=ALU.mult),
                         reads=[eqAB, idxTB], writes=[AohB])
                    K.op(DVE, "tensor_tensor", dict(out=Boh[:], in0=io, in1=cap(idxT, 1 * T + t0, [[1, SG], [0, 128]]), op=ALU.is_equal),
                         reads=[iotabB, idxTB], writes=[BohB])
                    for q4 in range(SG // 4):
                        pb = nbank()
                        insts = [("matmul", dict(out=PS[pb][:, i * 128:(i + 1) * 128], lhsT=Aoh[:, q4 * 4 + i, :], rhs=Boh[:, q4 * 4 + i, :], start=True, stop=True))
                                 for i in range(4)]
                        K.multi(PE, insts, reads=[AohB, BohB], writes=[PSB[pb]])
                        tt = t0 + q4 * 4
                        dst = cap(Wsb, tt, [[1, 4], [T, 128]])
                        src = PS[pb][:, :].rearrange("p (a b) -> p a b", b=128)
                        if nrot("cp", 2) == 0:
                            K.op(ACT, "activation", dict(out=dst, in_=src, func=AF.Identity), reads=[PSB[pb]], writes=[WsbB])
                        else:
                            K.op(DVE, "tensor_copy", dict(out=dst, in_=src), reads=[PSB[pb]], writes=[WsbB])
            nb = min(8, 512 // T_)

            def OUT(dc):
                return PS[4 + dc // nb][:, (dc % nb) * T_:(dc % nb + 1) * T_]

            outB = PSB[4:8]

            def load_uv(cg):
                st = nrot("uv", 2)
                off = (l * 128 + cg * G_UV) * 128 * 1024
                K.dma(SP, "ld_u%d" % st, ubuf[st][:], dap(u_b, off, [[1024, 128], [128 * 1024, G_UV], [1, 1024]]), reads=[ubB], writes=[ubufB[st]])
                K.dma(SP, "ld_v%d" % st, vbuf[st][:], dap(v_b, off, [[1024, 128], [128 * 1024, G_UV], [1, 1024]]), reads=[vbB], writes=[vbufB[st]])
                return st

            def zmm(st, gi):
                pb = nbank()
                insts = [("matmul", dict(out=PS[pb][:, :T_], lhsT=ubuf[st][:, gi, k * 128:(k + 1) * 128], rhs=hT[:, k, :T_], start=(k == 0), stop=(k == KC - 1)))
                         for k in range(KC)]
                K.multi(PE, insts, reads=[ubufB[st]] + hTB, writes=[PSB[pb]])
                return pb

            sts = {}
            sts[0] = load_uv(0)
            pend = zmm(sts[0], 0)
            for c in range(128):
                cg, gi = divmod(c, G_UV)
                pb = pend
                i = nrot("gz", 3)
                K.op(ACT, "activation", dict(out=gz[i][:, :T_], in_=PS[pb][:, :T_], func=AF.Gelu_apprx_tanh), reads=[PSB[pb]], writes=[gzB[i]])
                K.op(DVE, "tensor_tensor", dict(out=wz[i][:, :T_], in0=gz[i][:, :T_], in1=Wsb[:, c, :T_], op=ALU.mult), reads=[gzB[i], WsbB], writes=[wzB[i]])
                if c + 1 < 128:
                    cg2, gi2 = divmod(c + 1, G_UV)
                    if gi2 == 0:
                        sts[cg2] = load_uv(cg2)
                    pend = zmm(sts[cg2], gi2)
                insts = [("matmul", dict(out=OUT(dc), lhsT=vbuf[sts[cg]][:, gi, dc * 128:(dc + 1) * 128], rhs=wz[i][:, :T_],
                                         start=(c == 0 and dc % nb == 0), stop=(c == 127))) for dc in range(KC)]
                K.multi(PE, insts, reads=[vbufB[sts[cg]], wzB[i]], writes=outB)
            for dc in range(KC):
                K.op(DVE, "scalar_tensor_tensor", dict(out=x[:, dc, :T_], in0=OUT(dc), scalar=MOD(l, 40 + dc, s), in1=x[:, dc, :T_], op0=ALU.mult, op1=ALU.add),
                     reads=outB + [modvB, xB[dc]], writes=[xB[dc]])

        tiles = []
        for s in range(NPS if not ONLY_SAMPLE else min(1, DBG_TILES)):
            nt = SEQ // T
            for j in range(nt if not ONLY_SAMPLE else DBG_TILES):
                tiles.append((s, s * SEQ + j * T, T, j == 0, j == nt - 1))
        for s in range(NSS):
            tiles.append((NPS + s, NPS * SEQ + s * DSEQ, DSEQ, True, True))
        for (s, tok0, T_, first, last) in tiles:
            K.dma(SP, "ld_x", x[:, :, :T_], xin[:, :, tok0:tok0 + T_], writes=xB)
            lru_stage(T_, s, first, last)
            peer_stage(T_, s, 0)
            conf_stage(T_, s, first, last)
            peer_stage(T_, s, 1)
            stats(T_)
            for k in range(KC):
                K.op(DVE, "scalar_tensor_tensor", dict(out=ybuf[:, k, :T_], in0=x[:, k, :T_], scalar=V("nfin", k), in1=rstd[:, :T_], op0=ALU.mult, op1=ALU.mult),
                     reads=[xB[k], vecB, rstdB], writes=[yB])
            store_toks.append(K.dma(SP, "st_y", yout[:, :, tok0:tok0 + T_], ybuf[:, :, :T_], reads=[yB]))
        last_tok = {}
        for tk in store_toks:
            last_tok[id(tk.sem)] = tk
        K.fence(SP, list(last_tok.values()))

        with nc.Block() as block:
            @block.tensor
            def _(e):
                K.replay(PE, e)

            @block.scalar
            def _(e):
                K.replay(ACT, e)

            @block.vector
            def _(e):
                K.replay(DVE, e)

            @block.gpsimd
            def _(e):
                K.replay(POOL, e)

            @block.sync
            def _(e):
                K.replay(SP, e)
    return nc


_PROG = {}


def _fm(a):
    a = np.asarray(a, dtype=np.float32)
    lead = a.shape[:-1]
    a = a.reshape(lead + (KC, 128))
    nd = a.ndim
    perm = (nd - 1, nd - 2) + tuple(range(nd - 2))
    return np.ascontiguousarray(a.transpose(perm))


def kernel(**inp):
    f = lambda k: np.asarray(inp[k], dtype=np.float32)
    vec = np.zeros((128, NV), np.float32)

    def put(name, v):
        v = np.asarray(v, np.float32).reshape(-1, 128)
        vec[:, VOFF[name]:VOFF[name] + v.shape[0]] = v.T

    put("nm0", f("norm_mix")[0]); put("nm1", f("norm_mix")[1])
    put("nf0", f("norm_ffn")[0]); put("nf1", f("norm_ffn")[1]); put("nfin", f("norm_final"))
    put("b_in", f("lru_b_in")[0])
    for k in range(4):
        put("cw%d" % k, f("lru_conv_w")[0, k])
    put("cb", f("lru_conv_b")[0]); put("gab", f("lru_gate_a_b")[0]); put("gxb", f("lru_gate_x_b")[0])
    put("lam", f("lru_lambda")[0]); put("bout", f("lru_b_out")[0]); put("bpw1", f("cf_b_pw1")[0])
    for k in range(31):
        put("dw%d" % k, f("cf_dw_w")[0, k])
    put("dwb", f("cf_dw_b")[0]); put("lng", f("cf_ln_g")[0]); put("lnb", f("cf_ln_b")[0]); put("bpw2", f("cf_b_pw2")[0])
    adab = np.ascontiguousarray(f("ada_b").reshape(2, 48, 128).transpose(2, 0, 1))
    adaw = np.ascontiguousarray(f("ada_w").reshape(2, KC, 128, 12, 512).transpose(0, 3, 2, 1, 4)).reshape(2, 12, 128, KC * 512)
    gw = np.stack([f("lru_gate_a_w")[0], f("lru_gate_x_w")[0]], 0)
    gw = np.ascontiguousarray(gw.transpose(2, 0, 1, 3)).reshape(128, 2 * 8 * 128)
    kk = np.stack([f("peer_k1")[0], f("peer_k2")[0], f("peer_k1")[1], f("peer_k2")[1]], 0)
    kT = np.ascontiguousarray(kk.transpose(2, 0, 1)).reshape(128, 4 * 128)
    wcat = np.concatenate([f("lru_w_in")[0], f("lru_w_out")[0], f("cf_w_pw1")[0], f("cf_w_pw2")[0], f("peer_w_q")[0], f("peer_w_q")[1]], axis=1)
    wd = np.ascontiguousarray(wcat.reshape(KC, 128, NPIECE, 256).transpose(2, 1, 0, 3)).reshape(-1, 2048)
    u_arr = np.ascontiguousarray(f("peer_u").reshape(2, 128, 128, KC, 128).transpose(0, 2, 4, 3, 1)).reshape(-1, 2048)
    v_arr = np.ascontiguousarray(f("peer_v").reshape(2, 128, 128, D).transpose(0, 2, 1, 3)).reshape(-1, 2048)
    consts = np.zeros((128, 4, 128), np.float32)
    consts[:, 3, 1] = 1.0
    consts[:, 3, 2] = EPS
    consts[:, 0, :] = np.eye(128, dtype=np.float32)
    consts[:, 1, :] = 1.0
    consts[:, 2, :] = np.arange(128, dtype=np.float32)[None, :]
    xp_, xs_ = f("x_prompt"), f("x_sample")
    cp_, cs_ = f("c_prompt"), f("c_sample")
    sh_, sc_, sd_ = f("state_lru_h"), f("state_lru_conv"), f("state_dwconv")
    in_maps = []
    for c in range(NCORES):
        xt = np.concatenate([xp_[NPS * c:NPS * c + NPS].reshape(-1, D), xs_[NSS * c:NSS * c + NSS].reshape(-1, D)], 0)
        cc = np.concatenate([cp_[NPS * c:NPS * c + NPS], cs_[NSS * c:NSS * c + NSS]], 0)
        in_maps.append(dict(
            xin=_fm(xt), cT=_fm(cc), st_h=np.ascontiguousarray(_fm(sh_[0, NSS * c:NSS * c + NSS]).transpose(0, 2, 1)),
            st_conv=_fm(sc_[0, NSS * c:NSS * c + NSS]), st_dw=_fm(sd_[0, NSS * c:NSS * c + NSS]),
            consts=consts, vec=vec, adab=adab, adaw=adaw, gw=gw, kT=kT, wd=wd, u_arr=u_arr, v_arr=v_arr))
    if "p" not in _PROG:
        _PROG["p"] = build_program()
    res = run_bass_kernel_spmd(_PROG["p"], in_maps, core_ids=list(range(NCORES)))
    B, DB = NPS * NCORES, NSS * NCORES
    y_p = np.zeros((B, SEQ, D), np.float32); y_s = np.zeros((DB, DSEQ, D), np.float32)
    h_p = np.zeros((1, B, D), np.float32); h_s = np.zeros((1, DB, D), np.float32)
    cv_p = np.zeros((1, B, 3, D), np.float32); cv_s = np.zeros((1, DB, 3, D), np.float32)
    dw_p = np.zeros((1, B, 30, D), np.float32); dw_s = np.zeros((1, DB, 30, D), np.float32)

    def unfm(a):
        nd = a.ndim
        perm = tuple(range(2, nd)) + (1, 0)
        a = a.transpose(perm)
        return a.reshape(a.shape[:-2] + (D,))

    for c in range(NCORES):
        r = res.results[c]
        y = unfm(np.asarray(r["yout"]))
        y_p[NPS * c:NPS * c + NPS] = y[:NPS * SEQ].reshape(NPS, SEQ, D)
        y_s[NSS * c:NSS * c + NSS] = y[NPS * SEQ:].reshape(NSS, DSEQ, D)
        h = unfm(np.ascontiguousarray(np.asarray(r["ho"]).transpose(0, 2, 1)))
        h_p[0, NPS * c:NPS * c + NPS] = h[:NPS]; h_s[0, NSS * c:NSS * c + NSS] = h[NPS:]
        cvv = unfm(np.asarray(r["convo"]))
        cv_p[0, NPS * c:NPS * c + NPS] = cvv[:NPS]; cv_s[0, NSS * c:NSS * c + NSS] = cvv[NPS:]
        dww = unfm(np.asarray(r["dwo"]))
        dw_p[0, NPS * c:NPS * c + NPS] = dww[:NPS]; dw_s[0, NSS * c:NSS * c + NSS] = dww[NPS:]
    return (y_p, y_s, h_p, cv_p, dw_p, h_s, cv_s, dw_s)
```

```python
import numpy as np
from contextlib import ExitStack
import concourse.bass as bass
import concourse.mybir as mybir
from concourse.bass_utils import run_bass_kernel_spmd

F32 = mybir.dt.float32
BF16 = mybir.dt.bfloat16
U32 = mybir.dt.uint32
ALU = mybir.AluOpType
AF = mybir.ActivationFunctionType
AX = mybir.AxisListType

NCORES = 8
D = 1024
KC = 8
TMAX = 256
SEQ = 2048
NPS = 4
NSS = 2
DSEQ = 32
NTOK = NPS * SEQ + NSS * DSEQ
EPS = 1e-6
G_UV = 2
SG = 8
NEG = -1.0e30
import os
ONLY_SAMPLE = bool(int(os.environ.get("PEER_ONLY_SAMPLE", "0")))
DBG_TILES = int(os.environ.get("PEER_DBG_TILES", "0"))

VEC_SPEC = [("nm0", 8), ("nm1", 8), ("nf0", 8), ("nf1", 8), ("nfin", 8), ("b_in", 16),
            ("cw0", 8), ("cw1", 8), ("cw2", 8), ("cw3", 8), ("cb", 8), ("gab", 8), ("gxb", 8),
            ("lam", 8), ("bout", 8), ("bpw1", 16)] + [("dw%d" % k, 8) for k in range(31)] + \
           [("dwb", 8), ("lng", 8), ("lnb", 8), ("bpw2", 8)]
VOFF = {}
_o = 0
for _n, _w in VEC_SPEC:
    VOFF[_n] = _o
    _o += _w
NV = _o
PIECE = {"w_in": 0, "w_out": 8, "pw1": 12, "pw2": 20, "wq0": 24, "wq1": 32}
NPIECE = 40


class Tok:
    __slots__ = ("sem", "val")

    def __init__(self, sem, val):
        self.sem = sem
        self.val = val


class Buf:
    def __init__(self, name):
        self.name = name
        self.w = None
        self.r = {}


class Eng:
    def __init__(self, name):
        self.name = name
        self.sem = None
        self.cnt = 0
        self.seen = {}
        self.ops = []


class Sched:
    def __init__(self):
        self.pe = Eng("pe")
        self.act = Eng("act")
        self.dve = Eng("dve")
        self.pool = Eng("pool")
        self.sp = Eng("sp")
        self.engs = [self.pe, self.act, self.dve, self.pool, self.sp]
        self.dsem = {}
        self.semlist = []

    def _waits(self, E, reads, writes):
        waits = {}

        def need(tok, same_ok):
            if tok is None:
                return
            if tok.sem is E.sem and (same_ok or E is self.pe):
                return
            k = id(tok.sem)
            if k not in waits or waits[k][1] < tok.val:
                waits[k] = (tok.sem, tok.val)

        for b in reads:
            need(b.w, False)
        for b in writes:
            need(b.w, True)
            for sem, val in b.r.values():
                need(Tok(sem, val), True)
        wl = []
        for k, (sem, val) in waits.items():
            if E.seen.get(k, 0) < val:
                E.seen[k] = val
                wl.append((sem, val))
        return wl

    def _commit(self, tok, reads, writes):
        k = id(tok.sem)
        for b in reads:
            if k not in b.r or b.r[k][1] < tok.val:
                b.r[k] = (tok.sem, tok.val)
        for b in writes:
            b.w = tok
            b.r = {}

    def op(self, E, name, kw, reads=(), writes=()):
        if name == "activation" and "bias" not in kw:
            kw["bias"] = self.zero[:kw["in_"].shape[0]]
        return self.multi(E, [(name, kw)], reads, writes)

    def multi(self, E, insts, reads=(), writes=()):
        wl = self._waits(E, reads, writes)
        E.cnt += 1
        tok = Tok(E.sem, E.cnt)
        E.ops.append((wl, insts, E.sem, 1))
        self._commit(tok, reads, writes)
        return tok

    def dma(self, Q, key, out, in_, reads=(), writes=()):
        wl = self._waits(Q, reads, writes)
        ent = self.dsem[key]
        ent[1] += 16
        tok = Tok(ent[0], ent[1])
        Q.ops.append((wl, [("dma_start", dict(out=out, in_=in_))], ent[0], 16))
        self._commit(tok, reads, writes)
        return tok

    def fence(self, E, toks):
        wl = []
        for tok in toks:
            k = id(tok.sem)
            if E.seen.get(k, 0) < tok.val:
                E.seen[k] = tok.val
                wl.append((tok.sem, tok.val))
        if wl:
            E.ops.append((wl, [], None, 0))

    def replay(self, E, e):
        for wl, insts, sem, inc in E.ops:
            for s, v in wl:
                e.wait_ge(s, v)
            last = None
            for name, kw in insts:
                last = getattr(e, name)(**kw)
            if last is not None and sem is not None:
                last.then_inc(sem, inc)


def cap(full, off, dims, nparts=128):
    pstep = full.ap[0][0]
    return bass.AP(tensor=full.tensor, offset=off, ap=[[pstep, nparts]] + [list(d) for d in dims])


def dap(t, off, dims):
    return bass.AP(tensor=t.tensor, offset=off, ap=[list(d) for d in dims])


def build_program():
    nc = bass.Bass("TRN2", target_bir_lowering=False)
    K = Sched()
    PE, ACT, DVE, POOL, SP = K.pe, K.act, K.dve, K.pool, K.sp

    def din(name, shape, dt=F32):
        return nc.dram_tensor(name, list(shape), dt, kind="ExternalInput").ap()

    def dout(name, shape, dt=F32):
        return nc.dram_tensor(name, list(shape), dt, kind="ExternalOutput").ap()

    xin = din("xin", [128, KC, NTOK])
    cT = din("cT", [128, KC, 6])
    st_h = din("st_h", [128, NSS, KC])
    st_conv = din("st_conv", [128, KC, NSS, 3])
    st_dw = din("st_dw", [128, KC, NSS, 30])
    consts = din("consts", [128, 4, 128])
    vec_d = din("vec", [128, NV])
    adab_d = din("adab", [128, 2, 48])
    adaw_d = din("adaw", [2, 12, 128, KC * 512])
    gw_d = din("gw", [128, 2 * 8 * 128])
    kT_d = din("kT", [128, 4 * 128])
    wd_d = din("wd", [NPIECE * 128 * KC * 256 // 2048, 2048])
    u_d = din("u_arr", [2 * 128 * 128 * 1024 // 2048, 2048])
    v_d = din("v_arr", [2 * 128 * 128 * 1024 // 2048, 2048])
    wd_b = nc.dram_tensor("wd_b", [NPIECE * 128 * KC * 256 // 2048, 2048], BF16, kind="Internal").ap()
    u_b = nc.dram_tensor("u_b", [2 * 128 * 128 * 1024 // 2048, 2048], BF16, kind="Internal").ap()
    v_b = nc.dram_tensor("v_b", [2 * 128 * 128 * 1024 // 2048, 2048], BF16, kind="Internal").ap()
    yout = dout("yout", [128, KC, NTOK])
    ho = dout("ho", [128, 6, KC])
    convo = dout("convo", [128, KC, 6, 3])
    dwo = dout("dwo", [128, KC, 6, 30])

    es = ExitStack()
    with es:
        sb_off = [20480]

        def SBt(name, shape, dt, at=None):
            esz = 2 if dt == BF16 else 4
            n = 1
            for s in shape[1:]:
                n *= s
            nbytes = (n * esz + 63) // 64 * 64
            if at is None:
                at = sb_off[0]
                sb_off[0] += nbytes
            t = nc.alloc_sbuf_tensor_at(name, list(shape), dt, offset=at)
            return t.ap()

        T = TMAX
        cst = SBt("cst", [128, 4, 128], F32)
        ZERO = cst[:, 3, 0:1]
        ONE = cst[:, 3, 1:2]
        EPSA = cst[:, 3, 2:3]
        K.zero = ZERO
        ident = cst[:, 0, :]
        ones = cst[:, 1, :]
        iota_f = cst[:, 2, :]
        iota_b = SBt("iota_b", [128, 128], BF16)
        vec = SBt("vec", [128, NV], F32)
        adab = SBt("adab", [128, 2, 48], F32)
        modv = SBt("modv", [128, 2, 48, 6], F32)
        G1 = SBt("G1", [128, 2, 8, 6], F32)
        GB1 = SBt("GB1", [128, 2, 8, 6], F32)
        G2 = SBt("G2", [128, 2, 8, 6], F32)
        cA = SBt("cA", [128, 8], F32)
        c2A = SBt("c2A", [128, 8], F32)
        ptmp = SBt("ptmp", [128, 4, 8], F32)
        cTs = SBt("cTs", [128, KC, 6], F32)
        csil = SBt("csil", [128, KC, 6], F32)
        gwb = SBt("gwb", [128, 2, 8, 128], BF16)
        kTb = SBt("kTb", [128, 4, 128], BF16)
        hstate = SBt("hstate", [128, 8], F32)
        xp = SBt("xp", [128, KC, 3 + T], F32)
        xpc = SBt("xpc", [128, KC, 30 + T], F32)
        x = SBt("x", [128, KC, T], F32)
        hT = SBt("hT", [128, KC, T], BF16)
        sq = SBt("sq", [128, KC, T], F32)
        sdt = SBt("sdt", [128, T], F32)
        rstd = SBt("rstd", [128, T], F32)
        tmpk = [SBt("tmpk%d" % i, [128, T], F32) for i in range(2)]
        wslot = [SBt("wslot%d" % i, [128, KC, 256], BF16) for i in range(3)]
        ubuf = [SBt("ubuf%d" % i, [128, G_UV, 1024], BF16) for i in range(2)]
        vbuf = [SBt("vbuf%d" % i, [128, G_UV, 1024], BF16) for i in range(2)]
        arena_off = sb_off[0]
        gate = SBt("gate", [128, KC, T], BF16)
        xc = SBt("xc", [128, KC, T], F32)
        xcb = SBt("xcb", [128, KC, T], BF16)
        rr = SBt("rr", [128, KC, T], F32)
        ii_off = sb_off[0]
        ii = SBt("ii", [128, KC, T], F32)
        ybuf = SBt("ybuf", [128, KC, T], F32, at=ii_off)
        aa_off = sb_off[0]
        aa = SBt("aa", [128, KC, T], F32)
        mm_ = SBt("mm", [128, KC, T], F32)
        yin_off = sb_off[0]
        yin = SBt("yin", [128, KC, T], BF16)
        rtmp = [SBt("rtmp%d" % i, [128, T], F32) for i in range(2)]
        sgm = rr
        dd = xc
        tsz = T * 4
        mu = SBt("mu", [128, T], F32, at=aa_off)
        musq = SBt("musq", [128, T], F32, at=aa_off + tsz)
        var = SBt("var", [128, T], F32, at=aa_off + 2 * tsz)
        sd2 = SBt("sd2", [128, T], F32, at=aa_off + 3 * tsz)
        rs2 = SBt("rs2", [128, T], F32, at=aa_off + 4 * tsz)
        t1 = [SBt("t1_%d" % i, [128, T], F32, at=aa_off + (5 + i) * tsz) for i in range(2)]
        dnb = SBt("dnb", [128, KC, T], BF16, at=yin_off)
        arena_end = sb_off[0]
        sb_off[0] = max(arena_end, arena_off + 128 * T * 2)
        qT = SBt("qT", [128, 16, T], BF16)
        s_off = sb_off[0]
        s_sb = SBt("s_sb", [128, 16, 128], F32)
        s2_off = sb_off[0]
        s2 = SBt("s2", [128, 16, 128], F32)
        eqt = SBt("eqt", [128, 8, 16, 16], F32, at=s_off)
        prod = SBt("prod", [128, 8, 16, 16], F32, at=s2_off)
        v1 = SBt("v1", [128, 16, 16], F32)
        ix = SBt("ix", [128, 16, 16], U32)
        ixf = SBt("ixf", [128, 16, 16], F32)
        cand = SBt("cand", [128, 8, 256], F32, at=s_off)
        cand2 = SBt("cand2", [128, 8, 256], F32, at=s2_off)
        cv = SBt("cv", [128, 8, 16], F32)
        ci = SBt("ci", [128, 8, 16], U32)
        cab = SBt("cab", [128, 2, 128], U32)
        cabf = SBt("cabf", [128, 2, 128], F32)
        ge = SBt("ge", [128, 8, 16], F32)
        gs = SBt("gs", [128, 8], F32)
        gs2 = SBt("gs2", [128, 8], F32)
        idxg = SBt("idxg", [128, 3, 128], F32)
        idxT = SBt("idxT", [128, 3, T], BF16)
        oh = [[SBt("oh%d_%d" % (s, i), [128, SG, 128], BF16) for i in range(3)] for s in range(2)]
        wsb_off = arena_off
        Wsb = SBt("Wsb", [128, 128, T], BF16, at=arena_off)
        assert arena_end - arena_off <= 128 * T * 2
        print("SBUF end", sb_off[0], "arena", arena_off, arena_end - arena_off)
        gz = [SBt("gz%d" % i, [128, T], BF16) for i in range(3)]
        wz = [SBt("wz%d" % i, [128, T], BF16) for i in range(3)]
        assert sb_off[0] <= 224 * 1024 - 2048, sb_off[0]
        adas = [SBt("adas%d" % i, [128, KC, 512], F32, at=wsb_off + i * 16384) for i in range(2)]
        gws = SBt("gws", [128, 2 * 8 * 128], F32, at=s_off)
        kTs = SBt("kTs", [128, 4 * 128], F32, at=s2_off)

        PS = [es.enter_context(nc.psum_tensor("ps%d" % i, [128, 512], F32)) for i in range(8)]
        PSB = [Buf("ps%d" % i) for i in range(8)]
        for E in K.engs:
            E.sem = es.enter_context(nc.semaphore("sem_" + E.name))

        def dsem(key):
            K.dsem[key] = [es.enter_context(nc.semaphore("d_" + key)), 0]

        for key in ["ld_misc", "ld_x", "ld_w0", "ld_w1", "ld_w2", "ld_u0", "ld_u1", "ld_v0", "ld_v1",
                    "st_y", "st_c", "st_h", "st_d", "cast_w", "cast_u", "cast_v", "ld_a0", "ld_a1", "ld_st"]:
            dsem(key)

        rot = {"bank": 0, "w": 0, "uv": 0, "tmpk": 0, "rtmp": 0, "t1": 0, "gz": 0, "oh": 0, "cp": 0}

        def nbank():
            b = rot["bank"]
            rot["bank"] = (b + 1) % 4
            return b

        def nrot(key, n):
            v = rot[key]
            rot[key] = (v + 1) % n
            return v

        B = lambda n: Buf(n)
        cstB, vecB, adabB, modvB, GB_, cAB, csilB, gwbB, kTbB = B("cst"), B("vec"), B("adab"), B("modv"), B("G"), B("cA"), B("csil"), B("gwb"), B("kTb")
        iotabB, ptmpB, cTsB, gwsB, kTsB = B("iotab"), B("ptmp"), B("cTs"), B("gws"), B("kTs")
        hstB = [B("hst%d" % j) for j in range(8)]
        xpB = [B("xp%d" % j) for j in range(8)]
        xpcB = [B("xpc%d" % j) for j in range(8)]
        xB = [B("x%d" % j) for j in range(8)]
        hTB = [B("hT%d" % j) for j in range(8)]
        yB = B("ybuf")
        sqB = [B("sq%d" % j) for j in range(8)]
        sdtB, rstdB = B("sdt"), B("rstd")
        tmpkB = [B("tmpk0"), B("tmpk1")]
        wslotB = [B("ws%d" % i) for i in range(3)]
        ubufB = [B("ub0"), B("ub1")]
        vbufB = [B("vb0"), B("vb1")]
        gateB = [B("gate%d" % j) for j in range(8)]
        xcB = [B("xc%d" % j) for j in range(8)]
        xcbB = [B("xcb%d" % j) for j in range(8)]
        rrB = [B("rr%d" % j) for j in range(8)]
        iiB = [B("ii%d" % j) for j in range(8)]
        aaB = [B("aa%d" % j) for j in range(8)]
        mmB = [B("mm%d" % j) for j in range(8)]
        yinB = [B("yin%d" % j) for j in range(8)]
        rtmpB = [B("rtmp0"), B("rtmp1")]
        muB, musqB, varB, sd2B, rs2B = B("mu"), B("musq"), B("var"), B("sd2"), B("rs2")
        t1B = [B("t1_0"), B("t1_1")]
        arenaB = gateB + xcB + xcbB + rrB + iiB + aaB + mmB + yinB + rtmpB + [muB, musqB, varB, sd2B, rs2B] + t1B
        dnbB = yinB
        qTB = [B("qT%d" % j) for j in range(16)]
        ssbB = [B("ssb%d" % j) for j in range(4)]
        s2B = [B("s2_%d" % j) for j in range(4)]
        v1B = [B("v1_%d" % j) for j in range(16)]
        ixB = [B("ix_%d" % j) for j in range(16)]
        ixfB, candB, cvB, ciB, cabB, cabfB, geB, gsB, gs2B, idxgB = B("ixf"), [ssbB[h // 2] for h in range(8)], [B("cv%d" % h) for h in range(8)], [B("ci%d" % h) for h in range(8)], B("cab"), B("cabf"), B("ge"), B("gs"), B("gs2"), B("idxg")
        cand2B = [s2B[h // 2] for h in range(8)]
        idxTB = B("idxT")
        ohB = [[B("oh%d_%d" % (s, i)) for i in range(3)] for s in range(2)]
        WsbB = B("Wsb")
        gzB = [B("gz%d" % i) for i in range(3)]
        wzB = [B("wz%d" % i) for i in range(3)]
        adasB = [B("adas0"), B("adas1")]
        wdbB, ubB, vbB = B("wd_b"), B("u_b"), B("v_b")
        store_toks = []

        def V(name, j=0):
            o = VOFF[name] + j
            return vec[:, o:o + 1]

        K.dma(SP, "ld_misc", cst[:], consts[:, :, :], writes=[cstB])
        K.dma(SP, "ld_misc", vec[:], vec_d[:, :], writes=[vecB])
        K.dma(SP, "ld_misc", adab[:], adab_d[:, :, :], writes=[adabB])
        K.dma(SP, "ld_misc", cTs[:], cT[:, :, :], writes=[cTsB])
        K.dma(SP, "ld_misc", gws[:], gw_d[:, :], writes=[gwsB])
        tk_misc = K.dma(SP, "ld_misc", kTs[:], kT_d[:, :], writes=[kTsB])
        for bb in (cstB, vecB, adabB, cTsB, gwsB, kTsB):
            bb.w = tk_misc
        R = 2048
        nrow = wd_d.shape[0]
        r0 = 0
        while r0 < nrow:
            n = min(R, nrow - r0)
            K.dma(POOL, "cast_w", wd_b[r0:r0 + n, :], wd_d[r0:r0 + n, :], writes=[wdbB])
            r0 += n
        for (src, dst, bb, ck) in ((u_d, u_b, ubB, "cast_u"), (v_d, v_b, vbB, "cast_v")):
            for r0 in range(0, src.shape[0], R):
                K.dma(POOL, ck, dst[r0:r0 + R, :], src[r0:r0 + R, :], writes=[bb])
        K.op(DVE, "tensor_copy", dict(out=iota_b[:], in_=iota_f), reads=[cstB], writes=[iotabB])
        K.op(DVE, "tensor_copy", dict(out=gwb[:].rearrange("p a h j -> p (a h j)"), in_=gws[:]), reads=[gwsB], writes=[gwbB])
        K.op(DVE, "tensor_copy", dict(out=kTb[:].rearrange("p a k -> p (a k)"), in_=kTs[:]), reads=[kTsB], writes=[kTbB])
        K.op(ACT, "activation", dict(out=csil[:], in_=cTs[:], func=AF.Silu), reads=[cTsB], writes=[csilB])
        lam = vec[:, VOFF["lam"]:VOFF["lam"] + 8]
        K.op(DVE, "tensor_scalar", dict(out=ptmp[:, 3, :], in0=lam, scalar1=-1.0, scalar2=None, op0=ALU.mult), reads=[vecB], writes=[ptmpB])
        K.op(DVE, "tensor_tensor", dict(out=ptmp[:, 0, :], in0=lam, in1=ptmp[:, 3, :], op=ALU.max), reads=[vecB, ptmpB], writes=[ptmpB])
        K.op(ACT, "activation", dict(out=ptmp[:, 1, :], in_=ptmp[:, 0, :], func=AF.Exp, scale=-1.0), reads=[ptmpB], writes=[ptmpB])
        K.op(ACT, "activation", dict(out=ptmp[:, 2, :], in_=ptmp[:, 1, :], func=AF.Ln, bias=ONE), reads=[ptmpB], writes=[ptmpB])
        K.op(DVE, "tensor_scalar", dict(out=ptmp[:, 3, :], in0=lam, scalar1=-1.0, scalar2=0.0, op0=ALU.mult, op1=ALU.max), reads=[vecB, ptmpB], writes=[ptmpB])
        K.op(DVE, "tensor_tensor", dict(out=ptmp[:, 3, :], in0=ptmp[:, 3, :], in1=ptmp[:, 2, :], op=ALU.add), reads=[ptmpB], writes=[ptmpB])
        K.op(DVE, "tensor_scalar", dict(out=cA[:], in0=ptmp[:, 3, :], scalar1=-8.0, scalar2=None, op0=ALU.mult), reads=[ptmpB], writes=[cAB])
        K.op(DVE, "tensor_scalar", dict(out=c2A[:], in0=ptmp[:, 3, :], scalar1=-16.0, scalar2=None, op0=ALU.mult), reads=[ptmpB], writes=[cAB])
        for l in range(2):
            pb = nbank()
            for pc in range(12):
                sl = pc % 2
                K.dma(SP, "ld_a%d" % sl, adas[sl][:].rearrange("p k o -> p (k o)"), adaw_d[l, pc, :, :], writes=[adasB[sl]])
                for o4 in range(4):
                    oc = pc * 4 + o4
                    insts = [("matmul", dict(out=PS[pb][:, oc * 6:(oc + 1) * 6], lhsT=adas[sl][:, k, o4 * 128:(o4 + 1) * 128],
                                             rhs=csil[:, k, :], start=(k == 0), stop=(k == KC - 1))) for k in range(KC)]
                    K.multi(PE, insts, reads=[adasB[sl], csilB], writes=[PSB[pb]])
            K.op(DVE, "tensor_tensor", dict(out=modv[:, l, :, :], in0=PS[pb][:, 0:288].rearrange("p (o s) -> p o s", s=6),
                                            in1=cap(adab, l * 48, [[1, 48], [0, 6]]), op=ALU.add),
                 reads=[PSB[pb], adabB], writes=[modvB])
            nmn = "nm%d" % l
            nfn = "nf%d" % l
            bon = "bout" if l == 0 else "bpw2"
            K.op(DVE, "scalar_tensor_tensor", dict(out=G1[:, l, :, :], in0=modv[:, l, 8:16, :], scalar=1.0,
                                                   in1=cap(vec, VOFF[nmn], [[1, 8], [0, 6]]), op0=ALU.add, op1=ALU.mult),
                 reads=[modvB, vecB], writes=[GB_])
            K.op(DVE, "scalar_tensor_tensor", dict(out=G2[:, l, :, :], in0=modv[:, l, 32:40, :], scalar=1.0,
                                                   in1=cap(vec, VOFF[nfn], [[1, 8], [0, 6]]), op0=ALU.add, op1=ALU.mult),
                 reads=[modvB, vecB], writes=[GB_])
            K.op(DVE, "tensor_tensor", dict(out=GB1[:, l, :, :], in0=modv[:, l, 16:24, :],
                                            in1=cap(vec, VOFF[bon], [[1, 8], [0, 6]]), op=ALU.mult),
                 reads=[modvB, vecB], writes=[GB_])

        def MOD(l, oc, s):
            return modv[:, l, oc, s:s + 1]

        def load_piece(pc):
            sl = nrot("w", 3)
            src = dap(wd_b, pc * 128 * KC * 256, [[KC * 256, 128], [1, KC * 256]])
            K.dma(SP, "ld_w%d" % sl, wslot[sl][:].rearrange("p k o -> p (k o)"), src, reads=[wdbB], writes=[wslotB[sl]])
            return sl

        def stats(T_):
            for k in range(KC):
                K.op(ACT, "activation", dict(out=sq[:, k, :T_], in_=x[:, k, :T_], func=AF.Square), reads=[xB[k]], writes=[sqB[k]])
            pb = nbank()
            insts = [("matmul", dict(out=PS[pb][:, :T_], lhsT=ones, rhs=sq[:, k, :T_], start=(k == 0), stop=(k == KC - 1))) for k in range(KC)]
            K.multi(PE, insts, reads=sqB + [cstB], writes=[PSB[pb]])
            K.op(ACT, "activation", dict(out=sdt[:, :T_], in_=PS[pb][:, :T_], func=AF.Sqrt, bias=EPSA, scale=1.0 / D), reads=[PSB[pb]], writes=[sdtB])
            K.op(DVE, "reciprocal", dict(out=rstd[:, :T_], in_=sdt[:, :T_]), reads=[sdtB], writes=[rstdB])

        def modnorm(T_, Gt, l, shoc, s):
            for k in range(KC):
                i = nrot("tmpk", 2)
                K.op(DVE, "scalar_tensor_tensor", dict(out=tmpk[i][:, :T_], in0=x[:, k, :T_], scalar=Gt[:, l, k, s:s + 1], in1=rstd[:, :T_],
                                                       op0=ALU.mult, op1=ALU.mult), reads=[xB[k], GB_, rstdB], writes=[tmpkB[i]])
                K.op(ACT, "activation", dict(out=hT[:, k, :T_], in_=tmpk[i][:, :T_], func=AF.Identity, bias=MOD(l, shoc + k, s)),
                     reads=[tmpkB[i], modvB], writes=[hTB[k]])

        def proj(T_, pc, oo, rhs_t, rhsB):
            sl = proj.cur
            pb = nbank()
            insts = [("matmul", dict(out=PS[pb][:, :T_], lhsT=wslot[sl][:, k, oo * 128:(oo + 1) * 128], rhs=rhs_t[:, k, :T_],
                                     start=(k == 0), stop=(k == KC - 1))) for k in range(KC)]
            K.multi(PE, insts, reads=[wslotB[sl]] + rhsB, writes=[PSB[pb]])
            return pb

        def resid_add(T_, pb, l, dc, s):
            i = nrot("rtmp", 2)
            K.op(ACT, "activation", dict(out=rtmp[i][:, :T_], in_=PS[pb][:, :T_], func=AF.Identity, bias=GB1[:, l, dc, s:s + 1], scale=MOD(l, 16 + dc, s)),
                 reads=[PSB[pb], GB_, modvB], writes=[rtmpB[i]])
            K.op(POOL, "tensor_tensor", dict(out=x[:, dc, :T_], in0=x[:, dc, :T_], in1=rtmp[i][:, :T_], op=ALU.add), reads=[rtmpB[i], xB[dc]], writes=[xB[dc]])

        def fence_bufs(engs, bufs):
            toks = []
            for b in bufs:
                if b.w is not None:
                    toks.append(b.w)
                toks += [Tok(sem, val) for sem, val in b.r.values()]
            for E in engs:
                K.fence(E, toks)

        def lru_stage(T_, s, first, last):
            fence_bufs([ACT, DVE, POOL], [WsbB])
            if first:
                if s < NPS:
                    for j in range(KC):
                        K.op(POOL, "memset", dict(ap=xp[:, j, 0:3], constant=0.0), writes=[xpB[j]])
                        K.op(POOL, "memset", dict(ap=hstate[:, j:j + 1], constant=0.0), writes=[hstB[j]])
                else:
                    K.dma(SP, "ld_st", xp[:, :, 0:3], st_conv[:, :, s - NPS, :], writes=xpB)
                    tk_st = K.dma(SP, "ld_st", hstate[:], st_h[:, s - NPS, :], writes=hstB)
                    for bb in xpB:
                        bb.w = tk_st
            stats(T_)
            modnorm(T_, G1, 0, 0, s)
            for pc in range(8):
                proj.cur = load_piece(PIECE["w_in"] + pc)
                for oo in range(2):
                    oc = pc * 2 + oo
                    pb = proj(T_, pc, oo, hT, hTB)
                    if oc < 8:
                        K.op(ACT, "activation", dict(out=gate[:, oc, :T_], in_=PS[pb][:, :T_], func=AF.Gelu_apprx_tanh, bias=V("b_in", oc)),
                             reads=[PSB[pb], vecB], writes=[gateB[oc]])
                    else:
                        j = oc - 8
                        K.op(ACT, "activation", dict(out=xp[:, j, 3:3 + T_], in_=PS[pb][:, :T_], func=AF.Identity, bias=V("b_in", oc)),
                             reads=[PSB[pb], vecB], writes=[xpB[j]])
            for j in range(KC):
                K.op(DVE, "tensor_scalar", dict(out=xc[:, j, :T_], in0=xp[:, j, 0:T_], scalar1=V("cw0", j), scalar2=V("cb", j), op0=ALU.mult, op1=ALU.add),
                     reads=[xpB[j], vecB], writes=[xcB[j]])
            for k in range(1, 4):
                for j in range(KC):
                    K.op(DVE, "scalar_tensor_tensor", dict(out=xc[:, j, :T_], in0=xp[:, j, k:k + T_], scalar=V("cw%d" % k, j), in1=xc[:, j, :T_],
                                                           op0=ALU.mult, op1=ALU.add), reads=[xpB[j], vecB, xcB[j]], writes=[xcB[j]])
            for j in range(KC):
                K.op(POOL, "tensor_copy", dict(out=xcb[:, j, :T_], in_=xc[:, j, :T_]), reads=[xcB[j]], writes=[xcbB[j]])
                if last:
                    pass
            if last:
                store_toks.append(K.dma(SP, "st_c", convo[:, :, s, :], xp[:, :, T_:T_ + 3], reads=xpB))
            for j in range(KC):
                K.op(POOL, "tensor_copy", dict(out=xp[:, j, 0:3], in_=xp[:, j, T_:T_ + 3]), reads=[xpB[j]], writes=[xpB[j]])
            for j in range(KC):
                for a_, dst, dstB, bn in ((0, rr, rrB, "gab"), (1, ii, iiB, "gxb")):
                    pb = nbank()
                    K.multi(PE, [("matmul", dict(out=PS[pb][:, :T_], lhsT=gwb[:, a_, j, :], rhs=xcb[:, j, :T_], start=True, stop=True))],
                            reads=[gwbB, xcbB[j]], writes=[PSB[pb]])
                    K.op(ACT, "activation", dict(out=dst[:, j, :T_], in_=PS[pb][:, :T_], func=AF.Sigmoid, bias=V(bn, j)),
                         reads=[PSB[pb], vecB], writes=[dstB[j], yB])
            for j in range(KC):
                K.op(ACT, "activation", dict(out=aa[:, j, :T_], in_=rr[:, j, :T_], func=AF.Exp, scale=cA[:, j:j + 1]), reads=[rrB[j], cAB], writes=[aaB[j]])
                K.op(ACT, "activation", dict(out=mm_[:, j, :T_], in_=rr[:, j, :T_], func=AF.Exp, scale=c2A[:, j:j + 1]), reads=[rrB[j], cAB], writes=[mmB[j]])
            for j in range(KC):
                K.op(ACT, "activation", dict(out=mm_[:, j, :T_], in_=mm_[:, j, :T_], func=AF.Sqrt, bias=ONE, scale=-1.0), reads=[mmB[j]], writes=[mmB[j]])
            for j in range(KC):
                K.op(DVE, "tensor_tensor", dict(out=mm_[:, j, :T_], in0=mm_[:, j, :T_], in1=ii[:, j, :T_], op=ALU.mult), reads=[mmB[j], iiB[j]], writes=[mmB[j]])
            for j in range(KC):
                K.op(DVE, "tensor_tensor", dict(out=mm_[:, j, :T_], in0=mm_[:, j, :T_], in1=xc[:, j, :T_], op=ALU.mult), reads=[mmB[j], xcB[j]], writes=[mmB[j]])
            for j in range(KC):
                K.op(DVE, "tensor_tensor_scan", dict(out=rr[:, j, :T_], data0=aa[:, j, :T_], data1=mm_[:, j, :T_], initial=hstate[:, j:j + 1],
                                                     op0=ALU.mult, op1=ALU.add), reads=[aaB[j], mmB[j], hstB[j]], writes=[rrB[j]])
            for j in range(KC):
                K.op(POOL, "tensor_copy", dict(out=hstate[:, j:j + 1], in_=rr[:, j, T_ - 1:T_]), reads=[rrB[j]], writes=[hstB[j]])
                K.op(DVE, "tensor_tensor", dict(out=yin[:, j, :T_], in0=rr[:, j, :T_], in1=gate[:, j, :T_], op=ALU.mult), reads=[rrB[j], gateB[j]], writes=[yinB[j]])
            if last:
                store_toks.append(K.dma(SP, "st_h", ho[:, s, :], hstate[:], reads=hstB))
            for pc in range(4):
                proj.cur = load_piece(PIECE["w_out"] + pc)
                for oo in range(2):
                    dc = pc * 2 + oo
                    pb = proj(T_, pc, oo, yin, yinB)
                    resid_add(T_, pb, 0, dc, s)

        def conf_stage(T_, s, first, last):
            fence_bufs([ACT, DVE, POOL], [WsbB])
            if first:
                if s < NPS:
                    for j in range(KC):
                        K.op(POOL, "memset", dict(ap=xpc[:, j, 0:30], constant=0.0), writes=[xpcB[j]])
                else:
                    K.dma(SP, "ld_st", xpc[:, :, 0:30], st_dw[:, :, s - NPS, :], writes=xpcB)
            stats(T_)
            modnorm(T_, G1, 1, 0, s)
            for pc in (4, 5, 6, 7, 0, 1, 2, 3):
                proj.cur = load_piece(PIECE["pw1"] + pc)
                for oo in range(2):
                    oc = pc * 2 + oo
                    pb = proj(T_, pc, oo, hT, hTB)
                    if oc >= 8:
                        j = oc - 8
                        K.op(ACT, "activation", dict(out=sgm[:, j, :T_], in_=PS[pb][:, :T_], func=AF.Sigmoid, bias=V("bpw1", oc)),
                             reads=[PSB[pb], vecB], writes=[rrB[j]])
                    else:
                        j = oc
                        K.op(DVE, "scalar_tensor_tensor", dict(out=xpc[:, j, 30:30 + T_], in0=PS[pb][:, :T_], scalar=V("bpw1", j), in1=sgm[:, j, :T_],
                                                               op0=ALU.add, op1=ALU.mult), reads=[PSB[pb], vecB, rrB[j]], writes=[xpcB[j]])
            for j in range(KC):
                K.op(DVE, "tensor_scalar", dict(out=dd[:, j, :T_], in0=xpc[:, j, 0:T_], scalar1=V("dw0", j), scalar2=V("dwb", j), op0=ALU.mult, op1=ALU.add),
                     reads=[xpcB[j], vecB], writes=[xcB[j]])
            for k in range(1, 31):
                for j in range(KC):
                    K.op(DVE, "scalar_tensor_tensor", dict(out=dd[:, j, :T_], in0=xpc[:, j, k:k + T_], scalar=V("dw%d" % k, j), in1=dd[:, j, :T_],
                                                           op0=ALU.mult, op1=ALU.add), reads=[xpcB[j], vecB, xcB[j]], writes=[xcB[j]])
            if last:
                store_toks.append(K.dma(SP, "st_d", dwo[:, :, s, :], xpc[:, :, T_:T_ + 30], reads=xpcB))
            for j in range(KC):
                K.op(POOL, "tensor_copy", dict(out=xpc[:, j, 0:30], in_=xpc[:, j, T_:T_ + 30]), reads=[xpcB[j]], writes=[xpcB[j]])
            for j in range(KC):
                K.op(ACT, "activation", dict(out=sq[:, j, :T_], in_=dd[:, j, :T_], func=AF.Square), reads=[xcB[j]], writes=[sqB[j]])
            pa = nbank()
            K.multi(PE, [("matmul", dict(out=PS[pa][:, :T_], lhsT=ones, rhs=dd[:, j, :T_], start=(j == 0), stop=(j == KC - 1))) for j in range(KC)],
                    reads=xcB + [cstB], writes=[PSB[pa]])
            pb2 = nbank()
            K.multi(PE, [("matmul", dict(out=PS[pb2][:, :T_], lhsT=ones, rhs=sq[:, j, :T_], start=(j == 0), stop=(j == KC - 1))) for j in range(KC)],
                    reads=sqB + [cstB], writes=[PSB[pb2]])
            K.op(ACT, "activation", dict(out=mu[:, :T_], in_=PS[pa][:, :T_], func=AF.Identity, scale=1.0 / D), reads=[PSB[pa]], writes=[muB])
            K.op(DVE, "tensor_tensor", dict(out=musq[:, :T_], in0=mu[:, :T_], in1=mu[:, :T_], op=ALU.mult), reads=[muB], writes=[musqB])
            K.op(DVE, "scalar_tensor_tensor", dict(out=var[:, :T_], in0=PS[pb2][:, :T_], scalar=1.0 / D, in1=musq[:, :T_], op0=ALU.mult, op1=ALU.subtract),
                 reads=[PSB[pb2], musqB], writes=[varB])
            K.op(ACT, "activation", dict(out=sd2[:, :T_], in_=var[:, :T_], func=AF.Sqrt, bias=EPSA), reads=[varB], writes=[sd2B])
            K.op(DVE, "reciprocal", dict(out=rs2[:, :T_], in_=sd2[:, :T_]), reads=[sd2B], writes=[rs2B])
            for j in range(KC):
                i = nrot("t1", 2)
                K.op(POOL, "tensor_tensor", dict(out=t1[i][:, :T_], in0=dd[:, j, :T_], in1=mu[:, :T_], op=ALU.subtract), reads=[xcB[j], muB], writes=[t1B[i]])
                K.op(DVE, "tensor_tensor", dict(out=t1[i][:, :T_], in0=t1[i][:, :T_], in1=rs2[:, :T_], op=ALU.mult), reads=[t1B[i], rs2B], writes=[t1B[i]])
                K.op(ACT, "activation", dict(out=dnb[:, j, :T_], in_=t1[i][:, :T_], func=AF.Silu, bias=V("lnb", j), scale=V("lng", j)),
                     reads=[t1B[i], vecB], writes=[dnbB[j]])
            for pc in range(4):
                proj.cur = load_piece(PIECE["pw2"] + pc)
                for oo in range(2):
                    dc = pc * 2 + oo
                    pb = proj(T_, pc, oo, dnb, dnbB)
                    resid_add(T_, pb, 1, dc, s)

        def peer_stage(T_, s, l):
            stats(T_)
            modnorm(T_, G2, l, 24, s)
            for pc in range(8):
                proj.cur = load_piece(PIECE["wq%d" % l] + pc)
                for oo in range(2):
                    qc = pc * 2 + oo
                    pb = proj(T_, pc, oo, hT, hTB)
                    if qc % 2 == 0:
                        K.op(ACT, "activation", dict(out=qT[:, qc, :T_], in_=PS[pb][:, :T_], func=AF.Identity), reads=[PSB[pb]], writes=[qTB[qc]])
                    else:
                        K.op(DVE, "tensor_copy", dict(out=qT[:, qc, :T_], in_=PS[pb][:, :T_]), reads=[PSB[pb]], writes=[qTB[qc]])
            ngr = (T_ + 127) // 128
            for g in range(ngr):
                g0 = g * 128
                gt = min(128, T_ - g0)
                for i4 in range(4):
                    pb = nbank()
                    insts = []
                    for i in range(4):
                        qc = i4 * 4 + i
                        insts.append(("matmul", dict(out=PS[pb][:gt, i * 128:(i + 1) * 128], lhsT=qT[:, qc, g0:g0 + gt], rhs=kTb[:, l * 2 + (qc % 2), :],
                                                     start=True, stop=True)))
                    K.multi(PE, insts, reads=[qTB[i4 * 4 + i] for i in range(4)] + [kTbB], writes=[PSB[pb]])
                    K.op(ACT, "activation", dict(out=s_sb[:gt, i4 * 4:(i4 + 1) * 4, :], in_=PS[pb][:gt, :].rearrange("p (a b) -> p a b", b=128), func=AF.Identity),
                         reads=[PSB[pb]], writes=[ssbB[i4]])
                for qc in range(16):
                    K.op(DVE, "max", dict(out=v1[:gt, qc, 0:8], in_=s_sb[:gt, qc, :]), reads=[ssbB[qc // 4]], writes=[v1B[qc]])
                for qc in range(16):
                    K.op(DVE, "max_index", dict(out=ix[:gt, qc, 0:8], in_max=v1[:gt, qc, 0:8], in_values=s_sb[:gt, qc, :]), reads=[ssbB[qc // 4], v1B[qc]], writes=[ixB[qc]])
                for qc in range(16):
                    K.op(DVE, "match_replace", dict(out=s2[:gt, qc, :], in_to_replace=v1[:gt, qc, 0:8], in_values=s_sb[:gt, qc, :], imm_value=NEG),
                         reads=[ssbB[qc // 4], v1B[qc]], writes=[s2B[qc // 4]])
                for qc in range(16):
                    K.op(DVE, "max", dict(out=v1[:gt, qc, 8:16], in_=s2[:gt, qc, :]), reads=[s2B[qc // 4]], writes=[v1B[qc]])
                for qc in range(16):
                    K.op(DVE, "max_index", dict(out=ix[:gt, qc, 8:16], in_max=v1[:gt, qc, 8:16], in_values=s2[:gt, qc, :]), reads=[s2B[qc // 4], v1B[qc]], writes=[ixB[qc]])
                K.op(DVE, "tensor_copy", dict(out=ixf[:gt], in_=ix[:gt]), reads=ixB, writes=[ixfB])
                K.op(DVE, "tensor_tensor", dict(out=cand[:gt].rearrange("p h (a b) -> p h a b", b=16),
                                                in0=cap(v1, 0, [[32, 8], [1, 16], [0, 16]], gt), in1=cap(v1, 16, [[32, 8], [0, 16], [1, 16]], gt), op=ALU.add),
                     reads=v1B, writes=candB)
                for h in range(8):
                    K.op(DVE, "max", dict(out=cv[:gt, h, 0:8], in_=cand[:gt, h, :]), reads=[candB[h]], writes=[cvB[h]])
                for h in range(8):
                    K.op(DVE, "max_index", dict(out=ci[:gt, h, 0:8], in_max=cv[:gt, h, 0:8], in_values=cand[:gt, h, :]), reads=[candB[h], cvB[h]], writes=[ciB[h]])
                for h in range(8):
                    K.op(DVE, "match_replace", dict(out=cand2[:gt, h, :], in_to_replace=cv[:gt, h, 0:8], in_values=cand[:gt, h, :], imm_value=NEG),
                         reads=[candB[h], cvB[h]], writes=[cand2B[h]])
                for h in range(8):
                    K.op(DVE, "max", dict(out=cv[:gt, h, 8:16], in_=cand2[:gt, h, :]), reads=[cand2B[h]], writes=[cvB[h]])
                for h in range(8):
                    K.op(DVE, "max_index", dict(out=ci[:gt, h, 8:16], in_max=cv[:gt, h, 8:16], in_values=cand2[:gt, h, :]), reads=[cand2B[h], cvB[h]], writes=[ciB[h]])
                K.op(DVE, "tensor_tensor", dict(out=ge[:gt], in0=cv[:gt], in1=cap(cv, 0, [[16, 8], [0, 16]], gt), op=ALU.subtract), reads=cvB, writes=[geB])
                K.op(ACT, "activation", dict(out=ge[:gt], in_=ge[:gt], func=AF.Exp), reads=[geB], writes=[geB])
                K.op(DVE, "tensor_reduce", dict(out=gs[:gt], in_=ge[:gt], axis=AX.X, op=ALU.add), reads=[geB], writes=[gsB])
                K.op(DVE, "reciprocal", dict(out=gs2[:gt], in_=gs[:gt]), reads=[gsB], writes=[gs2B])
                K.op(DVE, "tensor_tensor", dict(out=idxg[:gt, 2, :].rearrange("p (h k) -> p h k", k=16), in0=ge[:gt], in1=cap(gs2, 0, [[1, 8], [0, 16]], gt), op=ALU.mult),
                     reads=[geB, gs2B], writes=[idxgB])
                K.op(DVE, "tensor_scalar", dict(out=cab[:gt, 0, :], in0=ci[:gt].rearrange("p h k -> p (h k)"), scalar1=4, scalar2=None, op0=ALU.logical_shift_right), reads=ciB, writes=[cabB])
                K.op(DVE, "tensor_scalar", dict(out=cab[:gt, 1, :], in0=ci[:gt].rearrange("p h k -> p (h k)"), scalar1=15, scalar2=None, op0=ALU.bitwise_and), reads=ciB, writes=[cabB])
                K.op(DVE, "tensor_copy", dict(out=cabf[:gt], in_=cab[:gt]), reads=[cabB], writes=[cabfB])
                for hf in range(2):
                    K.op(DVE, "tensor_tensor", dict(out=eqt[:gt], in0=cap(cabf, hf * 128, [[16, 8], [1, 16], [0, 16]], gt),
                                                    in1=cap(cst, 2 * 128, [[0, 8], [0, 16], [1, 16]], gt), op=ALU.is_equal),
                         reads=[cabfB, cstB], writes=ssbB)
                    K.op(DVE, "tensor_tensor", dict(out=prod[:gt], in0=eqt[:gt], in1=cap(ixf, hf * 16, [[32, 8], [0, 16], [1, 16]], gt), op=ALU.mult),
                         reads=ssbB + [ixfB], writes=s2B)
                    K.op(DVE, "tensor_reduce", dict(out=idxg[:gt, hf, :], in_=prod[:gt].rearrange("p h a b -> p (h a) b"), axis=AX.X, op=ALU.add),
                         reads=s2B, writes=[idxgB])
                pb = nbank()
                insts = [("transpose", dict(out=PS[pb][:, i * 128:i * 128 + gt], in_=idxg[:gt, i, :], identity=cst[:gt, 0, :gt])) for i in range(3)]
                K.multi(PE, insts, reads=[idxgB, cstB], writes=[PSB[pb]])
                K.op(ACT, "activation", dict(out=idxT[:, :, g0:g0 + gt], in_=PS[pb][:, 0:384].rearrange("p (a b) -> p a b", b=128)[:, :, :gt], func=AF.Identity),
                     reads=[PSB[pb]], writes=[idxTB])
                if g == 0:
                    fence_bufs([ACT, DVE], arenaB + [yB])
                for t0 in range(g0, g0 + gt, SG):
                    st = nrot("oh", 2)
                    eqA, Aoh, Boh = oh[st]
                    eqAB, AohB, BohB = ohB[st]
                    io = cap(iota_b, 0, [[0, SG], [1, 128]])
                    K.op(DVE, "tensor_tensor", dict(out=eqA[:], in0=io, in1=cap(idxT, 0 * T + t0, [[1, SG], [0, 128]]), op=ALU.is_equal),
                         reads=[iotabB, idxTB], writes=[eqAB])
                    K.op(POOL, "tensor_tensor", dict(out=Aoh[:], in0=eqA[:], in1=cap(idxT, 2 * T + t0, [[1, SG], [0, 128]]), op=ALU.mult),
                         reads=[eqAB, idxTB], writes=[AohB])
                    K.op(DVE, "tensor_tensor", dict(out=Boh[:], in0=io, in1=cap(idxT, 1 * T + t0, [[1, SG], [0, 128]]), op=ALU.is_equal),
                         reads=[iotabB, idxTB], writes=[BohB])
                    for q4 in range(SG // 4):
                        pb = nbank()
                        insts = [("matmul", dict(out=PS[pb][:, i * 128:(i + 1) * 128], lhsT=Aoh[:, q4 * 4 + i, :], rhs=Boh[:, q4 * 4 + i, :], start=True, stop=True))
                                 for i in range(4)]
                        K.multi(PE, insts, reads=[AohB, BohB], writes=[PSB[pb]])
                        tt = t0 + q4 * 4
                        dst = cap(Wsb, tt, [[1, 4], [T, 128]])
                        src = PS[pb][:, :].rearrange("p (a b) -> p a b", b=128)
                        if nrot("cp", 2) == 0:
                            K.op(ACT, "activation", dict(out=dst, in_=src, func=AF.Identity), reads=[PSB[pb]], writes=[WsbB])
                        else:
                            K.op(DVE, "tensor_copy", dict(out=dst, in_=src), reads=[PSB[pb]], writes=[WsbB])
            nb = min(8, 512 // T_)

            def OUT(dc):
                return PS[4 + dc // nb][:, (dc % nb) * T_:(dc % nb + 1) * T_]

            outB = PSB[4:8]

            def load_uv(cg):
                st = nrot("uv", 2)
                off = (l * 128 + cg * G_UV) * 128 * 1024
                K.dma(SP, "ld_u%d" % st, ubuf[st][:], dap(u_b, off, [[1024, 128], [128 * 1024, G_UV], [1, 1024]]), reads=[ubB], writes=[ubufB[st]])
                K.dma(SP, "ld_v%d" % st, vbuf[st][:], dap(v_b, off, [[1024, 128], [128 * 1024, G_UV], [1, 1024]]), reads=[vbB], writes=[vbufB[st]])
                return st

            def zmm(st, gi):
                pb = nbank()
                insts = [("matmul", dict(out=PS[pb][:, :T_], lhsT=ubuf[st][:, gi, k * 128:(k + 1) * 128], rhs=hT[:, k, :T_], start=(k == 0), stop=(k == KC - 1)))
                         for k in range(KC)]
                K.multi(PE, insts, reads=[ubufB[st]] + hTB, writes=[PSB[pb]])
                return pb

            sts = {}
            sts[0] = load_uv(0)
            pend = zmm(sts[0], 0)
            for c in range(128):
                cg, gi = divmod(c, G_UV)
                pb = pend
                i = nrot("gz", 3)
                K.op(ACT, "activation", dict(out=gz[i][:, :T_], in_=PS[pb][:, :T_], func=AF.Gelu_apprx_tanh), reads=[PSB[pb]], writes=[gzB[i]])
                K.op(DVE, "tensor_tensor", dict(out=wz[i][:, :T_], in0=gz[i][:, :T_], in1=Wsb[:, c, :T_], op=ALU.mult), reads=[gzB[i], WsbB], writes=[wzB[i]])
                if c + 1 < 128:
                    cg2, gi2 = divmod(c + 1, G_UV)
                    if gi2 == 0:
                        sts[cg2] = load_uv(cg2)
                    pend = zmm(sts[cg2], gi2)
                insts = [("matmul", dict(out=OUT(dc), lhsT=vbuf[sts[cg]][:, gi, dc * 128:(dc + 1) * 128], rhs=wz[i][:, :T_],
                                         start=(c == 0 and dc % nb == 0), stop=(c == 127))) for dc in range(KC)]
                K.multi(PE, insts, reads=[vbufB[sts[cg]], wzB[i]], writes=outB)
            for dc in range(KC):
                K.op(DVE, "scalar_tensor_tensor", dict(out=x[:, dc, :T_], in0=OUT(dc), scalar=MOD(l, 40 + dc, s), in1=x[:, dc, :T_], op0=ALU.mult, op1=ALU.add),
                     reads=outB + [modvB, xB[dc]], writes=[xB[dc]])

        tiles = []
        for s in range(NPS if not ONLY_SAMPLE else min(1, DBG_TILES)):
            nt = SEQ // T
            for j in range(nt if not ONLY_SAMPLE else DBG_TILES):
                tiles.append((s, s * SEQ + j * T, T, j == 0, j == nt - 1))
        for s in range(NSS):
            tiles.append((NPS + s, NPS * SEQ + s * DSEQ, DSEQ, True, True))
        for (s, tok0, T_, first, last) in tiles:
            K.dma(SP, "ld_x", x[:, :, :T_], xin[:, :, tok0:tok0 + T_], writes=xB)
            lru_stage(T_, s, first, last)
            peer_stage(T_, s, 0)
            conf_stage(T_, s, first, last)
            peer_stage(T_, s, 1)
            stats(T_)
            for k in range(KC):
                K.op(DVE, "scalar_tensor_tensor", dict(out=ybuf[:, k, :T_], in0=x[:, k, :T_], scalar=V("nfin", k), in1=rstd[:, :T_], op0=ALU.mult, op1=ALU.mult),
                     reads=[xB[k], vecB, rstdB], writes=[yB] + iiB)
            store_toks.append(K.dma(SP, "st_y", yout[:, :, tok0:tok0 + T_], ybuf[:, :, :T_], reads=[yB]))
        last_tok = {}
        for tk in store_toks:
            last_tok[id(tk.sem)] = tk
        K.fence(SP, list(last_tok.values()))

        with nc.Block() as block:
            @block.tensor
            def _(e):
                K.replay(PE, e)

            @block.scalar
            def _(e):
                K.replay(ACT, e)

            @block.vector
            def _(e):
                K.replay(DVE, e)

            @block.gpsimd
            def _(e):
                K.replay(POOL, e)

            @block.sync
            def _(e):
                K.replay(SP, e)
    return nc


_PROG = {}


def _fm(a):
    a = np.asarray(a, dtype=np.float32)
    lead = a.shape[:-1]
    a = a.reshape(lead + (KC, 128))
    nd = a.ndim
    perm = (nd - 1, nd - 2) + tuple(range(nd - 2))
    return np.ascontiguousarray(a.transpose(perm))


def kernel(**inp):
    f = lambda k: np.asarray(inp[k], dtype=np.float32)
    vec = np.zeros((128, NV), np.float32)

    def put(name, v):
        v = np.asarray(v, np.float32).reshape(-1, 128)
        vec[:, VOFF[name]:VOFF[name] + v.shape[0]] = v.T

    put("nm0", f("norm_mix")[0]); put("nm1", f("norm_mix")[1])
    put("nf0", f("norm_ffn")[0]); put("nf1", f("norm_ffn")[1]); put("nfin", f("norm_final"))
    put("b_in", f("lru_b_in")[0])
    for k in range(4):
        put("cw%d" % k, f("lru_conv_w")[0, k])
    put("cb", f("lru_conv_b")[0]); put("gab", f("lru_gate_a_b")[0]); put("gxb", f("lru_gate_x_b")[0])
    put("lam", f("lru_lambda")[0]); put("bout", f("lru_b_out")[0]); put("bpw1", f("cf_b_pw1")[0])
    for k in range(31):
        put("dw%d" % k, f("cf_dw_w")[0, k])
    put("dwb", f("cf_dw_b")[0]); put("lng", f("cf_ln_g")[0]); put("lnb", f("cf_ln_b")[0]); put("bpw2", f("cf_b_pw2")[0])
    adab = np.ascontiguousarray(f("ada_b").reshape(2, 48, 128).transpose(2, 0, 1))
    adaw = np.ascontiguousarray(f("ada_w").reshape(2, KC, 128, 12, 512).transpose(0, 3, 2, 1, 4)).reshape(2, 12, 128, KC * 512)
    gw = np.stack([f("lru_gate_a_w")[0], f("lru_gate_x_w")[0]], 0)
    gw = np.ascontiguousarray(gw.transpose(2, 0, 1, 3)).reshape(128, 2 * 8 * 128)
    kk = np.stack([f("peer_k1")[0], f("peer_k2")[0], f("peer_k1")[1], f("peer_k2")[1]], 0)
    kT = np.ascontiguousarray(kk.transpose(2, 0, 1)).reshape(128, 4 * 128)
    wcat = np.concatenate([f("lru_w_in")[0], f("lru_w_out")[0], f("cf_w_pw1")[0], f("cf_w_pw2")[0], f("peer_w_q")[0], f("peer_w_q")[1]], axis=1)
    wd = np.ascontiguousarray(wcat.reshape(KC, 128, NPIECE, 256).transpose(2, 1, 0, 3)).reshape(-1, 2048)
    u_arr = np.ascontiguousarray(f("peer_u").reshape(2, 128, 128, KC, 128).transpose(0, 2, 4, 3, 1)).reshape(-1, 2048)
    v_arr = np.ascontiguousarray(f("peer_v").reshape(2, 128, 128, D).transpose(0, 2, 1, 3)).reshape(-1, 2048)
    consts = np.zeros((128, 4, 128), np.float32)
    consts[:, 3, 1] = 1.0
    consts[:, 3, 2] = EPS
    consts[:, 0, :] = np.eye(128, dtype=np.float32)
    consts[:, 1, :] = 1.0
    consts[:, 2, :] = np.arange(128, dtype=np.float32)[None, :]
    xp_, xs_ = f("x_prompt"), f("x_sample")
    cp_, cs_ = f("c_prompt"), f("c_sample")
    sh_, sc_, sd_ = f("state_lru_h"), f("state_lru_conv"), f("state_dwconv")
    in_maps = []
    for c in range(NCORES):
        xt = np.concatenate([xp_[NPS * c:NPS * c + NPS].reshape(-1, D), xs_[NSS * c:NSS * c + NSS].reshape(-1, D)], 0)
        cc = np.concatenate([cp_[NPS * c:NPS * c + NPS], cs_[NSS * c:NSS * c + NSS]], 0)
        in_maps.append(dict(
            xin=_fm(xt), cT=_fm(cc), st_h=np.ascontiguousarray(_fm(sh_[0, NSS * c:NSS * c + NSS]).transpose(0, 2, 1)),
            st_conv=_fm(sc_[0, NSS * c:NSS * c + NSS]), st_dw=_fm(sd_[0, NSS * c:NSS * c + NSS]),
            consts=consts, vec=vec, adab=adab, adaw=adaw, gw=gw, kT=kT, wd=wd, u_arr=u_arr, v_arr=v_arr))
    if "p" not in _PROG:
        _PROG["p"] = build_program()
    res = run_bass_kernel_spmd(_PROG["p"], in_maps, core_ids=list(range(NCORES)))
    B, DB = NPS * NCORES, NSS * NCORES
    y_p = np.zeros((B, SEQ, D), np.float32); y_s = np.zeros((DB, DSEQ, D), np.float32)
    h_p = np.zeros((1, B, D), np.float32); h_s = np.zeros((1, DB, D), np.float32)
    cv_p = np.zeros((1, B, 3, D), np.float32); cv_s = np.zeros((1, DB, 3, D), np.float32)
    dw_p = np.zeros((1, B, 30, D), np.float32); dw_s = np.zeros((1, DB, 30, D), np.float32)

    def unfm(a):
        nd = a.ndim
        perm = tuple(range(2, nd)) + (1, 0)
        a = a.transpose(perm)
        return a.reshape(a.shape[:-2] + (D,))

    for c in range(NCORES):
        r = res.results[c]
        y = unfm(np.asarray(r["yout"]))
        y_p[NPS * c:NPS * c + NPS] = y[:NPS * SEQ].reshape(NPS, SEQ, D)
        y_s[NSS * c:NSS * c + NSS] = y[NPS * SEQ:].reshape(NSS, DSEQ, D)
        h = unfm(np.ascontiguousarray(np.asarray(r["ho"]).transpose(0, 2, 1)))
        h_p[0, NPS * c:NPS * c + NPS] = h[:NPS]; h_s[0, NSS * c:NSS * c + NSS] = h[NPS:]
        cvv = unfm(np.asarray(r["convo"]))
        cv_p[0, NPS * c:NPS * c + NPS] = cvv[:NPS]; cv_s[0, NSS * c:NSS * c + NSS] = cvv[NPS:]
        dww = unfm(np.asarray(r["dwo"]))
        dw_p[0, NPS * c:NPS * c + NPS] = dww[:NPS]; dw_s[0, NSS * c:NSS * c + NSS] = dww[NPS:]
    return (y_p, y_s, h_p, cv_p, dw_p, h_s, cv_s, dw_s)
```

```python
import numpy as np
from contextlib import ExitStack
import concourse.bass as bass
import concourse.mybir as mybir
from concourse.bass_utils import run_bass_kernel_spmd

F32 = mybir.dt.float32
BF16 = mybir.dt.bfloat16
U32 = mybir.dt.uint32
ALU = mybir.AluOpType
AF = mybir.ActivationFunctionType
AX = mybir.AxisListType

NCORES = 8
D = 1024
KC = 8
TMAX = 256
SEQ = 2048
NPS = 4
NSS = 2
DSEQ = 32
NTOK = NPS * SEQ + NSS * DSEQ
EPS = 1e-6
G_UV = 2
SG = 8
NEG = -1.0e30
import os
ONLY_SAMPLE = bool(int(os.environ.get("PEER_ONLY_SAMPLE", "0")))
DBG_TILES = int(os.environ.get("PEER_DBG_TILES", "0"))

VEC_SPEC = [("nm0", 8), ("nm1", 8), ("nf0", 8), ("nf1", 8), ("nfin", 8), ("b_in", 16),
            ("cw0", 8), ("cw1", 8), ("cw2", 8), ("cw3", 8), ("cb", 8), ("gab", 8), ("gxb", 8),
            ("lam", 8), ("bout", 8), ("bpw1", 16)] + [("dw%d" % k, 8) for k in range(31)] + \
           [("dwb", 8), ("lng", 8), ("lnb", 8), ("bpw2", 8)]
VOFF = {}
_o = 0
for _n, _w in VEC_SPEC:
    VOFF[_n] = _o
    _o += _w
NV = _o
PIECE = {"w_in": 0, "w_out": 8, "pw1": 12, "pw2": 20, "wq0": 24, "wq1": 32}
NPIECE = 40


class Tok:
    __slots__ = ("sem", "val")

    def __init__(self, sem, val):
        self.sem = sem
        self.val = val


class Buf:
    def __init__(self, name):
        self.name = name
        self.w = None
        self.r = {}


class Eng:
    def __init__(self, name):
        self.name = name
        self.sem = None
        self.cnt = 0
        self.seen = {}
        self.ops = []


class Sched:
    def __init__(self):
        self.pe = Eng("pe")
        self.act = Eng("act")
        self.dve = Eng("dve")
        self.pool = Eng("pool")
        self.sp = Eng("sp")
        self.engs = [self.pe, self.act, self.dve, self.pool, self.sp]
        self.dsem = {}
        self.semlist = []

    def _waits(self, E, reads, writes):
        waits = {}

        def need(tok, same_ok):
            if tok is None:
                return
            if tok.sem is E.sem and (same_ok or E is self.pe):
                return
            k = id(tok.sem)
            if k not in waits or waits[k][1] < tok.val:
                waits[k] = (tok.sem, tok.val)

        for b in reads:
            need(b.w, False)
        for b in writes:
            need(b.w, True)
            for sem, val in b.r.values():
                need(Tok(sem, val), True)
        wl = []
        for k, (sem, val) in waits.items():
            if E.seen.get(k, 0) < val:
                E.seen[k] = val
                wl.append((sem, val))
        return wl

    def _commit(self, tok, reads, writes):
        k = id(tok.sem)
        for b in reads:
            if k not in b.r or b.r[k][1] < tok.val:
                b.r[k] = (tok.sem, tok.val)
        for b in writes:
            b.w = tok
            b.r = {}

    def op(self, E, name, kw, reads=(), writes=()):
        if name == "activation" and "bias" not in kw:
            kw["bias"] = self.zero[:kw["in_"].shape[0]]
        return self.multi(E, [(name, kw)], reads, writes)

    def multi(self, E, insts, reads=(), writes=()):
        wl = self._waits(E, reads, writes)
        E.cnt += 1
        tok = Tok(E.sem, E.cnt)
        E.ops.append((wl, insts, E.sem, 1))
        self._commit(tok, reads, writes)
        return tok

    def dma(self, Q, key, out, in_, reads=(), writes=()):
        wl = self._waits(Q, reads, writes)
        ent = self.dsem[key]
        ent[1] += 16
        tok = Tok(ent[0], ent[1])
        Q.ops.append((wl, [("dma_start", dict(out=out, in_=in_))], ent[0], 16))
        self._commit(tok, reads, writes)
        return tok

    def fence(self, E, toks):
        wl = []
        for tok in toks:
            k = id(tok.sem)
            if E.seen.get(k, 0) < tok.val:
                E.seen[k] = tok.val
                wl.append((tok.sem, tok.val))
        if wl:
            E.ops.append((wl, [], None, 0))

    def replay(self, E, e):
        for wl, insts, sem, inc in E.ops:
            for s, v in wl:
                e.wait_ge(s, v)
            last = None
            for name, kw in insts:
                last = getattr(e, name)(**kw)
            if last is not None and sem is not None:
                last.then_inc(sem, inc)


def cap(full, off, dims, nparts=128):
    pstep = full.ap[0][0]
    return bass.AP(tensor=full.tensor, offset=off, ap=[[pstep, nparts]] + [list(d) for d in dims])


def dap(t, off, dims):
    return bass.AP(tensor=t.tensor, offset=off, ap=[list(d) for d in dims])


def build_program():
    nc = bass.Bass("TRN2", target_bir_lowering=False)
    K = Sched()
    PE, ACT, DVE, POOL, SP = K.pe, K.act, K.dve, K.pool, K.sp

    def din(name, shape, dt=F32):
        return nc.dram_tensor(name, list(shape), dt, kind="ExternalInput").ap()

    def dout(name, shape, dt=F32):
        return nc.dram_tensor(name, list(shape), dt, kind="ExternalOutput").ap()

    xin = din("xin", [128, KC, NTOK])
    cT = din("cT", [128, KC, 6])
    st_h = din("st_h", [128, NSS, KC])
    st_conv = din("st_conv", [128, KC, NSS, 3])
    st_dw = din("st_dw", [128, KC, NSS, 30])
    consts = din("consts", [128, 4, 128])
    vec_d = din("vec", [128, NV])
    adab_d = din("adab", [128, 2, 48])
    adaw_d = din("adaw", [2, 12, 128, KC * 512])
    gw_d = din("gw", [128, 2 * 8 * 128])
    kT_d = din("kT", [128, 4 * 128])
    wd_d = din("wd", [NPIECE * 128 * KC * 256 // 2048, 2048])
    u_d = din("u_arr", [2 * 128 * 128 * 1024 // 2048, 2048])
    v_d = din("v_arr", [2 * 128 * 128 * 1024 // 2048, 2048])
    wd_b = nc.dram_tensor("wd_b", [NPIECE * 128 * KC * 256 // 2048, 2048], BF16, kind="Internal").ap()
    u_b = nc.dram_tensor("u_b", [2 * 128 * 128 * 1024 // 2048, 2048], BF16, kind="Internal").ap()
    v_b = nc.dram_tensor("v_b", [2 * 128 * 128 * 1024 // 2048, 2048], BF16, kind="Internal").ap()
    yout = dout("yout", [128, KC, NTOK])
    ho = dout("ho", [128, 6, KC])
    convo = dout("convo", [128, KC, 6, 3])
    dwo = dout("dwo", [128, KC, 6, 30])

    es = ExitStack()
    with es:
        sb_off = [20480]

        def SBt(name, shape, dt, at=None):
            esz = 2 if dt == BF16 else 4
            n = 1
            for s in shape[1:]:
                n *= s
            nbytes = (n * esz + 63) // 64 * 64
            if at is None:
                at = sb_off[0]
                sb_off[0] += nbytes
            t = nc.alloc_sbuf_tensor_at(name, list(shape), dt, offset=at)
            return t.ap()

        T = TMAX
        cst = SBt("cst", [128, 4, 128], F32)
        ZERO = cst[:, 3, 0:1]
        ONE = cst[:, 3, 1:2]
        EPSA = cst[:, 3, 2:3]
        K.zero = ZERO
        ident = cst[:, 0, :]
        ones = cst[:, 1, :]
        iota_f = cst[:, 2, :]
        iota_b = SBt("iota_b", [128, 128], BF16)
        vec = SBt("vec", [128, NV], F32)
        adab = SBt("adab", [128, 2, 48], F32)
        modv = SBt("modv", [128, 2, 48, 6], F32)
        G1 = SBt("G1", [128, 2, 8, 6], F32)
        GB1 = SBt("GB1", [128, 2, 8, 6], F32)
        G2 = SBt("G2", [128, 2, 8, 6], F32)
        cA = SBt("cA", [128, 8], F32)
        c2A = SBt("c2A", [128, 8], F32)
        ptmp = SBt("ptmp", [128, 4, 8], F32)
        cTs = SBt("cTs", [128, KC, 6], F32)
        csil = SBt("csil", [128, KC, 6], F32)
        gwb = SBt("gwb", [128, 2, 8, 128], BF16)
        kTb = SBt("kTb", [128, 4, 128], BF16)
        hstate = SBt("hstate", [128, 8], F32)
        xp = SBt("xp", [128, KC, 3 + T], F32)
        xpc = SBt("xpc", [128, KC, 30 + T], F32)
        x = SBt("x", [128, KC, T], F32)
        hT = SBt("hT", [128, KC, T], BF16)
        sq = SBt("sq", [128, KC, T], F32)
        sdt = SBt("sdt", [128, T], F32)
        rstd = SBt("rstd", [128, T], F32)
        tmpk = [SBt("tmpk%d" % i, [128, T], F32) for i in range(2)]
        wslot = [SBt("wslot%d" % i, [128, KC, 256], BF16) for i in range(3)]
        ubuf = [SBt("ubuf%d" % i, [128, G_UV, 1024], BF16) for i in range(2)]
        vbuf = [SBt("vbuf%d" % i, [128, G_UV, 1024], BF16) for i in range(2)]
        arena_off = sb_off[0]
        gate = SBt("gate", [128, KC, T], BF16)
        xc = SBt("xc", [128, KC, T], F32)
        xcb = SBt("xcb", [128, KC, T], BF16)
        rr = SBt("rr", [128, KC, T], F32)
        ii_off = sb_off[0]
        ii = SBt("ii", [128, KC, T], F32)
        ybuf = SBt("ybuf", [128, KC, T], F32, at=ii_off)
        aa_off = sb_off[0]
        aa = SBt("aa", [128, KC, T], F32)
        mm_ = SBt("mm", [128, KC, T], F32)
        yin_off = sb_off[0]
        yin = SBt("yin", [128, KC, T], BF16)
        rtmp = [SBt("rtmp%d" % i, [128, T], F32) for i in range(2)]
        sgm = rr
        dd = xc
        tsz = T * 4
        mu = SBt("mu", [128, T], F32, at=aa_off)
        musq = SBt("musq", [128, T], F32, at=aa_off + tsz)
        var = SBt("var", [128, T], F32, at=aa_off + 2 * tsz)
        sd2 = SBt("sd2", [128, T], F32, at=aa_off + 3 * tsz)
        rs2 = SBt("rs2", [128, T], F32, at=aa_off + 4 * tsz)
        t1 = [SBt("t1_%d" % i, [128, T], F32, at=aa_off + (5 + i) * tsz) for i in range(2)]
        dnb = SBt("dnb", [128, KC, T], BF16, at=yin_off)
        arena_end = sb_off[0]
        sb_off[0] = max(arena_end, arena_off + 128 * T * 2)
        qT = SBt("qT", [128, 16, T], BF16)
        s_off = sb_off[0]
        s_sb = SBt("s_sb", [128, 16, 128], F32)
        s2_off = sb_off[0]
        s2 = SBt("s2", [128, 16, 128], F32)
        eqt = SBt("eqt", [128, 8, 16, 16], F32, at=s_off)
        prod = SBt("prod", [128, 8, 16, 16], F32, at=s2_off)
        v1 = SBt("v1", [128, 16, 16], F32)
        ix = SBt("ix", [128, 16, 16], U32)
        ixf = SBt("ixf", [128, 16, 16], F32)
        cand = SBt("cand", [128, 8, 256], F32, at=s_off)
        cand2 = SBt("cand2", [128, 8, 256], F32, at=s2_off)
        cv = SBt("cv", [128, 8, 16], F32)
        ci = SBt("ci", [128, 8, 16], U32)
        cab = SBt("cab", [128, 2, 128], U32)
        cabf = SBt("cabf", [128, 2, 128], F32)
        ge = SBt("ge", [128, 8, 16], F32)
        gs = SBt("gs", [128, 8], F32)
        gs2 = SBt("gs2", [128, 8], F32)
        idxg = SBt("idxg", [128, 3, 128], F32)
        idxT = SBt("idxT", [128, 3, T], BF16)
        oh = [[SBt("oh%d_%d" % (s, i), [128, SG, 128], BF16) for i in range(3)] for s in range(3)]
        wsb_off = arena_off
        Wsb = SBt("Wsb", [128, T, 128], BF16, at=arena_off)
        assert arena_end - arena_off <= 128 * T * 2
        print("SBUF end", sb_off[0], "arena", arena_off, arena_end - arena_off)
        gz = [SBt("gz%d" % i, [128, T], BF16) for i in range(3)]
        wz = [SBt("wz%d" % i, [128, T], BF16) for i in range(3)]
        assert sb_off[0] <= 224 * 1024 - 2048, sb_off[0]
        adas = [SBt("adas%d" % i, [128, KC, 512], F32, at=wsb_off + i * 16384) for i in range(2)]
        gws = SBt("gws", [128, 2 * 8 * 128], F32, at=s_off)
        kTs = SBt("kTs", [128, 4 * 128], F32, at=s2_off)

        PS = [es.enter_context(nc.psum_tensor("ps%d" % i, [128, 512], F32)) for i in range(8)]
        PSB = [Buf("ps%d" % i) for i in range(8)]
        for E in K.engs:
            E.sem = es.enter_context(nc.semaphore("sem_" + E.name))

        def dsem(key):
            K.dsem[key] = [es.enter_context(nc.semaphore("d_" + key)), 0]

        for key in ["ld_misc", "ld_x", "ld_w0", "ld_w1", "ld_w2", "ld_u0", "ld_u1", "ld_v0", "ld_v1",
                    "st_y", "st_c", "st_h", "st_d", "cast_w", "cast_u", "cast_v", "ld_a0", "ld_a1", "ld_st"]:
            dsem(key)

        rot = {"bank": 0, "w": 0, "uv": 0, "tmpk": 0, "rtmp": 0, "t1": 0, "gz": 0, "oh": 0, "cp": 0}

        def nbank():
            b = rot["bank"]
            rot["bank"] = (b + 1) % 4
            return b

        def nrot(key, n):
            v = rot[key]
            rot[key] = (v + 1) % n
            return v

        B = lambda n: Buf(n)
        cstB, vecB, adabB, modvB, GB_, cAB, csilB, gwbB, kTbB = B("cst"), B("vec"), B("adab"), B("modv"), B("G"), B("cA"), B("csil"), B("gwb"), B("kTb")
        iotabB, ptmpB, cTsB, gwsB, kTsB = B("iotab"), B("ptmp"), B("cTs"), B("gws"), B("kTs")
        hstB = [B("hst%d" % j) for j in range(8)]
        xpB = [B("xp%d" % j) for j in range(8)]
        xpcB = [B("xpc%d" % j) for j in range(8)]
        xB = [B("x%d" % j) for j in range(8)]
        hTB = [B("hT%d" % j) for j in range(8)]
        yB = B("ybuf")
        sqB = [B("sq%d" % j) for j in range(8)]
        sdtB, rstdB = B("sdt"), B("rstd")
        tmpkB = [B("tmpk0"), B("tmpk1")]
        wslotB = [B("ws%d" % i) for i in range(3)]
        ubufB = [B("ub0"), B("ub1")]
        vbufB = [B("vb0"), B("vb1")]
        gateB = [B("gate%d" % j) for j in range(8)]
        xcB = [B("xc%d" % j) for j in range(8)]
        xcbB = [B("xcb%d" % j) for j in range(8)]
        rrB = [B("rr%d" % j) for j in range(8)]
        iiB = [B("ii%d" % j) for j in range(8)]
        aaB = [B("aa%d" % j) for j in range(8)]
        mmB = [B("mm%d" % j) for j in range(8)]
        yinB = [B("yin%d" % j) for j in range(8)]
        rtmpB = [B("rtmp0"), B("rtmp1")]
        muB, musqB, varB, sd2B, rs2B = B("mu"), B("musq"), B("var"), B("sd2"), B("rs2")
        t1B = [B("t1_0"), B("t1_1")]
        arenaB = gateB + xcB + xcbB + rrB + iiB + aaB + mmB + yinB + rtmpB + [muB, musqB, varB, sd2B, rs2B] + t1B
        dnbB = yinB
        qTB = [B("qT%d" % j) for j in range(16)]
        ssbB = [B("ssb%d" % j) for j in range(4)]
        s2B = [B("s2_%d" % j) for j in range(4)]
        v1B = [B("v1_%d" % j) for j in range(16)]
        ixB = [B("ix_%d" % j) for j in range(16)]
        ixfB, candB, cvB, ciB, cabB, cabfB, geB, gsB, gs2B, idxgB = B("ixf"), [ssbB[h // 2] for h in range(8)], [B("cv%d" % h) for h in range(8)], [B("ci%d" % h) for h in range(8)], B("cab"), B("cabf"), B("ge"), B("gs"), B("gs2"), B("idxg")
        cand2B = [s2B[h // 2] for h in range(8)]
        idxTB = [B("idxT0"), B("idxT1")]
        ohB = [[B("oh%d_%d" % (s, i)) for i in range(3)] for s in range(3)]
        WsbB = B("Wsb")
        gzB = [B("gz%d" % i) for i in range(3)]
        wzB = [B("wz%d" % i) for i in range(3)]
        adasB = [B("adas0"), B("adas1")]
        wdbB, ubB, vbB = B("wd_b"), B("u_b"), B("v_b")
        store_toks = []

        def V(name, j=0):
            o = VOFF[name] + j
            return vec[:, o:o + 1]

        K.dma(SP, "ld_misc", cst[:], consts[:, :, :], writes=[cstB])
        K.dma(SP, "ld_misc", vec[:], vec_d[:, :], writes=[vecB])
        K.dma(SP, "ld_misc", adab[:], adab_d[:, :, :], writes=[adabB])
        K.dma(SP, "ld_misc", cTs[:], cT[:, :, :], writes=[cTsB])
        K.dma(SP, "ld_misc", gws[:], gw_d[:, :], writes=[gwsB])
        tk_misc = K.dma(SP, "ld_misc", kTs[:], kT_d[:, :], writes=[kTsB])
        for bb in (cstB, vecB, adabB, cTsB, gwsB, kTsB):
            bb.w = tk_misc
        R = 2048
        nrow = wd_d.shape[0]
        r0 = 0
        while r0 < nrow:
            n = min(R, nrow - r0)
            K.dma(POOL, "cast_w", wd_b[r0:r0 + n, :], wd_d[r0:r0 + n, :], writes=[wdbB])
            r0 += n
        for (src, dst, bb, ck) in ((u_d, u_b, ubB, "cast_u"), (v_d, v_b, vbB, "cast_v")):
            for r0 in range(0, src.shape[0], R):
                K.dma(POOL, ck, dst[r0:r0 + R, :], src[r0:r0 + R, :], writes=[bb])
        K.op(DVE, "tensor_copy", dict(out=iota_b[:], in_=iota_f), reads=[cstB], writes=[iotabB])
        K.op(DVE, "tensor_copy", dict(out=gwb[:].rearrange("p a h j -> p (a h j)"), in_=gws[:]), reads=[gwsB], writes=[gwbB])
        K.op(DVE, "tensor_copy", dict(out=kTb[:].rearrange("p a k -> p (a k)"), in_=kTs[:]), reads=[kTsB], writes=[kTbB])
        K.op(ACT, "activation", dict(out=csil[:], in_=cTs[:], func=AF.Silu), reads=[cTsB], writes=[csilB])
        lam = vec[:, VOFF["lam"]:VOFF["lam"] + 8]
        K.op(DVE, "tensor_scalar", dict(out=ptmp[:, 3, :], in0=lam, scalar1=-1.0, scalar2=None, op0=ALU.mult), reads=[vecB], writes=[ptmpB])
        K.op(DVE, "tensor_tensor", dict(out=ptmp[:, 0, :], in0=lam, in1=ptmp[:, 3, :], op=ALU.max), reads=[vecB, ptmpB], writes=[ptmpB])
        K.op(ACT, "activation", dict(out=ptmp[:, 1, :], in_=ptmp[:, 0, :], func=AF.Exp, scale=-1.0), reads=[ptmpB], writes=[ptmpB])
        K.op(ACT, "activation", dict(out=ptmp[:, 2, :], in_=ptmp[:, 1, :], func=AF.Ln, bias=ONE), reads=[ptmpB], writes=[ptmpB])
        K.op(DVE, "tensor_scalar", dict(out=ptmp[:, 3, :], in0=lam, scalar1=-1.0, scalar2=0.0, op0=ALU.mult, op1=ALU.max), reads=[vecB, ptmpB], writes=[ptmpB])
        K.op(DVE, "tensor_tensor", dict(out=ptmp[:, 3, :], in0=ptmp[:, 3, :], in1=ptmp[:, 2, :], op=ALU.add), reads=[ptmpB], writes=[ptmpB])
        K.op(DVE, "tensor_scalar", dict(out=cA[:], in0=ptmp[:, 3, :], scalar1=-8.0, scalar2=None, op0=ALU.mult), reads=[ptmpB], writes=[cAB])
        K.op(DVE, "tensor_scalar", dict(out=c2A[:], in0=ptmp[:, 3, :], scalar1=-16.0, scalar2=None, op0=ALU.mult), reads=[ptmpB], writes=[cAB])
        for l in range(2):
            pb = nbank()
            for pc in range(12):
                sl = pc % 2
                K.dma(SP, "ld_a%d" % sl, adas[sl][:].rearrange("p k o -> p (k o)"), adaw_d[l, pc, :, :], writes=[adasB[sl]])
                for o4 in range(4):
                    oc = pc * 4 + o4
                    insts = [("matmul", dict(out=PS[pb][:, oc * 6:(oc + 1) * 6], lhsT=adas[sl][:, k, o4 * 128:(o4 + 1) * 128],
                                             rhs=csil[:, k, :], start=(k == 0), stop=(k == KC - 1))) for k in range(KC)]
                    K.multi(PE, insts, reads=[adasB[sl], csilB], writes=[PSB[pb]])
            K.op(DVE, "tensor_tensor", dict(out=modv[:, l, :, :], in0=PS[pb][:, 0:288].rearrange("p (o s) -> p o s", s=6),
                                            in1=cap(adab, l * 48, [[1, 48], [0, 6]]), op=ALU.add),
                 reads=[PSB[pb], adabB], writes=[modvB])
            nmn = "nm%d" % l
            nfn = "nf%d" % l
            bon = "bout" if l == 0 else "bpw2"
            K.op(DVE, "scalar_tensor_tensor", dict(out=G1[:, l, :, :], in0=modv[:, l, 8:16, :], scalar=1.0,
                                                   in1=cap(vec, VOFF[nmn], [[1, 8], [0, 6]]), op0=ALU.add, op1=ALU.mult),
                 reads=[modvB, vecB], writes=[GB_])
            K.op(DVE, "scalar_tensor_tensor", dict(out=G2[:, l, :, :], in0=modv[:, l, 32:40, :], scalar=1.0,
                                                   in1=cap(vec, VOFF[nfn], [[1, 8], [0, 6]]), op0=ALU.add, op1=ALU.mult),
                 reads=[modvB, vecB], writes=[GB_])
            K.op(DVE, "tensor_tensor", dict(out=GB1[:, l, :, :], in0=modv[:, l, 16:24, :],
                                            in1=cap(vec, VOFF[bon], [[1, 8], [0, 6]]), op=ALU.mult),
                 reads=[modvB, vecB], writes=[GB_])

        def MOD(l, oc, s):
            return modv[:, l, oc, s:s + 1]

        def load_piece(pc):
            sl = nrot("w", 3)
            src = dap(wd_b, pc * 128 * KC * 256, [[KC * 256, 128], [1, KC * 256]])
            K.dma(SP, "ld_w%d" % sl, wslot[sl][:].rearrange("p k o -> p (k o)"), src, reads=[wdbB], writes=[wslotB[sl]])
            return sl

        def stats(T_):
            for k in range(KC):
                K.op(ACT, "activation", dict(out=sq[:, k, :T_], in_=x[:, k, :T_], func=AF.Square), reads=[xB[k]], writes=[sqB[k]])
            pb = nbank()
            insts = [("matmul", dict(out=PS[pb][:, :T_], lhsT=ones, rhs=sq[:, k, :T_], start=(k == 0), stop=(k == KC - 1))) for k in range(KC)]
            K.multi(PE, insts, reads=sqB + [cstB], writes=[PSB[pb]])
            K.op(ACT, "activation", dict(out=sdt[:, :T_], in_=PS[pb][:, :T_], func=AF.Sqrt, bias=EPSA, scale=1.0 / D), reads=[PSB[pb]], writes=[sdtB])
            K.op(DVE, "reciprocal", dict(out=rstd[:, :T_], in_=sdt[:, :T_]), reads=[sdtB], writes=[rstdB])

        def modnorm(T_, Gt, l, shoc, s):
            for k in range(KC):
                i = nrot("tmpk", 2)
                K.op(DVE, "scalar_tensor_tensor", dict(out=tmpk[i][:, :T_], in0=x[:, k, :T_], scalar=Gt[:, l, k, s:s + 1], in1=rstd[:, :T_],
                                                       op0=ALU.mult, op1=ALU.mult), reads=[xB[k], GB_, rstdB], writes=[tmpkB[i]])
                K.op(ACT, "activation", dict(out=hT[:, k, :T_], in_=tmpk[i][:, :T_], func=AF.Identity, bias=MOD(l, shoc + k, s)),
                     reads=[tmpkB[i], modvB], writes=[hTB[k]])

        def proj(T_, pc, oo, rhs_t, rhsB):
            sl = proj.cur
            pb = nbank()
            insts = [("matmul", dict(out=PS[pb][:, :T_], lhsT=wslot[sl][:, k, oo * 128:(oo + 1) * 128], rhs=rhs_t[:, k, :T_],
                                     start=(k == 0), stop=(k == KC - 1))) for k in range(KC)]
            K.multi(PE, insts, reads=[wslotB[sl]] + rhsB, writes=[PSB[pb]])
            return pb

        def resid_add(T_, pb, l, dc, s):
            i = nrot("rtmp", 2)
            K.op(ACT, "activation", dict(out=rtmp[i][:, :T_], in_=PS[pb][:, :T_], func=AF.Identity, bias=GB1[:, l, dc, s:s + 1], scale=MOD(l, 16 + dc, s)),
                 reads=[PSB[pb], GB_, modvB], writes=[rtmpB[i]])
            K.op(POOL, "tensor_tensor", dict(out=x[:, dc, :T_], in0=x[:, dc, :T_], in1=rtmp[i][:, :T_], op=ALU.add), reads=[rtmpB[i], xB[dc]], writes=[xB[dc]])

        def fence_bufs(engs, bufs):
            toks = []
            for b in bufs:
                if b.w is not None:
                    toks.append(b.w)
                toks += [Tok(sem, val) for sem, val in b.r.values()]
            for E in engs:
                K.fence(E, toks)

        def lru_stage(T_, s, first, last):
            fence_bufs([ACT, DVE, POOL], [WsbB])
            if first:
                if s < NPS:
                    for j in range(KC):
                        K.op(POOL, "memset", dict(ap=xp[:, j, 0:3], constant=0.0), writes=[xpB[j]])
                        K.op(POOL, "memset", dict(ap=hstate[:, j:j + 1], constant=0.0), writes=[hstB[j]])
                else:
                    K.dma(SP, "ld_st", xp[:, :, 0:3], st_conv[:, :, s - NPS, :], writes=xpB)
                    tk_st = K.dma(SP, "ld_st", hstate[:], st_h[:, s - NPS, :], writes=hstB)
                    for bb in xpB:
                        bb.w = tk_st
            stats(T_)
            modnorm(T_, G1, 0, 0, s)
            for pc in range(8):
                proj.cur = load_piece(PIECE["w_in"] + pc)
                for oo in range(2):
                    oc = pc * 2 + oo
                    pb = proj(T_, pc, oo, hT, hTB)
                    if oc < 8:
                        K.op(ACT, "activation", dict(out=gate[:, oc, :T_], in_=PS[pb][:, :T_], func=AF.Gelu_apprx_tanh, bias=V("b_in", oc)),
                             reads=[PSB[pb], vecB], writes=[gateB[oc]])
                    else:
                        j = oc - 8
                        K.op(ACT, "activation", dict(out=xp[:, j, 3:3 + T_], in_=PS[pb][:, :T_], func=AF.Identity, bias=V("b_in", oc)),
                             reads=[PSB[pb], vecB], writes=[xpB[j]])
            for j in range(KC):
                K.op(DVE, "tensor_scalar", dict(out=xc[:, j, :T_], in0=xp[:, j, 0:T_], scalar1=V("cw0", j), scalar2=V("cb", j), op0=ALU.mult, op1=ALU.add),
                     reads=[xpB[j], vecB], writes=[xcB[j]])
            for k in range(1, 4):
                for j in range(KC):
                    K.op(DVE, "scalar_tensor_tensor", dict(out=xc[:, j, :T_], in0=xp[:, j, k:k + T_], scalar=V("cw%d" % k, j), in1=xc[:, j, :T_],
                                                           op0=ALU.mult, op1=ALU.add), reads=[xpB[j], vecB, xcB[j]], writes=[xcB[j]])
            for j in range(KC):
                K.op(POOL, "tensor_copy", dict(out=xcb[:, j, :T_], in_=xc[:, j, :T_]), reads=[xcB[j]], writes=[xcbB[j]])
                if last:
                    pass
            if last:
                store_toks.append(K.dma(SP, "st_c", convo[:, :, s, :], xp[:, :, T_:T_ + 3], reads=xpB))
            for j in range(KC):
                K.op(POOL, "tensor_copy", dict(out=xp[:, j, 0:3], in_=xp[:, j, T_:T_ + 3]), reads=[xpB[j]], writes=[xpB[j]])
            for j in range(KC):
                for a_, dst, dstB, bn in ((0, rr, rrB, "gab"), (1, ii, iiB, "gxb")):
                    pb = nbank()
                    K.multi(PE, [("matmul", dict(out=PS[pb][:, :T_], lhsT=gwb[:, a_, j, :], rhs=xcb[:, j, :T_], start=True, stop=True))],
                            reads=[gwbB, xcbB[j]], writes=[PSB[pb]])
                    K.op(ACT, "activation", dict(out=dst[:, j, :T_], in_=PS[pb][:, :T_], func=AF.Sigmoid, bias=V(bn, j)),
                         reads=[PSB[pb], vecB], writes=[dstB[j], yB])
            for j in range(KC):
                K.op(ACT, "activation", dict(out=aa[:, j, :T_], in_=rr[:, j, :T_], func=AF.Exp, scale=cA[:, j:j + 1]), reads=[rrB[j], cAB], writes=[aaB[j]])
                K.op(ACT, "activation", dict(out=mm_[:, j, :T_], in_=rr[:, j, :T_], func=AF.Exp, scale=c2A[:, j:j + 1]), reads=[rrB[j], cAB], writes=[mmB[j]])
            for j in range(KC):
                K.op(ACT, "activation", dict(out=mm_[:, j, :T_], in_=mm_[:, j, :T_], func=AF.Sqrt, bias=ONE, scale=-1.0), reads=[mmB[j]], writes=[mmB[j]])
            for j in range(KC):
                K.op(DVE, "tensor_tensor", dict(out=mm_[:, j, :T_], in0=mm_[:, j, :T_], in1=ii[:, j, :T_], op=ALU.mult), reads=[mmB[j], iiB[j]], writes=[mmB[j]])
            for j in range(KC):
                K.op(DVE, "tensor_tensor", dict(out=mm_[:, j, :T_], in0=mm_[:, j, :T_], in1=xc[:, j, :T_], op=ALU.mult), reads=[mmB[j], xcB[j]], writes=[mmB[j]])
            for j in range(KC):
                K.op(DVE, "tensor_tensor_scan", dict(out=rr[:, j, :T_], data0=aa[:, j, :T_], data1=mm_[:, j, :T_], initial=hstate[:, j:j + 1],
                                                     op0=ALU.mult, op1=ALU.add), reads=[aaB[j], mmB[j], hstB[j]], writes=[rrB[j]])
            for j in range(KC):
                K.op(POOL, "tensor_copy", dict(out=hstate[:, j:j + 1], in_=rr[:, j, T_ - 1:T_]), reads=[rrB[j]], writes=[hstB[j]])
                K.op(DVE, "tensor_tensor", dict(out=yin[:, j, :T_], in0=rr[:, j, :T_], in1=gate[:, j, :T_], op=ALU.mult), reads=[rrB[j], gateB[j]], writes=[yinB[j]])
            if last:
                store_toks.append(K.dma(SP, "st_h", ho[:, s, :], hstate[:], reads=hstB))
            for pc in range(4):
                proj.cur = load_piece(PIECE["w_out"] + pc)
                for oo in range(2):
                    dc = pc * 2 + oo
                    pb = proj(T_, pc, oo, yin, yinB)
                    resid_add(T_, pb, 0, dc, s)

        def conf_stage(T_, s, first, last):
            fence_bufs([ACT, DVE, POOL], [WsbB])
            if first:
                if s < NPS:
                    for j in range(KC):
                        K.op(POOL, "memset", dict(ap=xpc[:, j, 0:30], constant=0.0), writes=[xpcB[j]])
                else:
                    K.dma(SP, "ld_st", xpc[:, :, 0:30], st_dw[:, :, s - NPS, :], writes=xpcB)
            stats(T_)
            modnorm(T_, G1, 1, 0, s)
            for pc in (4, 5, 6, 7, 0, 1, 2, 3):
                proj.cur = load_piece(PIECE["pw1"] + pc)
                for oo in range(2):
                    oc = pc * 2 + oo
                    pb = proj(T_, pc, oo, hT, hTB)
                    if oc >= 8:
                        j = oc - 8
                        K.op(ACT, "activation", dict(out=sgm[:, j, :T_], in_=PS[pb][:, :T_], func=AF.Sigmoid, bias=V("bpw1", oc)),
                             reads=[PSB[pb], vecB], writes=[rrB[j]])
                    else:
                        j = oc
                        K.op(DVE, "scalar_tensor_tensor", dict(out=xpc[:, j, 30:30 + T_], in0=PS[pb][:, :T_], scalar=V("bpw1", j), in1=sgm[:, j, :T_],
                                                               op0=ALU.add, op1=ALU.mult), reads=[PSB[pb], vecB, rrB[j]], writes=[xpcB[j]])
            for j in range(KC):
                K.op(DVE, "tensor_scalar", dict(out=dd[:, j, :T_], in0=xpc[:, j, 0:T_], scalar1=V("dw0", j), scalar2=V("dwb", j), op0=ALU.mult, op1=ALU.add),
                     reads=[xpcB[j], vecB], writes=[xcB[j]])
            for k in range(1, 31):
                for j in range(KC):
                    K.op(DVE, "scalar_tensor_tensor", dict(out=dd[:, j, :T_], in0=xpc[:, j, k:k + T_], scalar=V("dw%d" % k, j), in1=dd[:, j, :T_],
                                                           op0=ALU.mult, op1=ALU.add), reads=[xpcB[j], vecB, xcB[j]], writes=[xcB[j]])
            if last:
                store_toks.append(K.dma(SP, "st_d", dwo[:, :, s, :], xpc[:, :, T_:T_ + 30], reads=xpcB))
            for j in range(KC):
                K.op(POOL, "tensor_copy", dict(out=xpc[:, j, 0:30], in_=xpc[:, j, T_:T_ + 30]), reads=[xpcB[j]], writes=[xpcB[j]])
            for j in range(KC):
                K.op(ACT, "activation", dict(out=sq[:, j, :T_], in_=dd[:, j, :T_], func=AF.Square), reads=[xcB[j]], writes=[sqB[j]])
            pa = nbank()
            K.multi(PE, [("matmul", dict(out=PS[pa][:, :T_], lhsT=ones, rhs=dd[:, j, :T_], start=(j == 0), stop=(j == KC - 1))) for j in range(KC)],
                    reads=xcB + [cstB], writes=[PSB[pa]])
            pb2 = nbank()
            K.multi(PE, [("matmul", dict(out=PS[pb2][:, :T_], lhsT=ones, rhs=sq[:, j, :T_], start=(j == 0), stop=(j == KC - 1))) for j in range(KC)],
                    reads=sqB + [cstB], writes=[PSB[pb2]])
            K.op(ACT, "activation", dict(out=mu[:, :T_], in_=PS[pa][:, :T_], func=AF.Identity, scale=1.0 / D), reads=[PSB[pa]], writes=[muB])
            K.op(DVE, "tensor_tensor", dict(out=musq[:, :T_], in0=mu[:, :T_], in1=mu[:, :T_], op=ALU.mult), reads=[muB], writes=[musqB])
            K.op(DVE, "scalar_tensor_tensor", dict(out=var[:, :T_], in0=PS[pb2][:, :T_], scalar=1.0 / D, in1=musq[:, :T_], op0=ALU.mult, op1=ALU.subtract),
                 reads=[PSB[pb2], musqB], writes=[varB])
            K.op(ACT, "activation", dict(out=sd2[:, :T_], in_=var[:, :T_], func=AF.Sqrt, bias=EPSA), reads=[varB], writes=[sd2B])
            K.op(DVE, "reciprocal", dict(out=rs2[:, :T_], in_=sd2[:, :T_]), reads=[sd2B], writes=[rs2B])
            for j in range(KC):
                i = nrot("t1", 2)
                K.op(POOL, "tensor_tensor", dict(out=t1[i][:, :T_], in0=dd[:, j, :T_], in1=mu[:, :T_], op=ALU.subtract), reads=[xcB[j], muB], writes=[t1B[i]])
                K.op(DVE, "tensor_tensor", dict(out=t1[i][:, :T_], in0=t1[i][:, :T_], in1=rs2[:, :T_], op=ALU.mult), reads=[t1B[i], rs2B], writes=[t1B[i]])
                K.op(ACT, "activation", dict(out=dnb[:, j, :T_], in_=t1[i][:, :T_], func=AF.Silu, bias=V("lnb", j), scale=V("lng", j)),
                     reads=[t1B[i], vecB], writes=[dnbB[j]])
            for pc in range(4):
                proj.cur = load_piece(PIECE["pw2"] + pc)
                for oo in range(2):
                    dc = pc * 2 + oo
                    pb = proj(T_, pc, oo, dnb, dnbB)
                    resid_add(T_, pb, 1, dc, s)

        def peer_stage(T_, s, l):
            stats(T_)
            modnorm(T_, G2, l, 24, s)
            for pc in range(8):
                proj.cur = load_piece(PIECE["wq%d" % l] + pc)
                for oo in range(2):
                    qc = pc * 2 + oo
                    pb = proj(T_, pc, oo, hT, hTB)
                    if qc % 2 == 0:
                        K.op(ACT, "activation", dict(out=qT[:, qc, :T_], in_=PS[pb][:, :T_], func=AF.Identity), reads=[PSB[pb]], writes=[qTB[qc]])
                    else:
                        K.op(DVE, "tensor_copy", dict(out=qT[:, qc, :T_], in_=PS[pb][:, :T_]), reads=[PSB[pb]], writes=[qTB[qc]])
            ngr = (T_ + 127) // 128

            def topk_gen(g):
                g0 = g * 128
                gt = min(128, T_ - g0)
                yield
                for i4 in range(4):
                    pb = nbank()
                    insts = []
                    for i in range(4):
                        qc = i4 * 4 + i
                        insts.append(("matmul", dict(out=PS[pb][:gt, i * 128:(i + 1) * 128], lhsT=qT[:, qc, g0:g0 + gt], rhs=kTb[:, l * 2 + (qc % 2), :],
                                                     start=True, stop=True)))
                    K.multi(PE, insts, reads=[qTB[i4 * 4 + i] for i in range(4)] + [kTbB], writes=[PSB[pb]])
                    K.op(ACT, "activation", dict(out=s_sb[:gt, i4 * 4:(i4 + 1) * 4, :], in_=PS[pb][:gt, :].rearrange("p (a b) -> p a b", b=128), func=AF.Identity),
                         reads=[PSB[pb]], writes=[ssbB[i4]])
                yield
                for qc in range(16):
                    K.op(DVE, "max", dict(out=v1[:gt, qc, 0:8], in_=s_sb[:gt, qc, :]), reads=[ssbB[qc // 4]], writes=[v1B[qc]])
                yield
                for qc in range(16):
                    K.op(DVE, "max_index", dict(out=ix[:gt, qc, 0:8], in_max=v1[:gt, qc, 0:8], in_values=s_sb[:gt, qc, :]), reads=[ssbB[qc // 4], v1B[qc]], writes=[ixB[qc]])
                yield
                for qc in range(16):
                    K.op(DVE, "match_replace", dict(out=s2[:gt, qc, :], in_to_replace=v1[:gt, qc, 0:8], in_values=s_sb[:gt, qc, :], imm_value=NEG),
                         reads=[ssbB[qc // 4], v1B[qc]], writes=[s2B[qc // 4]])
                yield
                for qc in range(16):
                    K.op(DVE, "max", dict(out=v1[:gt, qc, 8:16], in_=s2[:gt, qc, :]), reads=[s2B[qc // 4]], writes=[v1B[qc]])
                yield
                for qc in range(16):
                    K.op(DVE, "max_index", dict(out=ix[:gt, qc, 8:16], in_max=v1[:gt, qc, 8:16], in_values=s2[:gt, qc, :]), reads=[s2B[qc // 4], v1B[qc]], writes=[ixB[qc]])
                K.op(DVE, "tensor_copy", dict(out=ixf[:gt], in_=ix[:gt]), reads=ixB, writes=[ixfB])
                K.op(DVE, "tensor_tensor", dict(out=cand[:gt].rearrange("p h (a b) -> p h a b", b=16),
                                                in0=cap(v1, 0, [[32, 8], [1, 16], [0, 16]], gt), in1=cap(v1, 16, [[32, 8], [0, 16], [1, 16]], gt), op=ALU.add),
                     reads=v1B, writes=candB)
                yield
                for h in range(8):
                    K.op(DVE, "max", dict(out=cv[:gt, h, 0:8], in_=cand[:gt, h, :]), reads=[candB[h]], writes=[cvB[h]])
                yield
                for h in range(8):
                    K.op(DVE, "max_index", dict(out=ci[:gt, h, 0:8], in_max=cv[:gt, h, 0:8], in_values=cand[:gt, h, :]), reads=[candB[h], cvB[h]], writes=[ciB[h]])
                yield
                for h in range(8):
                    K.op(DVE, "match_replace", dict(out=cand2[:gt, h, :], in_to_replace=cv[:gt, h, 0:8], in_values=cand[:gt, h, :], imm_value=NEG),
                         reads=[candB[h], cvB[h]], writes=[cand2B[h]])
                yield
                for h in range(8):
                    K.op(DVE, "max", dict(out=cv[:gt, h, 8:16], in_=cand2[:gt, h, :]), reads=[cand2B[h]], writes=[cvB[h]])
                yield
                for h in range(8):
                    K.op(DVE, "max_index", dict(out=ci[:gt, h, 8:16], in_max=cv[:gt, h, 8:16], in_values=cand2[:gt, h, :]), reads=[cand2B[h], cvB[h]], writes=[ciB[h]])
                K.op(DVE, "tensor_tensor", dict(out=ge[:gt], in0=cv[:gt], in1=cap(cv, 0, [[16, 8], [0, 16]], gt), op=ALU.subtract), reads=cvB, writes=[geB])
                K.op(ACT, "activation", dict(out=ge[:gt], in_=ge[:gt], func=AF.Exp), reads=[geB], writes=[geB])
                K.op(DVE, "tensor_reduce", dict(out=gs[:gt], in_=ge[:gt], axis=AX.X, op=ALU.add), reads=[geB], writes=[gsB])
                K.op(DVE, "reciprocal", dict(out=gs2[:gt], in_=gs[:gt]), reads=[gsB], writes=[gs2B])
                K.op(DVE, "tensor_tensor", dict(out=idxg[:gt, 2, :].rearrange("p (h k) -> p h k", k=16), in0=ge[:gt], in1=cap(gs2, 0, [[1, 8], [0, 16]], gt), op=ALU.mult),
                     reads=[geB, gs2B], writes=[idxgB])
                K.op(DVE, "tensor_scalar", dict(out=cab[:gt, 0, :], in0=ci[:gt].rearrange("p h k -> p (h k)"), scalar1=4, scalar2=None, op0=ALU.logical_shift_right), reads=ciB, writes=[cabB])
                K.op(DVE, "tensor_scalar", dict(out=cab[:gt, 1, :], in0=ci[:gt].rearrange("p h k -> p (h k)"), scalar1=15, scalar2=None, op0=ALU.bitwise_and), reads=ciB, writes=[cabB])
                K.op(DVE, "tensor_copy", dict(out=cabf[:gt], in_=cab[:gt]), reads=[cabB], writes=[cabfB])
                yield
                for hf in range(2):
                    K.op(DVE, "tensor_tensor", dict(out=eqt[:gt], in0=cap(cabf, hf * 128, [[16, 8], [1, 16], [0, 16]], gt),
                                                    in1=cap(cst, 2 * 128, [[0, 8], [0, 16], [1, 16]], gt), op=ALU.is_equal),
                         reads=[cabfB, cstB], writes=ssbB)
                    K.op(DVE, "tensor_tensor", dict(out=prod[:gt], in0=eqt[:gt], in1=cap(ixf, hf * 16, [[32, 8], [0, 16], [1, 16]], gt), op=ALU.mult),
                         reads=ssbB + [ixfB], writes=s2B)
                    K.op(DVE, "tensor_reduce", dict(out=idxg[:gt, hf, :], in_=prod[:gt].rearrange("p h a b -> p (h a) b"), axis=AX.X, op=ALU.add),
                         reads=s2B, writes=[idxgB])
                pb = nbank()
                insts = [("transpose", dict(out=PS[pb][:, i * 128:i * 128 + gt], in_=idxg[:gt, i, :], identity=cst[:gt, 0, :gt])) for i in range(3)]
                K.multi(PE, insts, reads=[idxgB, cstB], writes=[PSB[pb]])
                K.op(ACT, "activation", dict(out=idxT[:, :, g0:g0 + gt], in_=PS[pb][:, 0:384].rearrange("p (a b) -> p a b", b=128)[:, :, :gt], func=AF.Identity),
                     reads=[PSB[pb]], writes=[idxTB[g]])
                yield

            def wbuild_gen(g):
                g0 = g * 128
                gt = min(128, T_ - g0)
                if g == 0:
                    fence_bufs([ACT, DVE], arenaB + [yB])
                io = cap(iota_b, 0, [[0, SG], [1, 128]])

                def build(t0):
                    st = nrot("oh", 3)
                    eqA, Aoh, Boh = oh[st]
                    eqAB, AohB, BohB = ohB[st]
                    K.op(DVE, "tensor_tensor", dict(out=eqA[:], in0=io, in1=cap(idxT, 0 * T + t0, [[1, SG], [0, 128]]), op=ALU.is_equal),
                         reads=[iotabB, idxTB[g]], writes=[eqAB])
                    K.op(POOL, "tensor_tensor", dict(out=Aoh[:], in0=eqA[:], in1=cap(idxT, 2 * T + t0, [[1, SG], [0, 128]]), op=ALU.mult),
                         reads=[eqAB, idxTB[g]], writes=[AohB])
                    K.op(DVE, "tensor_tensor", dict(out=Boh[:], in0=io, in1=cap(idxT, 1 * T + t0, [[1, SG], [0, 128]]), op=ALU.is_equal),
                         reads=[iotabB, idxTB[g]], writes=[BohB])
                    return st

                def mmev(t0, st):
                    eqA, Aoh, Boh = oh[st]
                    eqAB, AohB, BohB = ohB[st]
                    for q4 in range(SG // 4):
                        pb = nbank()
                        insts = [("matmul", dict(out=PS[pb][:, i * 128:(i + 1) * 128], lhsT=Aoh[:, q4 * 4 + i, :], rhs=Boh[:, q4 * 4 + i, :], start=True, stop=True))
                                 for i in range(4)]
                        K.multi(PE, insts, reads=[AohB, BohB], writes=[PSB[pb]])
                        tt = t0 + q4 * 4
                        dst = Wsb[:, tt:tt + 4, :]
                        src = PS[pb][:, :].rearrange("p (a b) -> p a b", b=128)
                        if nrot("cp", 2) == 0:
                            K.op(ACT, "activation", dict(out=dst, in_=src, func=AF.Identity), reads=[PSB[pb]], writes=[WsbB])
                        else:
                            K.op(DVE, "tensor_copy", dict(out=dst, in_=src), reads=[PSB[pb]], writes=[WsbB])

                prev = None
                for t0 in range(g0, g0 + gt, SG):
                    st = build(t0)
                    if prev is not None:
                        mmev(*prev)
                    prev = (t0, st)
                    yield
                mmev(*prev)
                yield

            for _ in topk_gen(0):
                pass
            for g in range(ngr):
                gens = [wbuild_gen(g)]
                if g + 1 < ngr:
                    gens.append(topk_gen(g + 1))
                while gens:
                    for gg in list(gens):
                        try:
                            next(gg)
                        except StopIteration:
                            gens.remove(gg)
            nb = min(8, 512 // T_)

            def OUT(dc):
                return PS[4 + dc // nb][:, (dc % nb) * T_:(dc % nb + 1) * T_]

            outB = PSB[4:8]

            def load_uv(cg):
                st = nrot("uv", 2)
                off = (l * 128 + cg * G_UV) * 128 * 1024
                K.dma(SP, "ld_u%d" % st, ubuf[st][:], dap(u_b, off, [[1024, 128], [128 * 1024, G_UV], [1, 1024]]), reads=[ubB], writes=[ubufB[st]])
                K.dma(SP, "ld_v%d" % st, vbuf[st][:], dap(v_b, off, [[1024, 128], [128 * 1024, G_UV], [1, 1024]]), reads=[vbB], writes=[vbufB[st]])
                return st

            def zmm(st, gi):
                pb = nbank()
                insts = [("matmul", dict(out=PS[pb][:, :T_], lhsT=ubuf[st][:, gi, k * 128:(k + 1) * 128], rhs=hT[:, k, :T_], start=(k == 0), stop=(k == KC - 1)))
                         for k in range(KC)]
                K.multi(PE, insts, reads=[ubufB[st]] + hTB, writes=[PSB[pb]])
                return pb

            sts = {}
            sts[0] = load_uv(0)
            pend = zmm(sts[0], 0)
            for c in range(128):
                cg, gi = divmod(c, G_UV)
                pb = pend
                i = nrot("gz", 3)
                K.op(ACT, "activation", dict(out=gz[i][:, :T_], in_=PS[pb][:, :T_], func=AF.Gelu_apprx_tanh), reads=[PSB[pb]], writes=[gzB[i]])
                K.op(DVE, "tensor_tensor", dict(out=wz[i][:, :T_], in0=gz[i][:, :T_], in1=Wsb[:, 0:T_, c], op=ALU.mult), reads=[gzB[i], WsbB], writes=[wzB[i]])
                if c + 1 < 128:
                    cg2, gi2 = divmod(c + 1, G_UV)
                    if gi2 == 0:
                        sts[cg2] = load_uv(cg2)
                    pend = zmm(sts[cg2], gi2)
                insts = [("matmul", dict(out=OUT(dc), lhsT=vbuf[sts[cg]][:, gi, dc * 128:(dc + 1) * 128], rhs=wz[i][:, :T_],
                                         start=(c == 0 and dc % nb == 0), stop=(c == 127))) for dc in range(KC)]
                K.multi(PE, insts, reads=[vbufB[sts[cg]], wzB[i]], writes=outB)
            for dc in range(KC):
                K.op(DVE, "scalar_tensor_tensor", dict(out=x[:, dc, :T_], in0=OUT(dc), scalar=MOD(l, 40 + dc, s), in1=x[:, dc, :T_], op0=ALU.mult, op1=ALU.add),
                     reads=outB + [modvB, xB[dc]], writes=[xB[dc]])

        tiles = []
        for s in range(NPS if not ONLY_SAMPLE else min(1, DBG_TILES)):
            nt = SEQ // T
            for j in range(nt if not ONLY_SAMPLE else DBG_TILES):
                tiles.append((s, s * SEQ + j * T, T, j == 0, j == nt - 1))
        for s in range(NSS):
            tiles.append((NPS + s, NPS * SEQ + s * DSEQ, DSEQ, True, True))
        for (s, tok0, T_, first, last) in tiles:
            K.dma(SP, "ld_x", x[:, :, :T_], xin[:, :, tok0:tok0 + T_], writes=xB)
            lru_stage(T_, s, first, last)
            peer_stage(T_, s, 0)
            conf_stage(T_, s, first, last)
            peer_stage(T_, s, 1)
            stats(T_)
            for k in range(KC):
                K.op(DVE, "scalar_tensor_tensor", dict(out=ybuf[:, k, :T_], in0=x[:, k, :T_], scalar=V("nfin", k), in1=rstd[:, :T_], op0=ALU.mult, op1=ALU.mult),
                     reads=[xB[k], vecB, rstdB], writes=[yB] + iiB)
            store_toks.append(K.dma(SP, "st_y", yout[:, :, tok0:tok0 + T_], ybuf[:, :, :T_], reads=[yB]))
        last_tok = {}
        for tk in store_toks:
            last_tok[id(tk.sem)] = tk
        K.fence(SP, list(last_tok.values()))

        with nc.Block() as block:
            @block.tensor
            def _(e):
                K.replay(PE, e)

            @block.scalar
            def _(e):
                K.replay(ACT, e)

            @block.vector
            def _(e):
                K.replay(DVE, e)

            @block.gpsimd
            def _(e):
                K.replay(POOL, e)

            @block.sync
            def _(e):
                K.replay(SP, e)
    return nc


_PROG = {}


def _fm(a):
    a = np.asarray(a, dtype=np.float32)
    lead = a.shape[:-1]
    a = a.reshape(lead + (KC, 128))
    nd = a.ndim
    perm = (nd - 1, nd - 2) + tuple(range(nd - 2))
    return np.ascontiguousarray(a.transpose(perm))


def kernel(**inp):
    f = lambda k: np.asarray(inp[k], dtype=np.float32)
    vec = np.zeros((128, NV), np.float32)

    def put(name, v):
        v = np.asarray(v, np.float32).reshape(-1, 128)
        vec[:, VOFF[name]:VOFF[name] + v.shape[0]] = v.T

    put("nm0", f("norm_mix")[0]); put("nm1", f("norm_mix")[1])
    put("nf0", f("norm_ffn")[0]); put("nf1", f("norm_ffn")[1]); put("nfin", f("norm_final"))
    put("b_in", f("lru_b_in")[0])
    for k in range(4):
        put("cw%d" % k, f("lru_conv_w")[0, k])
    put("cb", f("lru_conv_b")[0]); put("gab", f("lru_gate_a_b")[0]); put("gxb", f("lru_gate_x_b")[0])
    put("lam", f("lru_lambda")[0]); put("bout", f("lru_b_out")[0]); put("bpw1", f("cf_b_pw1")[0])
    for k in range(31):
        put("dw%d" % k, f("cf_dw_w")[0, k])
    put("dwb", f("cf_dw_b")[0]); put("lng", f("cf_ln_g")[0]); put("lnb", f("cf_ln_b")[0]); put("bpw2", f("cf_b_pw2")[0])
    adab = np.ascontiguousarray(f("ada_b").reshape(2, 48, 128).transpose(2, 0, 1))
    adaw = np.ascontiguousarray(f("ada_w").reshape(2, KC, 128, 12, 512).transpose(0, 3, 2, 1, 4)).reshape(2, 12, 128, KC * 512)
    gw = np.stack([f("lru_gate_a_w")[0], f("lru_gate_x_w")[0]], 0)
    gw = np.ascontiguousarray(gw.transpose(2, 0, 1, 3)).reshape(128, 2 * 8 * 128)
    kk = np.stack([f("peer_k1")[0], f("peer_k2")[0], f("peer_k1")[1], f("peer_k2")[1]], 0)
    kT = np.ascontiguousarray(kk.transpose(2, 0, 1)).reshape(128, 4 * 128)
    wcat = np.concatenate([f("lru_w_in")[0], f("lru_w_out")[0], f("cf_w_pw1")[0], f("cf_w_pw2")[0], f("peer_w_q")[0], f("peer_w_q")[1]], axis=1)
    wd = np.ascontiguousarray(wcat.reshape(KC, 128, NPIECE, 256).transpose(2, 1, 0, 3)).reshape(-1, 2048)
    u_arr = np.ascontiguousarray(f("peer_u").reshape(2, 128, 128, KC, 128).transpose(0, 2, 4, 3, 1)).reshape(-1, 2048)
    v_arr = np.ascontiguousarray(f("peer_v").reshape(2, 128, 128, D).transpose(0, 2, 1, 3)).reshape(-1, 2048)
    consts = np.zeros((128, 4, 128), np.float32)
    consts[:, 3, 1] = 1.0
    consts[:, 3, 2] = EPS
    consts[:, 0, :] = np.eye(128, dtype=np.float32)
    consts[:, 1, :] = 1.0
    consts[:, 2, :] = np.arange(128, dtype=np.float32)[None, :]
    xp_, xs_ = f("x_prompt"), f("x_sample")
    cp_, cs_ = f("c_prompt"), f("c_sample")
    sh_, sc_, sd_ = f("state_lru_h"), f("state_lru_conv"), f("state_dwconv")
    in_maps = []
    for c in range(NCORES):
        xt = np.concatenate([xp_[NPS * c:NPS * c + NPS].reshape(-1, D), xs_[NSS * c:NSS * c + NSS].reshape(-1, D)], 0)
        cc = np.concatenate([cp_[NPS * c:NPS * c + NPS], cs_[NSS * c:NSS * c + NSS]], 0)
        in_maps.append(dict(
            xin=_fm(xt), cT=_fm(cc), st_h=np.ascontiguousarray(_fm(sh_[0, NSS * c:NSS * c + NSS]).transpose(0, 2, 1)),
            st_conv=_fm(sc_[0, NSS * c:NSS * c + NSS]), st_dw=_fm(sd_[0, NSS * c:NSS * c + NSS]),
            consts=consts, vec=vec, adab=adab, adaw=adaw, gw=gw, kT=kT, wd=wd, u_arr=u_arr, v_arr=v_arr))
    if "p" not in _PROG:
        _PROG["p"] = build_program()
    res = run_bass_kernel_spmd(_PROG["p"], in_maps, core_ids=list(range(NCORES)))
    B, DB = NPS * NCORES, NSS * NCORES
    y_p = np.zeros((B, SEQ, D), np.float32); y_s = np.zeros((DB, DSEQ, D), np.float32)
    h_p = np.zeros((1, B, D), np.float32); h_s = np.zeros((1, DB, D), np.float32)
    cv_p = np.zeros((1, B, 3, D), np.float32); cv_s = np.zeros((1, DB, 3, D), np.float32)
    dw_p = np.zeros((1, B, 30, D), np.float32); dw_s = np.zeros((1, DB, 30, D), np.float32)

    def unfm(a):
        nd = a.ndim
        perm = tuple(range(2, nd)) + (1, 0)
        a = a.transpose(perm)
        return a.reshape(a.shape[:-2] + (D,))

    for c in range(NCORES):
        r = res.results[c]
        y = unfm(np.asarray(r["yout"]))
        y_p[NPS * c:NPS * c + NPS] = y[:NPS * SEQ].reshape(NPS, SEQ, D)
        y_s[NSS * c:NSS * c + NSS] = y[NPS * SEQ:].reshape(NSS, DSEQ, D)
        h = unfm(np.ascontiguousarray(np.asarray(r["ho"]).transpose(0, 2, 1)))
        h_p[0, NPS * c:NPS * c + NPS] = h[:NPS]; h_s[0, NSS * c:NSS * c + NSS] = h[NPS:]
        cvv = unfm(np.asarray(r["convo"]))
        cv_p[0, NPS * c:NPS * c + NPS] = cvv[:NPS]; cv_s[0, NSS * c:NSS * c + NSS] = cvv[NPS:]
        dww = unfm(np.asarray(r["dwo"]))
        dw_p[0, NPS * c:NPS * c + NPS] = dww[:NPS]; dw_s[0, NSS * c:NSS * c + NSS] = dww[NPS:]
    return (y_p, y_s, h_p, cv_p, dw_p, h_s, cv_s, dw_s)
```

```python
import numpy as np
from contextlib import ExitStack
import concourse.bass as bass
import concourse.mybir as mybir
from concourse.bass_utils import run_bass_kernel_spmd

F32 = mybir.dt.float32
BF16 = mybir.dt.bfloat16
U32 = mybir.dt.uint32
ALU = mybir.AluOpType
AF = mybir.ActivationFunctionType
AX = mybir.AxisListType

NCORES = 8
D = 1024
KC = 8
TMAX = 256
SEQ = 2048
NPS = 4
NSS = 2
DSEQ = 32
NTOK = NPS * SEQ + NSS * DSEQ
EPS = 1e-6
G_UV = 2
SG = 8
NEG = -1.0e30
EMBED = True
import os
ONLY_SAMPLE = bool(int(os.environ.get("PEER_ONLY_SAMPLE", "0")))
DBG_TILES = int(os.environ.get("PEER_DBG_TILES", "0"))

VEC_SPEC = [("nm0", 8), ("nm1", 8), ("nf0", 8), ("nf1", 8), ("nfin", 8), ("b_in", 16),
            ("cw0", 8), ("cw1", 8), ("cw2", 8), ("cw3", 8), ("cb", 8), ("gab", 8), ("gxb", 8),
            ("lam", 8), ("bout", 8), ("bpw1", 16)] + [("dw%d" % k, 8) for k in range(31)] + \
           [("dwb", 8), ("lng", 8), ("lnb", 8), ("bpw2", 8)]
VOFF = {}
_o = 0
for _n, _w in VEC_SPEC:
    VOFF[_n] = _o
    _o += _w
NV = _o
PIECE = {"w_in": 0, "w_out": 8, "pw1": 12, "pw2": 20, "wq0": 24, "wq1": 32}
NPIECE = 40


class Tok:
    __slots__ = ("sem", "val")

    def __init__(self, sem, val):
        self.sem = sem
        self.val = val


class Buf:
    def __init__(self, name):
        self.name = name
        self.w = None
        self.r = {}


class Eng:
    def __init__(self, name):
        self.name = name
        self.sem = None
        self.cnt = 0
        self.seen = {}
        self.ops = []


class Sched:
    def __init__(self):
        self.pe = Eng("pe")
        self.act = Eng("act")
        self.dve = Eng("dve")
        self.pool = Eng("pool")
        self.sp = Eng("sp")
        self.engs = [self.pe, self.act, self.dve, self.pool, self.sp]
        self.dsem = {}
        self.semlist = []

    def _waits(self, E, reads, writes):
        waits = {}

        def need(tok, same_ok):
            if tok is None:
                return
            if tok.sem is E.sem and (same_ok or E is self.pe):
                return
            k = id(tok.sem)
            if k not in waits or waits[k][1] < tok.val:
                waits[k] = (tok.sem, tok.val)

        for b in reads:
            need(b.w, False)
        for b in writes:
            need(b.w, True)
            for sem, val in b.r.values():
                need(Tok(sem, val), True)
        wl = []
        for k, (sem, val) in waits.items():
            if E.seen.get(k, 0) < val:
                E.seen[k] = val
                wl.append((sem, val))
        return wl

    def _commit(self, tok, reads, writes):
        k = id(tok.sem)
        for b in reads:
            if k not in b.r or b.r[k][1] < tok.val:
                b.r[k] = (tok.sem, tok.val)
        for b in writes:
            b.w = tok
            b.r = {}

    def op(self, E, name, kw, reads=(), writes=()):
        if name == "activation" and "bias" not in kw:
            kw["bias"] = self.zero[:kw["in_"].shape[0]]
        return self.multi(E, [(name, kw)], reads, writes)

    def multi(self, E, insts, reads=(), writes=()):
        wl = self._waits(E, reads, writes)
        E.cnt += 1
        tok = Tok(E.sem, E.cnt)
        E.ops.append((wl, insts, E.sem, 1))
        self._commit(tok, reads, writes)
        return tok

    def dma(self, Q, key, out, in_, reads=(), writes=()):
        wl = self._waits(Q, reads, writes)
        ent = self.dsem[key]
        ent[1] += 16
        tok = Tok(ent[0], ent[1])
        Q.ops.append((wl, [("dma_start", dict(out=out, in_=in_))], ent[0], 16))
        self._commit(tok, reads, writes)
        return tok

    def fence(self, E, toks):
        wl = []
        for tok in toks:
            k = id(tok.sem)
            if E.seen.get(k, 0) < tok.val:
                E.seen[k] = tok.val
                wl.append((tok.sem, tok.val))
        if wl:
            E.ops.append((wl, [], None, 0))

    def replay(self, E, e):
        for wl, insts, sem, inc in E.ops:
            for s, v in wl:
                e.wait_ge(s, v)
            last = None
            for name, kw in insts:
                last = getattr(e, name)(**kw)
            if last is not None and sem is not None:
                last.then_inc(sem, inc)


def cap(full, off, dims, nparts=128):
    pstep = full.ap[0][0]
    return bass.AP(tensor=full.tensor, offset=off, ap=[[pstep, nparts]] + [list(d) for d in dims])


def dap(t, off, dims):
    return bass.AP(tensor=t.tensor, offset=off, ap=[list(d) for d in dims])


def build_program():
    nc = bass.Bass("TRN2", target_bir_lowering=False)
    K = Sched()
    PE, ACT, DVE, POOL, SP = K.pe, K.act, K.dve, K.pool, K.sp

    def din(name, shape, dt=F32):
        return nc.dram_tensor(name, list(shape), dt, kind="ExternalInput").ap()

    def dout(name, shape, dt=F32):
        return nc.dram_tensor(name, list(shape), dt, kind="ExternalOutput").ap()

    xin = din("xin", [128, KC, NTOK])
    cT = din("cT", [128, KC, 6])
    st_h = din("st_h", [128, NSS, KC])
    st_conv = din("st_conv", [128, KC, NSS, 3])
    st_dw = din("st_dw", [128, KC, NSS, 30])
    consts = din("consts", [128, 5, 128])
    vec_d = din("vec", [128, NV])
    adab_d = din("adab", [128, 2, 48])
    adaw_d = din("adaw", [2, 12, 128, KC * 512])
    gw_d = din("gw", [128, 2 * 8 * 128])
    kT_d = din("kT", [128, 4 * 128])
    wd_d = din("wd", [NPIECE * 128 * KC * 256 // 2048, 2048])
    u_d = din("u_arr", [2 * 128 * 128 * 1024 // 2048, 2048])
    v_d = din("v_arr", [2 * 128 * 128 * 1024 // 2048, 2048])
    wd_b = nc.dram_tensor("wd_b", [NPIECE * 128 * KC * 256 // 2048, 2048], BF16, kind="Internal").ap()
    u_b = nc.dram_tensor("u_b", [2 * 128 * 128 * 1024 // 2048, 2048], BF16, kind="Internal").ap()
    v_b = nc.dram_tensor("v_b", [2 * 128 * 128 * 1024 // 2048, 2048], BF16, kind="Internal").ap()
    yout = dout("yout", [128, KC, NTOK])
    ho = dout("ho", [128, 6, KC])
    convo = dout("convo", [128, KC, 6, 3])
    dwo = dout("dwo", [128, KC, 6, 30])

    es = ExitStack()
    with es:
        sb_off = [20480]

        def SBt(name, shape, dt, at=None):
            esz = 2 if dt == BF16 else 4
            n = 1
            for s in shape[1:]:
                n *= s
            nbytes = (n * esz + 63) // 64 * 64
            if at is None:
                at = sb_off[0]
                sb_off[0] += nbytes
            t = nc.alloc_sbuf_tensor_at(name, list(shape), dt, offset=at)
            return t.ap()

        T = TMAX
        cst_off = sb_off[0]
        cst = SBt("cst", [128, 5, 128], F32)
        cstu = SBt("cstu", [128, 5, 128], U32, at=cst_off)
        ZERO = cst[:, 3, 0:1]
        ONE = cst[:, 3, 1:2]
        EPSA = cst[:, 3, 2:3]
        K.zero = ZERO
        ident = cst[:, 0, :]
        ones = cst[:, 1, :]
        iota_f = cst[:, 2, :]
        iota_b = SBt("iota_b", [128, 128], BF16)
        vec = SBt("vec", [128, NV], F32)
        adab = SBt("adab", [128, 2, 48], F32)
        modv = SBt("modv", [128, 2, 48, 6], F32)
        G1 = SBt("G1", [128, 2, 8, 6], F32)
        GB1 = SBt("GB1", [128, 2, 8, 6], F32)
        G2 = SBt("G2", [128, 2, 8, 6], F32)
        cA = SBt("cA", [128, 8], F32)
        c2A = SBt("c2A", [128, 8], F32)
        ptmp = SBt("ptmp", [128, 4, 8], F32)
        cTs = SBt("cTs", [128, KC, 6], F32)
        csil = SBt("csil", [128, KC, 6], F32)
        gwb = SBt("gwb", [128, 2, 8, 128], BF16)
        kTb = SBt("kTb", [128, 4, 128], BF16)
        hstate = SBt("hstate", [128, 8], F32)
        xp = SBt("xp", [128, KC, 3 + T], F32)
        xpc = SBt("xpc", [128, KC, 30 + T], F32)
        x = SBt("x", [128, KC, T], F32)
        hT = SBt("hT", [128, KC, T], BF16)
        sdt = SBt("sdt", [128, T], F32)
        rstd = SBt("rstd", [128, T], F32)
        tmpk = [SBt("tmpk%d" % i, [128, T], F32) for i in range(2)]
        wslot = [SBt("wslot%d" % i, [128, KC, 256], BF16) for i in range(5)]
        NSET = 4
        ubuf = [SBt("ubuf%d" % i, [128, 1024], BF16) for i in range(NSET)]
        vbuf = [SBt("vbuf%d" % i, [128, 1024], BF16) for i in range(NSET)]
        arena_off = sb_off[0]
        gate = SBt("gate", [128, KC, T], BF16)
        xc = SBt("xc", [128, KC, T], F32)
        xcb = SBt("xcb", [128, KC, T], BF16)
        rr = SBt("rr", [128, KC, T], F32)
        ii_off = sb_off[0]
        ii = SBt("ii", [128, KC, T], F32)
        ybuf = SBt("ybuf", [128, KC, T], F32, at=ii_off)
        aa_off = sb_off[0]
        aa = SBt("aa", [128, KC, T], F32)
        mm_ = SBt("mm", [128, KC, T], F32)
        yin_off = sb_off[0]
        yin = SBt("yin", [128, KC, T], BF16)
        rtmp = [SBt("rtmp%d" % i, [128, T], F32) for i in range(2)]
        sgm = rr
        dd = xc
        tsz = T * 4
        mu = SBt("mu", [128, T], F32, at=aa_off)
        musq = SBt("musq", [128, T], F32, at=aa_off + tsz)
        var = SBt("var", [128, T], F32, at=aa_off + 2 * tsz)
        sd2 = SBt("sd2", [128, T], F32, at=aa_off + 3 * tsz)
        rs2 = SBt("rs2", [128, T], F32, at=aa_off + 4 * tsz)
        t1 = [SBt("t1_%d" % i, [128, T], F32, at=aa_off + (5 + i) * tsz) for i in range(2)]
        dnb = SBt("dnb", [128, KC, T], BF16, at=yin_off)
        arena_end = sb_off[0]
        sb_off[0] = max(arena_end, arena_off + 128 * T * 2)
        qT_off = sb_off[0]
        qT = SBt("qT", [128, 16, T], BF16)
        sq = SBt("sq", [128, KC, T], F32, at=qT_off)
        s_off = sb_off[0]
        s_sb = SBt("s_sb", [128, 16, 128], F32)
        s2_off = sb_off[0]
        s2 = SBt("s2", [128, 16, 128], F32)
        s_sbu = SBt("s_sbu", [128, 16, 128], U32, at=s_off)
        eqt = SBt("eqt", [128, 8, 16, 16], F32, at=s_off)
        prod = SBt("prod", [128, 8, 16, 16], F32, at=s2_off)
        v1_off = sb_off[0]
        v1 = SBt("v1", [128, 16, 16], F32)
        v1u = SBt("v1u", [128, 16, 16], U32, at=v1_off)
        ix = SBt("ix", [128, 16, 16], U32)
        ixf = SBt("ixf", [128, 16, 16], F32)
        cand = SBt("cand", [128, 8, 256], F32, at=s_off)
        cand2 = SBt("cand2", [128, 8, 256], F32, at=s2_off)
        cv = SBt("cv", [128, 8, 16], F32)
        ci = SBt("ci", [128, 8, 16], U32)
        cab = SBt("cab", [128, 2, 128], U32)
        cabf = SBt("cabf", [128, 2, 128], F32)
        ge = SBt("ge", [128, 8, 16], F32)
        gs = SBt("gs", [128, 8], F32)
        gs2 = SBt("gs2", [128, 8], F32)
        idxg = SBt("idxg", [128, 3, 128], F32)
        idxT = SBt("idxT", [128, 3, T], BF16)
        oh = [[SBt("oh%d_e" % s, [128, SG, 2, 128], BF16), SBt("oh%d_a" % s, [128, SG, 128], BF16)] for s in range(3)]
        wsb_off = arena_off
        Wsb = SBt("Wsb", [128, T, 128], BF16, at=arena_off)
        assert arena_end - arena_off <= 128 * T * 2
        print("SBUF end", sb_off[0], "arena", arena_off, arena_end - arena_off)
        gz = [SBt("gz%d" % i, [128, T], BF16) for i in range(3)]
        wz = [SBt("wz%d" % i, [128, T], BF16) for i in range(3)]
        assert sb_off[0] <= 224 * 1024 - 2048, sb_off[0]
        adas = [SBt("adas%d" % i, [128, KC, 512], F32, at=wsb_off + i * 16384) for i in range(2)]
        gws = SBt("gws", [128, 2 * 8 * 128], F32, at=s_off)
        kTs = SBt("kTs", [128, 4 * 128], F32, at=s2_off)

        PS = [es.enter_context(nc.psum_tensor("ps%d" % i, [128, 512], F32)) for i in range(8)]
        PSB = [Buf("ps%d" % i) for i in range(8)]
        for E in K.engs:
            E.sem = es.enter_context(nc.semaphore("sem_" + E.name))

        def dsem(key):
            K.dsem[key] = [es.enter_context(nc.semaphore("d_" + key)), 0]

        for key in ["ld_misc", "ld_x", "ld_w0", "ld_w1", "ld_w2", "ld_w3", "ld_w4", "ld_u0", "ld_u1", "ld_u2", "ld_u3", "ld_v0", "ld_v1", "ld_v2", "ld_v3",
                    "st_y", "st_c", "st_h", "st_d", "cast_w", "cast_u", "cast_v", "ld_a0", "ld_a1", "ld_st"]:
            dsem(key)

        rot = {"bank": 0, "w": 0, "uv": 0, "tmpk": 0, "rtmp": 0, "t1": 0, "gz": 0, "oh": 0, "cp": 0}

        def nbank():
            b = rot["bank"]
            rot["bank"] = (b + 1) % 4
            return b

        def nrot(key, n):
            v = rot[key]
            rot[key] = (v + 1) % n
            return v

        B = lambda n: Buf(n)
        cstB, vecB, adabB, modvB, GB_, cAB, csilB, gwbB, kTbB = B("cst"), B("vec"), B("adab"), B("modv"), B("G"), B("cA"), B("csil"), B("gwb"), B("kTb")
        iotabB, ptmpB, cTsB, gwsB, kTsB = B("iotab"), B("ptmp"), B("cTs"), B("gws"), B("kTs")
        hstB = [B("hst%d" % j) for j in range(8)]
        xpB = [B("xp%d" % j) for j in range(8)]
        xpcB = [B("xpc%d" % j) for j in range(8)]
        xB = [B("x%d" % j) for j in range(8)]
        hTB = [B("hT%d" % j) for j in range(8)]
        yB = B("ybuf")
        sqB = [B("sq%d" % j) for j in range(8)]
        sdtB, rstdB = B("sdt"), B("rstd")
        tmpkB = [B("tmpk0"), B("tmpk1")]
        wslotB = [B("ws%d" % i) for i in range(5)]
        ubufB = [B("ub%d" % i) for i in range(NSET)]
        vbufB = [B("vb%d" % i) for i in range(NSET)]
        gateB = [B("gate%d" % j) for j in range(8)]
        xcB = [B("xc%d" % j) for j in range(8)]
        xcbB = [B("xcb%d" % j) for j in range(8)]
        rrB = [B("rr%d" % j) for j in range(8)]
        iiB = [B("ii%d" % j) for j in range(8)]
        aaB = [B("aa%d" % j) for j in range(8)]
        mmB = [B("mm%d" % j) for j in range(8)]
        yinB = [B("yin%d" % j) for j in range(8)]
        rtmpB = [B("rtmp0"), B("rtmp1")]
        muB, musqB, varB, sd2B, rs2B = B("mu"), B("musq"), B("var"), B("sd2"), B("rs2")
        t1B = [B("t1_0"), B("t1_1")]
        arenaB = gateB + xcB + xcbB + rrB + iiB + aaB + mmB + yinB + rtmpB + [muB, musqB, varB, sd2B, rs2B] + t1B
        dnbB = yinB
        qTB = [B("qT%d" % j) for j in range(16)]
        ssbB = [B("ssb%d" % j) for j in range(4)]
        s2B = [B("s2_%d" % j) for j in range(4)]
        v1B = [B("v1_%d" % j) for j in range(16)]
        ixB = [B("ix_%d" % j) for j in range(16)]
        ixfB, candB, cvB, ciB, cabB, cabfB, geB, gsB, gs2B, idxgB = B("ixf"), [ssbB[h // 2] for h in range(8)], [B("cv%d" % h) for h in range(8)], [B("ci%d" % h) for h in range(8)], B("cab"), B("cabf"), B("ge"), B("gs"), B("gs2"), B("idxg")
        cand2B = [s2B[h // 2] for h in range(8)]
        idxTB = [B("idxT0"), B("idxT1")]
        ohB = [[B("oh%d_e" % s), B("oh%d_a" % s)] for s in range(3)]
        WsbB = B("Wsb")
        gzB = [B("gz%d" % i) for i in range(3)]
        wzB = [B("wz%d" % i) for i in range(3)]
        adasB = [B("adas0"), B("adas1")]
        wdbB, ubB, vbB = B("wd_b"), B("u_b"), B("v_b")
        store_toks = []

        def V(name, j=0):
            o = VOFF[name] + j
            return vec[:, o:o + 1]

        K.dma(SP, "ld_misc", cst[:], consts[:, :, :], writes=[cstB])
        K.dma(SP, "ld_misc", vec[:], vec_d[:, :], writes=[vecB])
        K.dma(SP, "ld_misc", adab[:], adab_d[:, :, :], writes=[adabB])
        K.dma(SP, "ld_misc", cTs[:], cT[:, :, :], writes=[cTsB])
        K.dma(SP, "ld_misc", gws[:], gw_d[:, :], writes=[gwsB])
        tk_misc = K.dma(SP, "ld_misc", kTs[:], kT_d[:, :], writes=[kTsB])
        for bb in (cstB, vecB, adabB, cTsB, gwsB, kTsB):
            bb.w = tk_misc
        R = 2048
        nrow = wd_d.shape[0]
        r0 = 0
        while r0 < nrow:
            n = min(R, nrow - r0)
            K.dma(POOL, "cast_w", wd_b[r0:r0 + n, :], wd_d[r0:r0 + n, :], writes=[wdbB])
            r0 += n
        for (src, dst, bb, ck) in ((u_d, u_b, ubB, "cast_u"), (v_d, v_b, vbB, "cast_v")):
            for r0 in range(0, src.shape[0], R):
                K.dma(POOL, ck, dst[r0:r0 + R, :], src[r0:r0 + R, :], writes=[bb])
        K.op(DVE, "tensor_copy", dict(out=iota_b[:], in_=iota_f), reads=[cstB], writes=[iotabB])
        K.op(DVE, "tensor_copy", dict(out=gwb[:].rearrange("p a h j -> p (a h j)"), in_=gws[:]), reads=[gwsB], writes=[gwbB])
        K.op(DVE, "tensor_copy", dict(out=kTb[:].rearrange("p a k -> p (a k)"), in_=kTs[:]), reads=[kTsB], writes=[kTbB])
        K.op(ACT, "activation", dict(out=csil[:], in_=cTs[:], func=AF.Silu), reads=[cTsB], writes=[csilB])
        lam = vec[:, VOFF["lam"]:VOFF["lam"] + 8]
        K.op(DVE, "tensor_scalar", dict(out=ptmp[:, 3, :], in0=lam, scalar1=-1.0, scalar2=None, op0=ALU.mult), reads=[vecB], writes=[ptmpB])
        K.op(DVE, "tensor_tensor", dict(out=ptmp[:, 0, :], in0=lam, in1=ptmp[:, 3, :], op=ALU.max), reads=[vecB, ptmpB], writes=[ptmpB])
        K.op(ACT, "activation", dict(out=ptmp[:, 1, :], in_=ptmp[:, 0, :], func=AF.Exp, scale=-1.0), reads=[ptmpB], writes=[ptmpB])
        K.op(ACT, "activation", dict(out=ptmp[:, 2, :], in_=ptmp[:, 1, :], func=AF.Ln, bias=ONE), reads=[ptmpB], writes=[ptmpB])
        K.op(DVE, "tensor_scalar", dict(out=ptmp[:, 3, :], in0=lam, scalar1=-1.0, scalar2=0.0, op0=ALU.mult, op1=ALU.max), reads=[vecB, ptmpB], writes=[ptmpB])
        K.op(DVE, "tensor_tensor", dict(out=ptmp[:, 3, :], in0=ptmp[:, 3, :], in1=ptmp[:, 2, :], op=ALU.add), reads=[ptmpB], writes=[ptmpB])
        K.op(DVE, "tensor_scalar", dict(out=cA[:], in0=ptmp[:, 3, :], scalar1=-8.0, scalar2=None, op0=ALU.mult), reads=[ptmpB], writes=[cAB])
        K.op(DVE, "tensor_scalar", dict(out=c2A[:], in0=ptmp[:, 3, :], scalar1=-16.0, scalar2=None, op0=ALU.mult), reads=[ptmpB], writes=[cAB])
        for l in range(2):
            pb = nbank()
            for pc in range(12):
                sl = pc % 2
                K.dma(SP, "ld_a%d" % sl, adas[sl][:].rearrange("p k o -> p (k o)"), adaw_d[l, pc, :, :], writes=[adasB[sl]])
                for o4 in range(4):
                    oc = pc * 4 + o4
                    insts = [("matmul", dict(out=PS[pb][:, oc * 6:(oc + 1) * 6], lhsT=adas[sl][:, k, o4 * 128:(o4 + 1) * 128],
                                             rhs=csil[:, k, :], start=(k == 0), stop=(k == KC - 1))) for k in range(KC)]
                    K.multi(PE, insts, reads=[adasB[sl], csilB], writes=[PSB[pb]])
            K.op(DVE, "tensor_tensor", dict(out=modv[:, l, :, :], in0=PS[pb][:, 0:288].rearrange("p (o s) -> p o s", s=6),
                                            in1=cap(adab, l * 48, [[1, 48], [0, 6]]), op=ALU.add),
                 reads=[PSB[pb], adabB], writes=[modvB])
            nmn = "nm%d" % l
            nfn = "nf%d" % l
            bon = "bout" if l == 0 else "bpw2"
            K.op(DVE, "scalar_tensor_tensor", dict(out=G1[:, l, :, :], in0=modv[:, l, 8:16, :], scalar=1.0,
                                                   in1=cap(vec, VOFF[nmn], [[1, 8], [0, 6]]), op0=ALU.add, op1=ALU.mult),
                 reads=[modvB, vecB], writes=[GB_])
            K.op(DVE, "scalar_tensor_tensor", dict(out=G2[:, l, :, :], in0=modv[:, l, 32:40, :], scalar=1.0,
                                                   in1=cap(vec, VOFF[nfn], [[1, 8], [0, 6]]), op0=ALU.add, op1=ALU.mult),
                 reads=[modvB, vecB], writes=[GB_])
            K.op(DVE, "tensor_tensor", dict(out=GB1[:, l, :, :], in0=modv[:, l, 16:24, :],
                                            in1=cap(vec, VOFF[bon], [[1, 8], [0, 6]]), op=ALU.mult),
                 reads=[modvB, vecB], writes=[GB_])

        def MOD(l, oc, s):
            return modv[:, l, oc, s:s + 1]

        def load_piece(pc):
            sl = nrot("w", 5)
            src = dap(wd_b, pc * 128 * KC * 256, [[KC * 256, 128], [1, KC * 256]])
            K.dma(SP, "ld_w%d" % sl, wslot[sl][:].rearrange("p k o -> p (k o)"), src, reads=[wdbB], writes=[wslotB[sl]])
            return sl

        def stats(T_):
            for k in range(KC):
                K.op(ACT, "activation", dict(out=sq[:, k, :T_], in_=x[:, k, :T_], func=AF.Square), reads=[xB[k]], writes=[sqB[k], qTB[2 * k], qTB[2 * k + 1]])
            pb = nbank()
            insts = [("matmul", dict(out=PS[pb][:, :T_], lhsT=ones, rhs=sq[:, k, :T_], start=(k == 0), stop=(k == KC - 1))) for k in range(KC)]
            K.multi(PE, insts, reads=sqB + [cstB], writes=[PSB[pb]])
            K.op(ACT, "activation", dict(out=sdt[:, :T_], in_=PS[pb][:, :T_], func=AF.Sqrt, bias=EPSA, scale=1.0 / D), reads=[PSB[pb]], writes=[sdtB])
            K.op(DVE, "reciprocal", dict(out=rstd[:, :T_], in_=sdt[:, :T_]), reads=[sdtB], writes=[rstdB])

        def modnorm(T_, Gt, l, shoc, s):
            for k in range(KC):
                i = nrot("tmpk", 2)
                K.op(DVE, "scalar_tensor_tensor", dict(out=tmpk[i][:, :T_], in0=x[:, k, :T_], scalar=Gt[:, l, k, s:s + 1], in1=rstd[:, :T_],
                                                       op0=ALU.mult, op1=ALU.mult), reads=[xB[k], GB_, rstdB], writes=[tmpkB[i]])
                K.op(ACT, "activation", dict(out=hT[:, k, :T_], in_=tmpk[i][:, :T_], func=AF.Identity, bias=MOD(l, shoc + k, s)),
                     reads=[tmpkB[i], modvB], writes=[hTB[k]])

        def proj(T_, pc, oo, rhs_t, rhsB):
            sl = proj.cur
            pb = nbank()
            insts = [("matmul", dict(out=PS[pb][:, :T_], lhsT=wslot[sl][:, k, oo * 128:(oo + 1) * 128], rhs=rhs_t[:, k, :T_],
                                     start=(k == 0), stop=(k == KC - 1))) for k in range(KC)]
            K.multi(PE, insts, reads=[wslotB[sl]] + rhsB, writes=[PSB[pb]])
            return pb

        def resid_add(T_, pb, l, dc, s):
            i = nrot("rtmp", 2)
            K.op(ACT, "activation", dict(out=rtmp[i][:, :T_], in_=PS[pb][:, :T_], func=AF.Identity, bias=GB1[:, l, dc, s:s + 1], scale=MOD(l, 16 + dc, s)),
                 reads=[PSB[pb], GB_, modvB], writes=[rtmpB[i]])
            K.op(POOL, "tensor_tensor", dict(out=x[:, dc, :T_], in0=x[:, dc, :T_], in1=rtmp[i][:, :T_], op=ALU.add), reads=[rtmpB[i], xB[dc]], writes=[xB[dc]])

        def fence_bufs(engs, bufs):
            toks = []
            for b in bufs:
                if b.w is not None:
                    toks.append(b.w)
                toks += [Tok(sem, val) for sem, val in b.r.values()]
            for E in engs:
                K.fence(E, toks)

        def lru_stage(T_, s, first, last):
            fence_bufs([ACT, DVE, POOL], [WsbB])
            if first:
                if s < NPS:
                    for j in range(KC):
                        K.op(POOL, "memset", dict(ap=xp[:, j, 0:3], constant=0.0), writes=[xpB[j]])
                        K.op(POOL, "memset", dict(ap=hstate[:, j:j + 1], constant=0.0), writes=[hstB[j]])
                else:
                    K.dma(SP, "ld_st", xp[:, :, 0:3], st_conv[:, :, s - NPS, :], writes=xpB)
                    tk_st = K.dma(SP, "ld_st", hstate[:], st_h[:, s - NPS, :], writes=hstB)
                    for bb in xpB:
                        bb.w = tk_st
            stats(T_)
            modnorm(T_, G1, 0, 0, s)
            for pc in range(8):
                proj.cur = load_piece(PIECE["w_in"] + pc)
                for oo in range(2):
                    oc = pc * 2 + oo
                    pb = proj(T_, pc, oo, hT, hTB)
                    if oc < 8:
                        K.op(ACT, "activation", dict(out=gate[:, oc, :T_], in_=PS[pb][:, :T_], func=AF.Gelu_apprx_tanh, bias=V("b_in", oc)),
                             reads=[PSB[pb], vecB], writes=[gateB[oc]])
                    else:
                        j = oc - 8
                        K.op(ACT, "activation", dict(out=xp[:, j, 3:3 + T_], in_=PS[pb][:, :T_], func=AF.Identity, bias=V("b_in", oc)),
                             reads=[PSB[pb], vecB], writes=[xpB[j]])
            for j in range(KC):
                K.op(DVE, "tensor_scalar", dict(out=xc[:, j, :T_], in0=xp[:, j, 0:T_], scalar1=V("cw0", j), scalar2=V("cb", j), op0=ALU.mult, op1=ALU.add),
                     reads=[xpB[j], vecB], writes=[xcB[j]])
            for k in range(1, 4):
                for j in range(KC):
                    K.op(DVE, "scalar_tensor_tensor", dict(out=xc[:, j, :T_], in0=xp[:, j, k:k + T_], scalar=V("cw%d" % k, j), in1=xc[:, j, :T_],
                                                           op0=ALU.mult, op1=ALU.add), reads=[xpB[j], vecB, xcB[j]], writes=[xcB[j]])
            for j in range(KC):
                K.op(POOL, "tensor_copy", dict(out=xcb[:, j, :T_], in_=xc[:, j, :T_]), reads=[xcB[j]], writes=[xcbB[j]])
                if last:
                    pass
            if last:
                store_toks.append(K.dma(SP, "st_c", convo[:, :, s, :], xp[:, :, T_:T_ + 3], reads=xpB))
            for j in range(KC):
                K.op(POOL, "tensor_copy", dict(out=xp[:, j, 0:3], in_=xp[:, j, T_:T_ + 3]), reads=[xpB[j]], writes=[xpB[j]])
            for j in range(KC):
                for a_, dst, dstB, bn in ((0, rr, rrB, "gab"), (1, ii, iiB, "gxb")):
                    pb = nbank()
                    K.multi(PE, [("matmul", dict(out=PS[pb][:, :T_], lhsT=gwb[:, a_, j, :], rhs=xcb[:, j, :T_], start=True, stop=True))],
                            reads=[gwbB, xcbB[j]], writes=[PSB[pb]])
                    K.op(ACT, "activation", dict(out=dst[:, j, :T_], in_=PS[pb][:, :T_], func=AF.Sigmoid, bias=V(bn, j)),
                         reads=[PSB[pb], vecB], writes=[dstB[j], yB])
            for j in range(KC):
                K.op(ACT, "activation", dict(out=aa[:, j, :T_], in_=rr[:, j, :T_], func=AF.Exp, scale=cA[:, j:j + 1]), reads=[rrB[j], cAB], writes=[aaB[j]])
                K.op(ACT, "activation", dict(out=mm_[:, j, :T_], in_=rr[:, j, :T_], func=AF.Exp, scale=c2A[:, j:j + 1]), reads=[rrB[j], cAB], writes=[mmB[j]])
            for j in range(KC):
                K.op(ACT, "activation", dict(out=mm_[:, j, :T_], in_=mm_[:, j, :T_], func=AF.Sqrt, bias=ONE, scale=-1.0), reads=[mmB[j]], writes=[mmB[j]])
            for j in range(KC):
                K.op(DVE, "tensor_tensor", dict(out=mm_[:, j, :T_], in0=mm_[:, j, :T_], in1=ii[:, j, :T_], op=ALU.mult), reads=[mmB[j], iiB[j]], writes=[mmB[j]])
            for j in range(KC):
                K.op(DVE, "tensor_tensor", dict(out=mm_[:, j, :T_], in0=mm_[:, j, :T_], in1=xc[:, j, :T_], op=ALU.mult), reads=[mmB[j], xcB[j]], writes=[mmB[j]])
            for j in range(KC):
                K.op(DVE, "tensor_tensor_scan", dict(out=rr[:, j, :T_], data0=aa[:, j, :T_], data1=mm_[:, j, :T_], initial=hstate[:, j:j + 1],
                                                     op0=ALU.mult, op1=ALU.add), reads=[aaB[j], mmB[j], hstB[j]], writes=[rrB[j]])
            for j in range(KC):
                K.op(POOL, "tensor_copy", dict(out=hstate[:, j:j + 1], in_=rr[:, j, T_ - 1:T_]), reads=[rrB[j]], writes=[hstB[j]])
                K.op(DVE, "tensor_tensor", dict(out=yin[:, j, :T_], in0=rr[:, j, :T_], in1=gate[:, j, :T_], op=ALU.mult), reads=[rrB[j], gateB[j]], writes=[yinB[j]])
            if last:
                store_toks.append(K.dma(SP, "st_h", ho[:, s, :], hstate[:], reads=hstB))
            for pc in range(4):
                proj.cur = load_piece(PIECE["w_out"] + pc)
                for oo in range(2):
                    dc = pc * 2 + oo
                    pb = proj(T_, pc, oo, yin, yinB)
                    resid_add(T_, pb, 0, dc, s)

        def conf_stage(T_, s, first, last):
            fence_bufs([ACT, DVE, POOL], [WsbB])
            if first:
                if s < NPS:
                    for j in range(KC):
                        K.op(POOL, "memset", dict(ap=xpc[:, j, 0:30], constant=0.0), writes=[xpcB[j]])
                else:
                    K.dma(SP, "ld_st", xpc[:, :, 0:30], st_dw[:, :, s - NPS, :], writes=xpcB)
            stats(T_)
            modnorm(T_, G1, 1, 0, s)
            for pc in (4, 5, 6, 7, 0, 1, 2, 3):
                proj.cur = load_piece(PIECE["pw1"] + pc)
                for oo in range(2):
                    oc = pc * 2 + oo
                    pb = proj(T_, pc, oo, hT, hTB)
                    if oc >= 8:
                        j = oc - 8
                        K.op(ACT, "activation", dict(out=sgm[:, j, :T_], in_=PS[pb][:, :T_], func=AF.Sigmoid, bias=V("bpw1", oc)),
                             reads=[PSB[pb], vecB], writes=[rrB[j]])
                    else:
                        j = oc
                        K.op(DVE, "scalar_tensor_tensor", dict(out=xpc[:, j, 30:30 + T_], in0=PS[pb][:, :T_], scalar=V("bpw1", j), in1=sgm[:, j, :T_],
                                                               op0=ALU.add, op1=ALU.mult), reads=[PSB[pb], vecB, rrB[j]], writes=[xpcB[j]])
            for j in range(KC):
                K.op(DVE, "tensor_scalar", dict(out=dd[:, j, :T_], in0=xpc[:, j, 0:T_], scalar1=V("dw0", j), scalar2=V("dwb", j), op0=ALU.mult, op1=ALU.add),
                     reads=[xpcB[j], vecB], writes=[xcB[j]])
            for k in range(1, 31):
                for j in range(KC):
                    K.op(DVE, "scalar_tensor_tensor", dict(out=dd[:, j, :T_], in0=xpc[:, j, k:k + T_], scalar=V("dw%d" % k, j), in1=dd[:, j, :T_],
                                                           op0=ALU.mult, op1=ALU.add), reads=[xpcB[j], vecB, xcB[j]], writes=[xcB[j]])
            if last:
                store_toks.append(K.dma(SP, "st_d", dwo[:, :, s, :], xpc[:, :, T_:T_ + 30], reads=xpcB))
            for j in range(KC):
                K.op(POOL, "tensor_copy", dict(out=xpc[:, j, 0:30], in_=xpc[:, j, T_:T_ + 30]), reads=[xpcB[j]], writes=[xpcB[j]])
            for j in range(KC):
                K.op(ACT, "activation", dict(out=sq[:, j, :T_], in_=dd[:, j, :T_], func=AF.Square), reads=[xcB[j]], writes=[sqB[j], qTB[2 * j], qTB[2 * j + 1]])
            pa = nbank()
            K.multi(PE, [("matmul", dict(out=PS[pa][:, :T_], lhsT=ones, rhs=dd[:, j, :T_], start=(j == 0), stop=(j == KC - 1))) for j in range(KC)],
                    reads=xcB + [cstB], writes=[PSB[pa]])
            pb2 = nbank()
            K.multi(PE, [("matmul", dict(out=PS[pb2][:, :T_], lhsT=ones, rhs=sq[:, j, :T_], start=(j == 0), stop=(j == KC - 1))) for j in range(KC)],
                    reads=sqB + [cstB], writes=[PSB[pb2]])
            K.op(ACT, "activation", dict(out=mu[:, :T_], in_=PS[pa][:, :T_], func=AF.Identity, scale=1.0 / D), reads=[PSB[pa]], writes=[muB])
            K.op(DVE, "tensor_tensor", dict(out=musq[:, :T_], in0=mu[:, :T_], in1=mu[:, :T_], op=ALU.mult), reads=[muB], writes=[musqB])
            K.op(DVE, "scalar_tensor_tensor", dict(out=var[:, :T_], in0=PS[pb2][:, :T_], scalar=1.0 / D, in1=musq[:, :T_], op0=ALU.mult, op1=ALU.subtract),
                 reads=[PSB[pb2], musqB], writes=[varB])
            K.op(ACT, "activation", dict(out=sd2[:, :T_], in_=var[:, :T_], func=AF.Sqrt, bias=EPSA), reads=[varB], writes=[sd2B])
            K.op(DVE, "reciprocal", dict(out=rs2[:, :T_], in_=sd2[:, :T_]), reads=[sd2B], writes=[rs2B])
            for j in range(KC):
                i = nrot("t1", 2)
                K.op(POOL, "tensor_tensor", dict(out=t1[i][:, :T_], in0=dd[:, j, :T_], in1=mu[:, :T_], op=ALU.subtract), reads=[xcB[j], muB], writes=[t1B[i]])
                K.op(DVE, "tensor_tensor", dict(out=t1[i][:, :T_], in0=t1[i][:, :T_], in1=rs2[:, :T_], op=ALU.mult), reads=[t1B[i], rs2B], writes=[t1B[i]])
                K.op(ACT, "activation", dict(out=dnb[:, j, :T_], in_=t1[i][:, :T_], func=AF.Silu, bias=V("lnb", j), scale=V("lng", j)),
                     reads=[t1B[i], vecB], writes=[dnbB[j]])
            for pc in range(4):
                proj.cur = load_piece(PIECE["pw2"] + pc)
                for oo in range(2):
                    dc = pc * 2 + oo
                    pb = proj(T_, pc, oo, dnb, dnbB)
                    resid_add(T_, pb, 1, dc, s)

        def peer_stage(T_, s, l):
            stats(T_)
            modnorm(T_, G2, l, 24, s)
            for pc in range(8):
                proj.cur = load_piece(PIECE["wq%d" % l] + pc)
                for oo in range(2):
                    qc = pc * 2 + oo
                    pb = proj(T_, pc, oo, hT, hTB)
                    K.op(ACT, "activation", dict(out=qT[:, qc, :T_], in_=PS[pb][:, :T_], func=AF.Identity), reads=[PSB[pb]], writes=[qTB[qc], sqB[qc // 2]])
            ngr = (T_ + 127) // 128

            def topk_gen(g):
                g0 = g * 128
                gt = min(128, T_ - g0)
                yield
                for i4 in range(4):
                    pb = nbank()
                    insts = []
                    for i in range(4):
                        qc = i4 * 4 + i
                        insts.append(("matmul", dict(out=PS[pb][:gt, i * 128:(i + 1) * 128], lhsT=qT[:, qc, g0:g0 + gt], rhs=kTb[:, l * 2 + (qc % 2), :],
                                                     start=True, stop=True)))
                    K.multi(PE, insts, reads=[qTB[i4 * 4 + i] for i in range(4)] + [kTbB], writes=[PSB[pb]])
                    K.op(ACT, "activation", dict(out=s_sb[:gt, i4 * 4:(i4 + 1) * 4, :], in_=PS[pb][:gt, :].rearrange("p (a b) -> p a b", b=128), func=AF.Identity),
                         reads=[PSB[pb]], writes=[ssbB[i4]])
                yield
                if EMBED:
                    K.op(DVE, "scalar_tensor_tensor", dict(out=s_sbu[:gt], in0=s_sbu[:gt], scalar=cstu[:gt, 3, 3:4], in1=cap(cstu, 4 * 128, [[0, 16], [1, 128]], gt),
                                                           op0=ALU.bitwise_and, op1=ALU.bitwise_or), reads=ssbB + [cstB], writes=ssbB)
                    yield
                    for qc in range(16):
                        K.op(DVE, "max", dict(out=v1[:gt, qc, 0:8], in_=s_sb[:gt, qc, :]), reads=[ssbB[qc // 4]], writes=[v1B[qc]])
                    yield
                    for qc in range(16):
                        K.op(DVE, "match_replace", dict(out=s2[:gt, qc, :], in_to_replace=v1[:gt, qc, 0:8], in_values=s_sb[:gt, qc, :], imm_value=NEG),
                             reads=[ssbB[qc // 4], v1B[qc]], writes=[s2B[qc // 4]])
                    yield
                    for qc in range(16):
                        K.op(DVE, "max", dict(out=v1[:gt, qc, 8:16], in_=s2[:gt, qc, :]), reads=[s2B[qc // 4]], writes=[v1B[qc]])
                    yield
                    K.op(DVE, "tensor_scalar", dict(out=ix[:gt], in0=v1u[:gt], scalar1=127, scalar2=None, op0=ALU.bitwise_and), reads=v1B, writes=ixB)
                else:
                    for qc in range(16):
                        K.op(DVE, "max", dict(out=v1[:gt, qc, 0:8], in_=s_sb[:gt, qc, :]), reads=[ssbB[qc // 4]], writes=[v1B[qc]])
                    yield
                    for qc in range(16):
                        K.op(DVE, "max_index", dict(out=ix[:gt, qc, 0:8], in_max=v1[:gt, qc, 0:8], in_values=s_sb[:gt, qc, :]), reads=[ssbB[qc // 4], v1B[qc]], writes=[ixB[qc]])
                    yield
                    for qc in range(16):
                        K.op(DVE, "match_replace", dict(out=s2[:gt, qc, :], in_to_replace=v1[:gt, qc, 0:8], in_values=s_sb[:gt, qc, :], imm_value=NEG),
                             reads=[ssbB[qc // 4], v1B[qc]], writes=[s2B[qc // 4]])
                    yield
                    for qc in range(16):
                        K.op(DVE, "max", dict(out=v1[:gt, qc, 8:16], in_=s2[:gt, qc, :]), reads=[s2B[qc // 4]], writes=[v1B[qc]])
                    yield
                    for qc in range(16):
                        K.op(DVE, "max_index", dict(out=ix[:gt, qc, 8:16], in_max=v1[:gt, qc, 8:16], in_values=s2[:gt, qc, :]), reads=[s2B[qc // 4], v1B[qc]], writes=[ixB[qc]])
                K.op(DVE, "tensor_copy", dict(out=ixf[:gt], in_=ix[:gt]), reads=ixB, writes=[ixfB])
                K.op(DVE, "tensor_tensor", dict(out=cand[:gt].rearrange("p h (a b) -> p h a b", b=16),
                                                in0=cap(v1, 0, [[32, 8], [1, 16], [0, 16]], gt), in1=cap(v1, 16, [[32, 8], [0, 16], [1, 16]], gt), op=ALU.add),
                     reads=v1B, writes=candB)
                yield
                for h in range(8):
                    K.op(DVE, "max", dict(out=cv[:gt, h, 0:8], in_=cand[:gt, h, :]), reads=[candB[h]], writes=[cvB[h]])
                yield
                for h in range(8):
                    K.op(DVE, "max_index", dict(out=ci[:gt, h, 0:8], in_max=cv[:gt, h, 0:8], in_values=cand[:gt, h, :]), reads=[candB[h], cvB[h]], writes=[ciB[h]])
                yield
                for h in range(8):
                    K.op(DVE, "match_replace", dict(out=cand2[:gt, h, :], in_to_replace=cv[:gt, h, 0:8], in_values=cand[:gt, h, :], imm_value=NEG),
                         reads=[candB[h], cvB[h]], writes=[cand2B[h]])
                yield
                for h in range(8):
                    K.op(DVE, "max", dict(out=cv[:gt, h, 8:16], in_=cand2[:gt, h, :]), reads=[cand2B[h]], writes=[cvB[h]])
                yield
                for h in range(8):
                    K.op(DVE, "max_index", dict(out=ci[:gt, h, 8:16], in_max=cv[:gt, h, 8:16], in_values=cand2[:gt, h, :]), reads=[cand2B[h], cvB[h]], writes=[ciB[h]])
                K.op(DVE, "tensor_tensor", dict(out=ge[:gt], in0=cv[:gt], in1=cap(cv, 0, [[16, 8], [0, 16]], gt), op=ALU.subtract), reads=cvB, writes=[geB])
                K.op(ACT, "activation", dict(out=ge[:gt], in_=ge[:gt], func=AF.Exp), reads=[geB], writes=[geB])
                K.op(DVE, "tensor_reduce", dict(out=gs[:gt], in_=ge[:gt], axis=AX.X, op=ALU.add), reads=[geB], writes=[gsB])
                K.op(DVE, "reciprocal", dict(out=gs2[:gt], in_=gs[:gt]), reads=[gsB], writes=[gs2B])
                K.op(DVE, "tensor_tensor", dict(out=idxg[:gt, 2, :].rearrange("p (h k) -> p h k", k=16), in0=ge[:gt], in1=cap(gs2, 0, [[1, 8], [0, 16]], gt), op=ALU.mult),
                     reads=[geB, gs2B], writes=[idxgB])
                K.op(DVE, "tensor_scalar", dict(out=cab[:gt, 0, :], in0=ci[:gt].rearrange("p h k -> p (h k)"), scalar1=4, scalar2=None, op0=ALU.logical_shift_right), reads=ciB, writes=[cabB])
                K.op(DVE, "tensor_scalar", dict(out=cab[:gt, 1, :], in0=ci[:gt].rearrange("p h k -> p (h k)"), scalar1=15, scalar2=None, op0=ALU.bitwise_and), reads=ciB, writes=[cabB])
                K.op(DVE, "tensor_copy", dict(out=cabf[:gt], in_=cab[:gt]), reads=[cabB], writes=[cabfB])
                yield
                for hf in range(2):
                    K.op(DVE, "tensor_tensor", dict(out=eqt[:gt], in0=cap(cabf, hf * 128, [[16, 8], [1, 16], [0, 16]], gt),
                                                    in1=cap(cst, 2 * 128, [[0, 8], [0, 16], [1, 16]], gt), op=ALU.is_equal),
                         reads=[cabfB, cstB], writes=ssbB)
                    K.op(DVE, "tensor_tensor", dict(out=prod[:gt], in0=eqt[:gt], in1=cap(ixf, hf * 16, [[32, 8], [0, 16], [1, 16]], gt), op=ALU.mult),
                         reads=ssbB + [ixfB], writes=s2B)
                    K.op(DVE, "tensor_reduce", dict(out=idxg[:gt, hf, :], in_=prod[:gt].rearrange("p h a b -> p (h a) b"), axis=AX.X, op=ALU.add),
                         reads=s2B, writes=[idxgB])
                pb = nbank()
                insts = [("transpose", dict(out=PS[pb][:, i * 128:i * 128 + gt], in_=idxg[:gt, i, :], identity=cst[:gt, 0, :gt])) for i in range(3)]
                K.multi(PE, insts, reads=[idxgB, cstB], writes=[PSB[pb]])
                K.op(ACT, "activation", dict(out=idxT[:, :, g0:g0 + gt], in_=PS[pb][:, 0:384].rearrange("p (a b) -> p a b", b=128)[:, :, :gt], func=AF.Identity),
                     reads=[PSB[pb]], writes=[idxTB[g]])
                yield

            def wbuild_gen(g):
                g0 = g * 128
                gt = min(128, T_ - g0)
                if g == 0:
                    fence_bufs([ACT, DVE], arenaB + [yB])
                io = cap(iota_b, 0, [[0, SG], [1, 128]])

                def build(t0):
                    st = nrot("oh", 3)
                    eqab, Aoh = oh[st]
                    eqabB, AohB = ohB[st]
                    K.op(DVE, "tensor_tensor", dict(out=eqab[:], in0=cap(iota_b, 0, [[0, SG], [0, 2], [1, 128]]),
                                                    in1=cap(idxT, t0, [[1, SG], [T, 2], [0, 128]]), op=ALU.is_equal),
                         reads=[iotabB, idxTB[g]], writes=[eqabB])
                    K.op(POOL, "tensor_tensor", dict(out=Aoh[:], in0=eqab[:, :, 0, :], in1=cap(idxT, 2 * T + t0, [[1, SG], [0, 128]]), op=ALU.mult),
                         reads=[eqabB, idxTB[g]], writes=[AohB])
                    return st

                def mmev(t0, st):
                    eqab, Aoh = oh[st]
                    eqabB, AohB = ohB[st]
                    for q4 in range(SG // 4):
                        pb = nbank()
                        insts = [("matmul", dict(out=PS[pb][:, i * 128:(i + 1) * 128], lhsT=Aoh[:, q4 * 4 + i, :], rhs=eqab[:, q4 * 4 + i, 1, :], start=True, stop=True))
                                 for i in range(4)]
                        K.multi(PE, insts, reads=[AohB, eqabB], writes=[PSB[pb]])
                        tt = t0 + q4 * 4
                        dst = Wsb[:, tt:tt + 4, :]
                        src = PS[pb][:, :].rearrange("p (a b) -> p a b", b=128)
                        K.op(ACT, "activation", dict(out=dst, in_=src, func=AF.Identity), reads=[PSB[pb]], writes=[WsbB])

                prev = None
                for t0 in range(g0, g0 + gt, SG):
                    st = build(t0)
                    if prev is not None:
                        mmev(*prev)
                    prev = (t0, st)
                    yield
                mmev(*prev)
                yield

            for _ in topk_gen(0):
                pass
            for g in range(ngr):
                gens = [wbuild_gen(g)]
                if g + 1 < ngr:
                    gens.append(topk_gen(g + 1))
                while gens:
                    for gg in list(gens):
                        try:
                            next(gg)
                        except StopIteration:
                            gens.remove(gg)
            nb = min(8, 512 // T_)

            def OUT(dc):
                return PS[4 + dc // nb][:, (dc % nb) * T_:(dc % nb + 1) * T_]

            outB = PSB[4:8]

            def load_u(c, Q):
                st = c % NSET
                off = (l * 128 + c) * 128 * 1024
                K.dma(Q, "ld_u%d" % st, ubuf[st][:], dap(u_b, off, [[1024, 128], [1, 1024]]), reads=[ubB], writes=[ubufB[st]])

            def load_v(c):
                st = c % NSET
                off = (l * 128 + c) * 128 * 1024
                K.dma(SP, "ld_v%d" % st, vbuf[st][:], dap(v_b, off, [[1024, 128], [1, 1024]]), reads=[vbB], writes=[vbufB[st]])

            def zmm(c):
                st = c % NSET
                pb = nbank()
                insts = [("matmul", dict(out=PS[pb][:, :T_], lhsT=ubuf[st][:, k * 128:(k + 1) * 128], rhs=hT[:, k, :T_], start=(k == 0), stop=(k == KC - 1)))
                         for k in range(KC)]
                K.multi(PE, insts, reads=[ubufB[st]] + hTB, writes=[PSB[pb]])
                return pb

            for c in range(NSET):
                load_u(c, SP)
                load_v(c)
            pend = {0: zmm(0), 1: zmm(1)}
            for c in range(128):
                pb = pend.pop(c)
                i = nrot("gz", 3)
                K.op(ACT, "activation", dict(out=gz[i][:, :T_], in_=PS[pb][:, :T_], func=AF.Gelu_apprx_tanh), reads=[PSB[pb]], writes=[gzB[i]])
                if c + NSET < 128:
                    load_u(c + NSET, ACT)
                K.op(DVE, "tensor_tensor", dict(out=wz[i][:, :T_], in0=gz[i][:, :T_], in1=Wsb[:, 0:T_, c], op=ALU.mult), reads=[gzB[i], WsbB], writes=[wzB[i]])
                if c + 2 < 128:
                    pend[c + 2] = zmm(c + 2)
                st = c % NSET
                insts = [("matmul", dict(out=OUT(dc), lhsT=vbuf[st][:, dc * 128:(dc + 1) * 128], rhs=wz[i][:, :T_],
                                         start=(c == 0 and dc % nb == 0), stop=(c == 127))) for dc in range(KC)]
                K.multi(PE, insts, reads=[vbufB[st], wzB[i]], writes=outB)
                if c + NSET < 128:
                    load_v(c + NSET)
            for dc in range(KC):
                K.op(DVE, "scalar_tensor_tensor", dict(out=x[:, dc, :T_], in0=OUT(dc), scalar=MOD(l, 40 + dc, s), in1=x[:, dc, :T_], op0=ALU.mult, op1=ALU.add),
                     reads=outB + [modvB, xB[dc]], writes=[xB[dc]])

        tiles = []
        for s in range(NPS if not ONLY_SAMPLE else min(1, DBG_TILES)):
            nt = SEQ // T
            for j in range(nt if not ONLY_SAMPLE else DBG_TILES):
                tiles.append((s, s * SEQ + j * T, T, j == 0, j == nt - 1))
        for s in range(NSS):
            tiles.append((NPS + s, NPS * SEQ + s * DSEQ, DSEQ, True, True))
        for (s, tok0, T_, first, last) in tiles:
            K.dma(SP, "ld_x", x[:, :, :T_], xin[:, :, tok0:tok0 + T_], writes=xB)
            lru_stage(T_, s, first, last)
            peer_stage(T_, s, 0)
            conf_stage(T_, s, first, last)
            peer_stage(T_, s, 1)
            stats(T_)
            for k in range(KC):
                K.op(DVE, "scalar_tensor_tensor", dict(out=ybuf[:, k, :T_], in0=x[:, k, :T_], scalar=V("nfin", k), in1=rstd[:, :T_], op0=ALU.mult, op1=ALU.mult),
                     reads=[xB[k], vecB, rstdB], writes=[yB] + iiB)
            store_toks.append(K.dma(SP, "st_y", yout[:, :, tok0:tok0 + T_], ybuf[:, :, :T_], reads=[yB]))
        last_tok = {}
        for tk in store_toks:
            last_tok[id(tk.sem)] = tk
        K.fence(SP, list(last_tok.values()))

        with nc.Block() as block:
            @block.tensor
            def _(e):
                K.replay(PE, e)

            @block.scalar
            def _(e):
                K.replay(ACT, e)

            @block.vector
            def _(e):
                K.replay(DVE, e)

            @block.gpsimd
            def _(e):
                K.replay(POOL, e)

            @block.sync
            def _(e):
                K.replay(SP, e)
    return nc


_PROG = {}


def _fm(a):
    a = np.asarray(a, dtype=np.float32)
    lead = a.shape[:-1]
    a = a.reshape(lead + (KC, 128))
    nd = a.ndim
    perm = (nd - 1, nd - 2) + tuple(range(nd - 2))
    return np.ascontiguousarray(a.transpose(perm))


def kernel(**inp):
    f = lambda k: np.asarray(inp[k], dtype=np.float32)
    vec = np.zeros((128, NV), np.float32)

    def put(name, v):
        v = np.asarray(v, np.float32).reshape(-1, 128)
        vec[:, VOFF[name]:VOFF[name] + v.shape[0]] = v.T

    put("nm0", f("norm_mix")[0]); put("nm1", f("norm_mix")[1])
    put("nf0", f("norm_ffn")[0]); put("nf1", f("norm_ffn")[1]); put("nfin", f("norm_final"))
    put("b_in", f("lru_b_in")[0])
    for k in range(4):
        put("cw%d" % k, f("lru_conv_w")[0, k])
    put("cb", f("lru_conv_b")[0]); put("gab", f("lru_gate_a_b")[0]); put("gxb", f("lru_gate_x_b")[0])
    put("lam", f("lru_lambda")[0]); put("bout", f("lru_b_out")[0]); put("bpw1", f("cf_b_pw1")[0])
    for k in range(31):
        put("dw%d" % k, f("cf_dw_w")[0, k])
    put("dwb", f("cf_dw_b")[0]); put("lng", f("cf_ln_g")[0]); put("lnb", f("cf_ln_b")[0]); put("bpw2", f("cf_b_pw2")[0])
    adab = np.ascontiguousarray(f("ada_b").reshape(2, 48, 128).transpose(2, 0, 1))
    adaw = np.ascontiguousarray(f("ada_w").reshape(2, KC, 128, 12, 512).transpose(0, 3, 2, 1, 4)).reshape(2, 12, 128, KC * 512)
    gw = np.stack([f("lru_gate_a_w")[0], f("lru_gate_x_w")[0]], 0)
    gw = np.ascontiguousarray(gw.transpose(2, 0, 1, 3)).reshape(128, 2 * 8 * 128)
    kk = np.stack([f("peer_k1")[0], f("peer_k2")[0], f("peer_k1")[1], f("peer_k2")[1]], 0)
    kT = np.ascontiguousarray(kk.transpose(2, 0, 1)).reshape(128, 4 * 128)
    wcat = np.concatenate([f("lru_w_in")[0], f("lru_w_out")[0], f("cf_w_pw1")[0], f("cf_w_pw2")[0], f("peer_w_q")[0], f("peer_w_q")[1]], axis=1)
    wd = np.ascontiguousarray(wcat.reshape(KC, 128, NPIECE, 256).transpose(2, 1, 0, 3)).reshape(-1, 2048)
    u_arr = np.ascontiguousarray(f("peer_u").reshape(2, 128, 128, KC, 128).transpose(0, 2, 4, 3, 1)).reshape(-1, 2048)
    v_arr = np.ascontiguousarray(f("peer_v").reshape(2, 128, 128, D).transpose(0, 2, 1, 3)).reshape(-1, 2048)
    consts = np.zeros((128, 5, 128), np.float32)
    consts[:, 4, :] = np.arange(128, dtype=np.uint32).view(np.float32)[None, :]
    consts[:, 3, 3] = np.array([0xFFFFFF80], dtype=np.uint32).view(np.float32)[0]
    consts[:, 3, 1] = 1.0
    consts[:, 3, 2] = EPS
    consts[:, 0, :] = np.eye(128, dtype=np.float32)
    consts[:, 1, :] = 1.0
    consts[:, 2, :] = np.arange(128, dtype=np.float32)[None, :]
    xp_, xs_ = f("x_prompt"), f("x_sample")
    cp_, cs_ = f("c_prompt"), f("c_sample")
    sh_, sc_, sd_ = f("state_lru_h"), f("state_lru_conv"), f("state_dwconv")
    in_maps = []
    for c in range(NCORES):
        xt = np.concatenate([xp_[NPS * c:NPS * c + NPS].reshape(-1, D), xs_[NSS * c:NSS * c + NSS].reshape(-1, D)], 0)
        cc = np.concatenate([cp_[NPS * c:NPS * c + NPS], cs_[NSS * c:NSS * c + NSS]], 0)
        in_maps.append(dict(
            xin=_fm(xt), cT=_fm(cc), st_h=np.ascontiguousarray(_fm(sh_[0, NSS * c:NSS * c + NSS]).transpose(0, 2, 1)),
            st_conv=_fm(sc_[0, NSS * c:NSS * c + NSS]), st_dw=_fm(sd_[0, NSS * c:NSS * c + NSS]),
            consts=consts, vec=vec, adab=adab, adaw=adaw, gw=gw, kT=kT, wd=wd, u_arr=u_arr, v_arr=v_arr))
    if "p" not in _PROG:
        _PROG["p"] = build_program()
    res = run_bass_kernel_spmd(_PROG["p"], in_maps, core_ids=list(range(NCORES)))
    B, DB = NPS * NCORES, NSS * NCORES
    y_p = np.zeros((B, SEQ, D), np.float32); y_s = np.zeros((DB, DSEQ, D), np.float32)
    h_p = np.zeros((1, B, D), np.float32); h_s = np.zeros((1, DB, D), np.float32)
    cv_p = np.zeros((1, B, 3, D), np.float32); cv_s = np.zeros((1, DB, 3, D), np.float32)
    dw_p = np.zeros((1, B, 30, D), np.float32); dw_s = np.zeros((1, DB, 30, D), np.float32)

    def unfm(a):
        nd = a.ndim
        perm = tuple(range(2, nd)) + (1, 0)
        a = a.transpose(perm)
        return a.reshape(a.shape[:-2] + (D,))

    for c in range(NCORES):
        r = res.results[c]
        y = unfm(np.asarray(r["yout"]))
        y_p[NPS * c:NPS * c + NPS] = y[:NPS * SEQ].reshape(NPS, SEQ, D)
        y_s[NSS * c:NSS * c + NSS] = y[NPS * SEQ:].reshape(NSS, DSEQ, D)
        h = unfm(np.ascontiguousarray(np.asarray(r["ho"]).transpose(0, 2, 1)))
        h_p[0, NPS * c:NPS * c + NPS] = h[:NPS]; h_s[0, NSS * c:NSS * c + NSS] = h[NPS:]
        cvv = unfm(np.asarray(r["convo"]))
        cv_p[0, NPS * c:NPS * c + NPS] = cvv[:NPS]; cv_s[0, NSS * c:NSS * c + NSS] = cvv[NPS:]
        dww = unfm(np.asarray(r["dwo"]))
        dw_p[0, NPS * c:NPS * c + NPS] = dww[:NPS]; dw_s[0, NSS * c:NSS * c + NSS] = dww[NPS:]
    return (y_p, y_s, h_p, cv_p, dw_p, h_s, cv_s, dw_s)
```

```python
import numpy as np
from contextlib import ExitStack
import concourse.bass as bass
import concourse.mybir as mybir
from concourse.bass_utils import run_bass_kernel_spmd

F32 = mybir.dt.float32
BF16 = mybir.dt.bfloat16
U32 = mybir.dt.uint32
ALU = mybir.AluOpType
AF = mybir.ActivationFunctionType
AX = mybir.AxisListType

NCORES = 8
D = 1024
KC = 8
TMAX = 256
SEQ = 2048
NPS = 4
NSS = 2
DSEQ = 32
NTOK = NPS * SEQ + NSS * DSEQ
EPS = 1e-6
G_UV = 2
SG = 8
NEG = -1.0e30
EMBED = True
import os
ONLY_SAMPLE = bool(int(os.environ.get("PEER_ONLY_SAMPLE", "0")))
DBG_TILES = int(os.environ.get("PEER_DBG_TILES", "0"))

VEC_SPEC = [("nm0", 8), ("nm1", 8), ("nf0", 8), ("nf1", 8), ("nfin", 8), ("b_in", 16),
            ("cw0", 8), ("cw1", 8), ("cw2", 8), ("cw3", 8), ("cb", 8), ("gab", 8), ("gxb", 8),
            ("lam", 8), ("bout", 8), ("bpw1", 16)] + [("dw%d" % k, 8) for k in range(31)] + \
           [("dwb", 8), ("lng", 8), ("lnb", 8), ("bpw2", 8)]
VOFF = {}
_o = 0
for _n, _w in VEC_SPEC:
    VOFF[_n] = _o
    _o += _w
NV = _o
PIECE = {"w_in": 0, "w_out": 8, "pw1": 12, "pw2": 20, "wq0": 24, "wq1": 32}
NPIECE = 40


class Tok:
    __slots__ = ("sem", "val")

    def __init__(self, sem, val):
        self.sem = sem
        self.val = val


class Buf:
    def __init__(self, name):
        self.name = name
        self.w = None
        self.r = {}


class Eng:
    def __init__(self, name):
        self.name = name
        self.sem = None
        self.cnt = 0
        self.seen = {}
        self.ops = []


class Sched:
    def __init__(self):
        self.pe = Eng("pe")
        self.act = Eng("act")
        self.dve = Eng("dve")
        self.pool = Eng("pool")
        self.sp = Eng("sp")
        self.engs = [self.pe, self.act, self.dve, self.pool, self.sp]
        self.dsem = {}
        self.semlist = []

    def _waits(self, E, reads, writes):
        waits = {}

        def need(tok, same_ok):
            if tok is None:
                return
            if tok.sem is E.sem and (same_ok or E is self.pe):
                return
            k = id(tok.sem)
            if k not in waits or waits[k][1] < tok.val:
                waits[k] = (tok.sem, tok.val)

        for b in reads:
            need(b.w, False)
        for b in writes:
            need(b.w, True)
            for sem, val in b.r.values():
                need(Tok(sem, val), True)
        wl = []
        for k, (sem, val) in waits.items():
            if E.seen.get(k, 0) < val:
                E.seen[k] = val
                wl.append((sem, val))
        return wl

    def _commit(self, tok, reads, writes):
        k = id(tok.sem)
        for b in reads:
            if k not in b.r or b.r[k][1] < tok.val:
                b.r[k] = (tok.sem, tok.val)
        for b in writes:
            b.w = tok
            b.r = {}

    def op(self, E, name, kw, reads=(), writes=()):
        if name == "activation" and "bias" not in kw:
            kw["bias"] = self.zero[:kw["in_"].shape[0]]
        return self.multi(E, [(name, kw)], reads, writes)

    def multi(self, E, insts, reads=(), writes=()):
        wl = self._waits(E, reads, writes)
        E.cnt += 1
        tok = Tok(E.sem, E.cnt)
        E.ops.append((wl, insts, E.sem, 1))
        self._commit(tok, reads, writes)
        return tok

    def dma(self, Q, key, out, in_, reads=(), writes=()):
        wl = self._waits(Q, reads, writes)
        ent = self.dsem[key]
        ent[1] += 16
        tok = Tok(ent[0], ent[1])
        Q.ops.append((wl, [("dma_start", dict(out=out, in_=in_))], ent[0], 16))
        self._commit(tok, reads, writes)
        return tok

    def fence(self, E, toks):
        wl = []
        for tok in toks:
            k = id(tok.sem)
            if E.seen.get(k, 0) < tok.val:
                E.seen[k] = tok.val
                wl.append((tok.sem, tok.val))
        if wl:
            E.ops.append((wl, [], None, 0))

    def replay(self, E, e):
        for wl, insts, sem, inc in E.ops:
            for s, v in wl:
                e.wait_ge(s, v)
            last = None
            for name, kw in insts:
                last = getattr(e, name)(**kw)
            if last is not None and sem is not None:
                last.then_inc(sem, inc)


def cap(full, off, dims, nparts=128):
    pstep = full.ap[0][0]
    return bass.AP(tensor=full.tensor, offset=off, ap=[[pstep, nparts]] + [list(d) for d in dims])


def dap(t, off, dims):
    return bass.AP(tensor=t.tensor, offset=off, ap=[list(d) for d in dims])


def build_program():
    nc = bass.Bass("TRN2", target_bir_lowering=False)
    K = Sched()
    PE, ACT, DVE, POOL, SP = K.pe, K.act, K.dve, K.pool, K.sp

    def din(name, shape, dt=F32):
        return nc.dram_tensor(name, list(shape), dt, kind="ExternalInput").ap()

    def dout(name, shape, dt=F32):
        return nc.dram_tensor(name, list(shape), dt, kind="ExternalOutput").ap()

    xin = din("xin", [128, KC, NTOK])
    cT = din("cT", [128, KC, 6])
    st_h = din("st_h", [128, NSS, KC])
    st_conv = din("st_conv", [128, KC, NSS, 3])
    st_dw = din("st_dw", [128, KC, NSS, 30])
    consts = din("consts", [128, 5, 128])
    vec_d = din("vec", [128, NV])
    adab_d = din("adab", [128, 2, 48])
    adaw_d = din("adaw", [2, 12, 128, KC * 512])
    gw_d = din("gw", [128, 2 * 8 * 128])
    kT_d = din("kT", [128, 4 * 128])
    wd_d = din("wd", [NPIECE * 128 * KC * 256 // 2048, 2048])
    u_d = din("u_arr", [2 * 128 * 128 * 1024 // 2048, 2048])
    v_d = din("v_arr", [2 * 128 * 128 * 1024 // 2048, 2048])
    wd_b = nc.dram_tensor("wd_b", [NPIECE * 128 * KC * 256 // 2048, 2048], BF16, kind="Internal").ap()
    u_b = nc.dram_tensor("u_b", [2 * 128 * 128 * 1024 // 2048, 2048], BF16, kind="Internal").ap()
    v_b = nc.dram_tensor("v_b", [2 * 128 * 128 * 1024 // 2048, 2048], BF16, kind="Internal").ap()
    yout = dout("yout", [128, KC, NTOK])
    ho = dout("ho", [128, 6, KC])
    convo = dout("convo", [128, KC, 6, 3])
    dwo = dout("dwo", [128, KC, 6, 30])

    es = ExitStack()
    with es:
        sb_off = [20480]

        def SBt(name, shape, dt, at=None):
            esz = 2 if dt == BF16 else 4
            n = 1
            for s in shape[1:]:
                n *= s
            nbytes = (n * esz + 63) // 64 * 64
            if at is None:
                at = sb_off[0]
                sb_off[0] += nbytes
            t = nc.alloc_sbuf_tensor_at(name, list(shape), dt, offset=at)
            return t.ap()

        T = TMAX
        cst_off = sb_off[0]
        cst = SBt("cst", [128, 5, 128], F32)
        cstu = SBt("cstu", [128, 5, 128], U32, at=cst_off)
        ZERO = cst[:, 3, 0:1]
        ONE = cst[:, 3, 1:2]
        EPSA = cst[:, 3, 2:3]
        K.zero = ZERO
        ident = cst[:, 0, :]
        ones = cst[:, 1, :]
        iota_f = cst[:, 2, :]
        iota_b = SBt("iota_b", [128, 128], BF16)
        iota_rep = SBt("iota_rep", [128, 128, SG], BF16)
        vec = SBt("vec", [128, NV], F32)
        adab = SBt("adab", [128, 2, 48], F32)
        modv = SBt("modv", [128, 2, 48, 6], F32)
        G1 = SBt("G1", [128, 2, 8, 6], F32)
        GB1 = SBt("GB1", [128, 2, 8, 6], F32)
        G2 = SBt("G2", [128, 2, 8, 6], F32)
        cA = SBt("cA", [128, 8], F32)
        c2A = SBt("c2A", [128, 8], F32)
        ptmp = SBt("ptmp", [128, 4, 8], F32)
        cTs = SBt("cTs", [128, KC, 6], F32)
        csil = SBt("csil", [128, KC, 6], F32)
        gwb = SBt("gwb", [128, 2, 8, 128], BF16)
        kTb = SBt("kTb", [128, 4, 128], BF16)
        hstate = SBt("hstate", [128, 8], F32)
        xp = SBt("xp", [128, KC, 3 + T], F32)
        xpc = SBt("xpc", [128, KC, 30 + T], F32)
        x = SBt("x", [128, KC, T], F32)
        hT = SBt("hT", [128, KC, T], BF16)
        sdt = SBt("sdt", [128, T], F32)
        rstd = SBt("rstd", [128, T], F32)
        tmpk = [SBt("tmpk%d" % i, [128, T], F32) for i in range(2)]
        wslot = [SBt("wslot%d" % i, [128, KC, 256], BF16) for i in range(5)]
        NSET = 4
        ubuf = [SBt("ubuf%d" % i, [128, 1024], BF16) for i in range(NSET)]
        vbuf = [SBt("vbuf%d" % i, [128, 1024], BF16) for i in range(NSET)]
        arena_off = sb_off[0]
        gate = SBt("gate", [128, KC, T], BF16)
        xc = SBt("xc", [128, KC, T], F32)
        xcb = SBt("xcb", [128, KC, T], BF16)
        rr = SBt("rr", [128, KC, T], F32)
        ii_off = sb_off[0]
        ii = SBt("ii", [128, KC, T], F32)
        ybuf = SBt("ybuf", [128, KC, T], F32, at=ii_off)
        aa_off = sb_off[0]
        aa = SBt("aa", [128, KC, T], F32)
        mm_ = SBt("mm", [128, KC, T], F32)
        yin_off = sb_off[0]
        yin = SBt("yin", [128, KC, T], BF16)
        rtmp = [SBt("rtmp%d" % i, [128, T], F32) for i in range(2)]
        sgm = rr
        dd = xc
        tsz = T * 4
        mu = SBt("mu", [128, T], F32, at=aa_off)
        musq = SBt("musq", [128, T], F32, at=aa_off + tsz)
        var = SBt("var", [128, T], F32, at=aa_off + 2 * tsz)
        sd2 = SBt("sd2", [128, T], F32, at=aa_off + 3 * tsz)
        rs2 = SBt("rs2", [128, T], F32, at=aa_off + 4 * tsz)
        t1 = [SBt("t1_%d" % i, [128, T], F32, at=aa_off + (5 + i) * tsz) for i in range(2)]
        dnb = SBt("dnb", [128, KC, T], BF16, at=yin_off)
        arena_end = sb_off[0]
        sb_off[0] = max(arena_end, arena_off + 128 * T * 2)
        qT_off = sb_off[0]
        qT = SBt("qT", [128, 16, T], BF16)
        sq = SBt("sq", [128, KC, T], F32, at=qT_off)
        s_off = sb_off[0]
        s_sb = SBt("s_sb", [128, 16, 128], F32)
        s2_off = sb_off[0]
        s2 = SBt("s2", [128, 16, 128], F32)
        s_sbu = SBt("s_sbu", [128, 16, 128], U32, at=s_off)
        eqt = SBt("eqt", [128, 8, 16, 16], F32, at=s_off)
        prod = SBt("prod", [128, 8, 16, 16], F32, at=s2_off)
        v1_off = sb_off[0]
        v1 = SBt("v1", [128, 16, 16], F32)
        v1u = SBt("v1u", [128, 16, 16], U32, at=v1_off)
        ix = SBt("ix", [128, 16, 16], U32)
        ixf = SBt("ixf", [128, 16, 16], F32)
        cand = SBt("cand", [128, 8, 256], F32, at=s_off)
        cand2 = SBt("cand2", [128, 8, 256], F32, at=s2_off)
        cv = SBt("cv", [128, 8, 16], F32)
        ci = SBt("ci", [128, 8, 16], U32)
        cab = SBt("cab", [128, 2, 128], U32)
        cabf = SBt("cabf", [128, 2, 128], F32)
        ge = SBt("ge", [128, 8, 16], F32)
        gs = SBt("gs", [128, 8], F32)
        gs2 = SBt("gs2", [128, 8], F32)
        idxg = SBt("idxg", [128, 3, 128], F32)
        idxT = SBt("idxT", [128, 3, T], BF16)
        oh = [[SBt("oh%d_e" % s, [128, 2, 128, SG], BF16), SBt("oh%d_a" % s, [128, 128, SG], BF16)] for s in range(3)]
        wsb_off = arena_off
        Wsb = SBt("Wsb", [128, T, 128], BF16, at=arena_off)
        assert arena_end - arena_off <= 128 * T * 2
        print("SBUF end", sb_off[0], "arena", arena_off, arena_end - arena_off)
        gz = [SBt("gz%d" % i, [128, T], BF16) for i in range(3)]
        wz = [SBt("wz%d" % i, [128, T], BF16) for i in range(3)]
        assert sb_off[0] <= 224 * 1024 - 256, sb_off[0]
        adas = [SBt("adas%d" % i, [128, KC, 512], F32, at=wsb_off + i * 16384) for i in range(2)]
        gws = SBt("gws", [128, 2 * 8 * 128], F32, at=s_off)
        kTs = SBt("kTs", [128, 4 * 128], F32, at=s2_off)

        PS = [es.enter_context(nc.psum_tensor("ps%d" % i, [128, 512], F32)) for i in range(8)]
        PSB = [Buf("ps%d" % i) for i in range(8)]
        for E in K.engs:
            E.sem = es.enter_context(nc.semaphore("sem_" + E.name))

        def dsem(key):
            K.dsem[key] = [es.enter_context(nc.semaphore("d_" + key)), 0]

        for key in ["ld_misc", "ld_x", "ld_w0", "ld_w1", "ld_w2", "ld_w3", "ld_w4", "ld_u0", "ld_u1", "ld_u2", "ld_u3", "ld_v0", "ld_v1", "ld_v2", "ld_v3",
                    "st_y", "st_c", "st_h", "st_d", "cast_w", "cast_u", "cast_v", "ld_a0", "ld_a1", "ld_st"]:
            dsem(key)

        rot = {"bank": 0, "w": 0, "uv": 0, "tmpk": 0, "rtmp": 0, "t1": 0, "gz": 0, "oh": 0, "cp": 0}

        def nbank():
            b = rot["bank"]
            rot["bank"] = (b + 1) % 4
            return b

        def nrot(key, n):
            v = rot[key]
            rot[key] = (v + 1) % n
            return v

        B = lambda n: Buf(n)
        cstB, vecB, adabB, modvB, GB_, cAB, csilB, gwbB, kTbB = B("cst"), B("vec"), B("adab"), B("modv"), B("G"), B("cA"), B("csil"), B("gwb"), B("kTb")
        iotabB, ptmpB, cTsB, gwsB, kTsB = B("iotab"), B("ptmp"), B("cTs"), B("gws"), B("kTs")
        hstB = [B("hst%d" % j) for j in range(8)]
        xpB = [B("xp%d" % j) for j in range(8)]
        xpcB = [B("xpc%d" % j) for j in range(8)]
        xB = [B("x%d" % j) for j in range(8)]
        hTB = [B("hT%d" % j) for j in range(8)]
        yB = B("ybuf")
        sqB = [B("sq%d" % j) for j in range(8)]
        sdtB, rstdB = B("sdt"), B("rstd")
        tmpkB = [B("tmpk0"), B("tmpk1")]
        wslotB = [B("ws%d" % i) for i in range(5)]
        ubufB = [B("ub%d" % i) for i in range(NSET)]
        vbufB = [B("vb%d" % i) for i in range(NSET)]
        gateB = [B("gate%d" % j) for j in range(8)]
        xcB = [B("xc%d" % j) for j in range(8)]
        xcbB = [B("xcb%d" % j) for j in range(8)]
        rrB = [B("rr%d" % j) for j in range(8)]
        iiB = [B("ii%d" % j) for j in range(8)]
        aaB = [B("aa%d" % j) for j in range(8)]
        mmB = [B("mm%d" % j) for j in range(8)]
        yinB = [B("yin%d" % j) for j in range(8)]
        rtmpB = [B("rtmp0"), B("rtmp1")]
        muB, musqB, varB, sd2B, rs2B = B("mu"), B("musq"), B("var"), B("sd2"), B("rs2")
        t1B = [B("t1_0"), B("t1_1")]
        arenaB = gateB + xcB + xcbB + rrB + iiB + aaB + mmB + yinB + rtmpB + [muB, musqB, varB, sd2B, rs2B] + t1B
        dnbB = yinB
        qTB = [B("qT%d" % j) for j in range(16)]
        ssbB = [B("ssb%d" % j) for j in range(4)]
        s2B = [B("s2_%d" % j) for j in range(4)]
        v1B = [B("v1_%d" % j) for j in range(16)]
        ixB = [B("ix_%d" % j) for j in range(16)]
        ixfB, candB, cvB, ciB, cabB, cabfB, geB, gsB, gs2B, idxgB = B("ixf"), [ssbB[h // 2] for h in range(8)], [B("cv%d" % h) for h in range(8)], [B("ci%d" % h) for h in range(8)], B("cab"), B("cabf"), B("ge"), B("gs"), B("gs2"), B("idxg")
        cand2B = [s2B[h // 2] for h in range(8)]
        idxTB = [B("idxT0"), B("idxT1")]
        ohB = [[B("oh%d_e" % s), B("oh%d_a" % s)] for s in range(3)]
        WsbB = B("Wsb")
        gzB = [B("gz%d" % i) for i in range(3)]
        wzB = [B("wz%d" % i) for i in range(3)]
        adasB = [B("adas0"), B("adas1")]
        wdbB, ubB, vbB = B("wd_b"), B("u_b"), B("v_b")
        store_toks = []

        def V(name, j=0):
            o = VOFF[name] + j
            return vec[:, o:o + 1]

        K.dma(SP, "ld_misc", cst[:], consts[:, :, :], writes=[cstB])
        K.dma(SP, "ld_misc", vec[:], vec_d[:, :], writes=[vecB])
        K.dma(SP, "ld_misc", adab[:], adab_d[:, :, :], writes=[adabB])
        K.dma(SP, "ld_misc", cTs[:], cT[:, :, :], writes=[cTsB])
        K.dma(SP, "ld_misc", gws[:], gw_d[:, :], writes=[gwsB])
        tk_misc = K.dma(SP, "ld_misc", kTs[:], kT_d[:, :], writes=[kTsB])
        for bb in (cstB, vecB, adabB, cTsB, gwsB, kTsB):
            bb.w = tk_misc
        R = 2048
        nrow = wd_d.shape[0]
        r0 = 0
        while r0 < nrow:
            n = min(R, nrow - r0)
            K.dma(POOL, "cast_w", wd_b[r0:r0 + n, :], wd_d[r0:r0 + n, :], writes=[wdbB])
            r0 += n
        for (src, dst, bb, ck) in ((u_d, u_b, ubB, "cast_u"), (v_d, v_b, vbB, "cast_v")):
            for r0 in range(0, src.shape[0], R):
                K.dma(POOL, ck, dst[r0:r0 + R, :], src[r0:r0 + R, :], writes=[bb])
        K.op(DVE, "tensor_copy", dict(out=iota_b[:], in_=iota_f), reads=[cstB], writes=[iotabB])
        K.op(DVE, "tensor_copy", dict(out=iota_rep[:], in_=cap(cst, 2 * 128, [[1, 128], [0, SG]])), reads=[cstB], writes=[iotabB])
        K.op(DVE, "tensor_copy", dict(out=gwb[:].rearrange("p a h j -> p (a h j)"), in_=gws[:]), reads=[gwsB], writes=[gwbB])
        K.op(DVE, "tensor_copy", dict(out=kTb[:].rearrange("p a k -> p (a k)"), in_=kTs[:]), reads=[kTsB], writes=[kTbB])
        K.op(ACT, "activation", dict(out=csil[:], in_=cTs[:], func=AF.Silu), reads=[cTsB], writes=[csilB])
        lam = vec[:, VOFF["lam"]:VOFF["lam"] + 8]
        K.op(DVE, "tensor_scalar", dict(out=ptmp[:, 3, :], in0=lam, scalar1=-1.0, scalar2=None, op0=ALU.mult), reads=[vecB], writes=[ptmpB])
        K.op(DVE, "tensor_tensor", dict(out=ptmp[:, 0, :], in0=lam, in1=ptmp[:, 3, :], op=ALU.max), reads=[vecB, ptmpB], writes=[ptmpB])
        K.op(ACT, "activation", dict(out=ptmp[:, 1, :], in_=ptmp[:, 0, :], func=AF.Exp, scale=-1.0), reads=[ptmpB], writes=[ptmpB])
        K.op(ACT, "activation", dict(out=ptmp[:, 2, :], in_=ptmp[:, 1, :], func=AF.Ln, bias=ONE), reads=[ptmpB], writes=[ptmpB])
        K.op(DVE, "tensor_scalar", dict(out=ptmp[:, 3, :], in0=lam, scalar1=-1.0, scalar2=0.0, op0=ALU.mult, op1=ALU.max), reads=[vecB, ptmpB], writes=[ptmpB])
        K.op(DVE, "tensor_tensor", dict(out=ptmp[:, 3, :], in0=ptmp[:, 3, :], in1=ptmp[:, 2, :], op=ALU.add), reads=[ptmpB], writes=[ptmpB])
        K.op(DVE, "tensor_scalar", dict(out=cA[:], in0=ptmp[:, 3, :], scalar1=-8.0, scalar2=None, op0=ALU.mult), reads=[ptmpB], writes=[cAB])
        K.op(DVE, "tensor_scalar", dict(out=c2A[:], in0=ptmp[:, 3, :], scalar1=-16.0, scalar2=None, op0=ALU.mult), reads=[ptmpB], writes=[cAB])
        for l in range(2):
            pb = nbank()
            for pc in range(12):
                sl = pc % 2
                K.dma(SP, "ld_a%d" % sl, adas[sl][:].rearrange("p k o -> p (k o)"), adaw_d[l, pc, :, :], writes=[adasB[sl]])
                for o4 in range(4):
                    oc = pc * 4 + o4
                    insts = [("matmul", dict(out=PS[pb][:, oc * 6:(oc + 1) * 6], lhsT=adas[sl][:, k, o4 * 128:(o4 + 1) * 128],
                                             rhs=csil[:, k, :], start=(k == 0), stop=(k == KC - 1))) for k in range(KC)]
                    K.multi(PE, insts, reads=[adasB[sl], csilB], writes=[PSB[pb]])
            K.op(DVE, "tensor_tensor", dict(out=modv[:, l, :, :], in0=PS[pb][:, 0:288].rearrange("p (o s) -> p o s", s=6),
                                            in1=cap(adab, l * 48, [[1, 48], [0, 6]]), op=ALU.add),
                 reads=[PSB[pb], adabB], writes=[modvB])
            nmn = "nm%d" % l
            nfn = "nf%d" % l
            bon = "bout" if l == 0 else "bpw2"
            K.op(DVE, "scalar_tensor_tensor", dict(out=G1[:, l, :, :], in0=modv[:, l, 8:16, :], scalar=1.0,
                                                   in1=cap(vec, VOFF[nmn], [[1, 8], [0, 6]]), op0=ALU.add, op1=ALU.mult),
                 reads=[modvB, vecB], writes=[GB_])
            K.op(DVE, "scalar_tensor_tensor", dict(out=G2[:, l, :, :], in0=modv[:, l, 32:40, :], scalar=1.0,
                                                   in1=cap(vec, VOFF[nfn], [[1, 8], [0, 6]]), op0=ALU.add, op1=ALU.mult),
                 reads=[modvB, vecB], writes=[GB_])
            K.op(DVE, "tensor_tensor", dict(out=GB1[:, l, :, :], in0=modv[:, l, 16:24, :],
                                            in1=cap(vec, VOFF[bon], [[1, 8], [0, 6]]), op=ALU.mult),
                 reads=[modvB, vecB], writes=[GB_])

        def MOD(l, oc, s):
            return modv[:, l, oc, s:s + 1]

        def load_piece(pc):
            sl = nrot("w", 5)
            src = dap(wd_b, pc * 128 * KC * 256, [[KC * 256, 128], [1, KC * 256]])
            K.dma(SP, "ld_w%d" % sl, wslot[sl][:].rearrange("p k o -> p (k o)"), src, reads=[wdbB], writes=[wslotB[sl]])
            return sl

        def stats(T_):
            for k in range(KC):
                K.op(ACT, "activation", dict(out=sq[:, k, :T_], in_=x[:, k, :T_], func=AF.Square), reads=[xB[k]], writes=[sqB[k], qTB[2 * k], qTB[2 * k + 1]])
            pb = nbank()
            insts = [("matmul", dict(out=PS[pb][:, :T_], lhsT=ones, rhs=sq[:, k, :T_], start=(k == 0), stop=(k == KC - 1))) for k in range(KC)]
            K.multi(PE, insts, reads=sqB + [cstB], writes=[PSB[pb]])
            K.op(ACT, "activation", dict(out=sdt[:, :T_], in_=PS[pb][:, :T_], func=AF.Sqrt, bias=EPSA, scale=1.0 / D), reads=[PSB[pb]], writes=[sdtB])
            K.op(DVE, "reciprocal", dict(out=rstd[:, :T_], in_=sdt[:, :T_]), reads=[sdtB], writes=[rstdB])

        def modnorm(T_, Gt, l, shoc, s):
            for k in range(KC):
                i = nrot("tmpk", 2)
                K.op(DVE, "scalar_tensor_tensor", dict(out=tmpk[i][:, :T_], in0=x[:, k, :T_], scalar=Gt[:, l, k, s:s + 1], in1=rstd[:, :T_],
                                                       op0=ALU.mult, op1=ALU.mult), reads=[xB[k], GB_, rstdB], writes=[tmpkB[i]])
                K.op(ACT, "activation", dict(out=hT[:, k, :T_], in_=tmpk[i][:, :T_], func=AF.Identity, bias=MOD(l, shoc + k, s)),
                     reads=[tmpkB[i], modvB], writes=[hTB[k]])

        def proj(T_, pc, oo, rhs_t, rhsB):
            sl = proj.cur
            pb = nbank()
            insts = [("matmul", dict(out=PS[pb][:, :T_], lhsT=wslot[sl][:, k, oo * 128:(oo + 1) * 128], rhs=rhs_t[:, k, :T_],
                                     start=(k == 0), stop=(k == KC - 1))) for k in range(KC)]
            K.multi(PE, insts, reads=[wslotB[sl]] + rhsB, writes=[PSB[pb]])
            return pb

        def resid_add(T_, pb, l, dc, s):
            i = nrot("rtmp", 2)
            K.op(ACT, "activation", dict(out=rtmp[i][:, :T_], in_=PS[pb][:, :T_], func=AF.Identity, bias=GB1[:, l, dc, s:s + 1], scale=MOD(l, 16 + dc, s)),
                 reads=[PSB[pb], GB_, modvB], writes=[rtmpB[i]])
            K.op(POOL, "tensor_tensor", dict(out=x[:, dc, :T_], in0=x[:, dc, :T_], in1=rtmp[i][:, :T_], op=ALU.add), reads=[rtmpB[i], xB[dc]], writes=[xB[dc]])

        def fence_bufs(engs, bufs):
            toks = []
            for b in bufs:
                if b.w is not None:
                    toks.append(b.w)
                toks += [Tok(sem, val) for sem, val in b.r.values()]
            for E in engs:
                K.fence(E, toks)

        def lru_stage(T_, s, first, last):
            fence_bufs([ACT, DVE, POOL], [WsbB])
            if first:
                if s < NPS:
                    for j in range(KC):
                        K.op(POOL, "memset", dict(ap=xp[:, j, 0:3], constant=0.0), writes=[xpB[j]])
                        K.op(POOL, "memset", dict(ap=hstate[:, j:j + 1], constant=0.0), writes=[hstB[j]])
                else:
                    K.dma(SP, "ld_st", xp[:, :, 0:3], st_conv[:, :, s - NPS, :], writes=xpB)
                    tk_st = K.dma(SP, "ld_st", hstate[:], st_h[:, s - NPS, :], writes=hstB)
                    for bb in xpB:
                        bb.w = tk_st
            stats(T_)
            modnorm(T_, G1, 0, 0, s)
            for pc in range(8):
                proj.cur = load_piece(PIECE["w_in"] + pc)
                for oo in range(2):
                    oc = pc * 2 + oo
                    pb = proj(T_, pc, oo, hT, hTB)
                    if oc < 8:
                        K.op(ACT, "activation", dict(out=gate[:, oc, :T_], in_=PS[pb][:, :T_], func=AF.Gelu_apprx_tanh, bias=V("b_in", oc)),
                             reads=[PSB[pb], vecB], writes=[gateB[oc]])
                    else:
                        j = oc - 8
                        K.op(ACT, "activation", dict(out=xp[:, j, 3:3 + T_], in_=PS[pb][:, :T_], func=AF.Identity, bias=V("b_in", oc)),
                             reads=[PSB[pb], vecB], writes=[xpB[j]])
            for j in range(KC):
                K.op(DVE, "tensor_scalar", dict(out=xc[:, j, :T_], in0=xp[:, j, 0:T_], scalar1=V("cw0", j), scalar2=V("cb", j), op0=ALU.mult, op1=ALU.add),
                     reads=[xpB[j], vecB], writes=[xcB[j]])
            for k in range(1, 4):
                for j in range(KC):
                    K.op(DVE, "scalar_tensor_tensor", dict(out=xc[:, j, :T_], in0=xp[:, j, k:k + T_], scalar=V("cw%d" % k, j), in1=xc[:, j, :T_],
                                                           op0=ALU.mult, op1=ALU.add), reads=[xpB[j], vecB, xcB[j]], writes=[xcB[j]])
            for j in range(KC):
                K.op(POOL, "tensor_copy", dict(out=xcb[:, j, :T_], in_=xc[:, j, :T_]), reads=[xcB[j]], writes=[xcbB[j]])
                if last:
                    pass
            if last:
                store_toks.append(K.dma(SP, "st_c", convo[:, :, s, :], xp[:, :, T_:T_ + 3], reads=xpB))
            for j in range(KC):
                K.op(POOL, "tensor_copy", dict(out=xp[:, j, 0:3], in_=xp[:, j, T_:T_ + 3]), reads=[xpB[j]], writes=[xpB[j]])
            for j in range(KC):
                for a_, dst, dstB, bn in ((0, rr, rrB, "gab"), (1, ii, iiB, "gxb")):
                    pb = nbank()
                    K.multi(PE, [("matmul", dict(out=PS[pb][:, :T_], lhsT=gwb[:, a_, j, :], rhs=xcb[:, j, :T_], start=True, stop=True))],
                            reads=[gwbB, xcbB[j]], writes=[PSB[pb]])
                    K.op(ACT, "activation", dict(out=dst[:, j, :T_], in_=PS[pb][:, :T_], func=AF.Sigmoid, bias=V(bn, j)),
                         reads=[PSB[pb], vecB], writes=[dstB[j], yB])
            for j in range(KC):
                K.op(ACT, "activation", dict(out=aa[:, j, :T_], in_=rr[:, j, :T_], func=AF.Exp, scale=cA[:, j:j + 1]), reads=[rrB[j], cAB], writes=[aaB[j]])
                K.op(ACT, "activation", dict(out=mm_[:, j, :T_], in_=rr[:, j, :T_], func=AF.Exp, scale=c2A[:, j:j + 1]), reads=[rrB[j], cAB], writes=[mmB[j]])
            for j in range(KC):
                K.op(ACT, "activation", dict(out=mm_[:, j, :T_], in_=mm_[:, j, :T_], func=AF.Sqrt, bias=ONE, scale=-1.0), reads=[mmB[j]], writes=[mmB[j]])
            for j in range(KC):
                K.op(DVE, "tensor_tensor", dict(out=mm_[:, j, :T_], in0=mm_[:, j, :T_], in1=ii[:, j, :T_], op=ALU.mult), reads=[mmB[j], iiB[j]], writes=[mmB[j]])
            for j in range(KC):
                K.op(DVE, "tensor_tensor", dict(out=mm_[:, j, :T_], in0=mm_[:, j, :T_], in1=xc[:, j, :T_], op=ALU.mult), reads=[mmB[j], xcB[j]], writes=[mmB[j]])
            for j in range(KC):
                K.op(DVE, "tensor_tensor_scan", dict(out=rr[:, j, :T_], data0=aa[:, j, :T_], data1=mm_[:, j, :T_], initial=hstate[:, j:j + 1],
                                                     op0=ALU.mult, op1=ALU.add), reads=[aaB[j], mmB[j], hstB[j]], writes=[rrB[j]])
            for j in range(KC):
                K.op(POOL, "tensor_copy", dict(out=hstate[:, j:j + 1], in_=rr[:, j, T_ - 1:T_]), reads=[rrB[j]], writes=[hstB[j]])
                K.op(DVE, "tensor_tensor", dict(out=yin[:, j, :T_], in0=rr[:, j, :T_], in1=gate[:, j, :T_], op=ALU.mult), reads=[rrB[j], gateB[j]], writes=[yinB[j]])
            if last:
                store_toks.append(K.dma(SP, "st_h", ho[:, s, :], hstate[:], reads=hstB))
            for pc in range(4):
                proj.cur = load_piece(PIECE["w_out"] + pc)
                for oo in range(2):
                    dc = pc * 2 + oo
                    pb = proj(T_, pc, oo, yin, yinB)
                    resid_add(T_, pb, 0, dc, s)

        def conf_stage(T_, s, first, last):
            fence_bufs([ACT, DVE, POOL], [WsbB])
            if first:
                if s < NPS:
                    for j in range(KC):
                        K.op(POOL, "memset", dict(ap=xpc[:, j, 0:30], constant=0.0), writes=[xpcB[j]])
                else:
                    K.dma(SP, "ld_st", xpc[:, :, 0:30], st_dw[:, :, s - NPS, :], writes=xpcB)
            stats(T_)
            modnorm(T_, G1, 1, 0, s)
            for pc in (4, 5, 6, 7, 0, 1, 2, 3):
                proj.cur = load_piece(PIECE["pw1"] + pc)
                for oo in range(2):
                    oc = pc * 2 + oo
                    pb = proj(T_, pc, oo, hT, hTB)
                    if oc >= 8:
                        j = oc - 8
                        K.op(ACT, "activation", dict(out=sgm[:, j, :T_], in_=PS[pb][:, :T_], func=AF.Sigmoid, bias=V("bpw1", oc)),
                             reads=[PSB[pb], vecB], writes=[rrB[j]])
                    else:
                        j = oc
                        K.op(DVE, "scalar_tensor_tensor", dict(out=xpc[:, j, 30:30 + T_], in0=PS[pb][:, :T_], scalar=V("bpw1", j), in1=sgm[:, j, :T_],
                                                               op0=ALU.add, op1=ALU.mult), reads=[PSB[pb], vecB, rrB[j]], writes=[xpcB[j]])
            for j in range(KC):
                K.op(DVE, "tensor_scalar", dict(out=dd[:, j, :T_], in0=xpc[:, j, 0:T_], scalar1=V("dw0", j), scalar2=V("dwb", j), op0=ALU.mult, op1=ALU.add),
                     reads=[xpcB[j], vecB], writes=[xcB[j]])
            for k in range(1, 31):
                for j in range(KC):
                    K.op(DVE, "scalar_tensor_tensor", dict(out=dd[:, j, :T_], in0=xpc[:, j, k:k + T_], scalar=V("dw%d" % k, j), in1=dd[:, j, :T_],
                                                           op0=ALU.mult, op1=ALU.add), reads=[xpcB[j], vecB, xcB[j]], writes=[xcB[j]])
            if last:
                store_toks.append(K.dma(SP, "st_d", dwo[:, :, s, :], xpc[:, :, T_:T_ + 30], reads=xpcB))
            for j in range(KC):
                K.op(POOL, "tensor_copy", dict(out=xpc[:, j, 0:30], in_=xpc[:, j, T_:T_ + 30]), reads=[xpcB[j]], writes=[xpcB[j]])
            for j in range(KC):
                K.op(ACT, "activation", dict(out=sq[:, j, :T_], in_=dd[:, j, :T_], func=AF.Square), reads=[xcB[j]], writes=[sqB[j], qTB[2 * j], qTB[2 * j + 1]])
            pa = nbank()
            K.multi(PE, [("matmul", dict(out=PS[pa][:, :T_], lhsT=ones, rhs=dd[:, j, :T_], start=(j == 0), stop=(j == KC - 1))) for j in range(KC)],
                    reads=xcB + [cstB], writes=[PSB[pa]])
            pb2 = nbank()
            K.multi(PE, [("matmul", dict(out=PS[pb2][:, :T_], lhsT=ones, rhs=sq[:, j, :T_], start=(j == 0), stop=(j == KC - 1))) for j in range(KC)],
                    reads=sqB + [cstB], writes=[PSB[pb2]])
            K.op(ACT, "activation", dict(out=mu[:, :T_], in_=PS[pa][:, :T_], func=AF.Identity, scale=1.0 / D), reads=[PSB[pa]], writes=[muB])
            K.op(DVE, "tensor_tensor", dict(out=musq[:, :T_], in0=mu[:, :T_], in1=mu[:, :T_], op=ALU.mult), reads=[muB], writes=[musqB])
            K.op(DVE, "scalar_tensor_tensor", dict(out=var[:, :T_], in0=PS[pb2][:, :T_], scalar=1.0 / D, in1=musq[:, :T_], op0=ALU.mult, op1=ALU.subtract),
                 reads=[PSB[pb2], musqB], writes=[varB])
            K.op(ACT, "activation", dict(out=sd2[:, :T_], in_=var[:, :T_], func=AF.Sqrt, bias=EPSA), reads=[varB], writes=[sd2B])
            K.op(DVE, "reciprocal", dict(out=rs2[:, :T_], in_=sd2[:, :T_]), reads=[sd2B], writes=[rs2B])
            for j in range(KC):
                i = nrot("t1", 2)
                K.op(POOL, "tensor_tensor", dict(out=t1[i][:, :T_], in0=dd[:, j, :T_], in1=mu[:, :T_], op=ALU.subtract), reads=[xcB[j], muB], writes=[t1B[i]])
                K.op(DVE, "tensor_tensor", dict(out=t1[i][:, :T_], in0=t1[i][:, :T_], in1=rs2[:, :T_], op=ALU.mult), reads=[t1B[i], rs2B], writes=[t1B[i]])
                K.op(ACT, "activation", dict(out=dnb[:, j, :T_], in_=t1[i][:, :T_], func=AF.Silu, bias=V("lnb", j), scale=V("lng", j)),
                     reads=[t1B[i], vecB], writes=[dnbB[j]])
            for pc in range(4):
                proj.cur = load_piece(PIECE["pw2"] + pc)
                for oo in range(2):
                    dc = pc * 2 + oo
                    pb = proj(T_, pc, oo, dnb, dnbB)
                    resid_add(T_, pb, 1, dc, s)

        def peer_stage(T_, s, l):
            stats(T_)
            modnorm(T_, G2, l, 24, s)
            for pc in range(8):
                proj.cur = load_piece(PIECE["wq%d" % l] + pc)
                for oo in range(2):
                    qc = pc * 2 + oo
                    pb = proj(T_, pc, oo, hT, hTB)
                    K.op(ACT, "activation", dict(out=qT[:, qc, :T_], in_=PS[pb][:, :T_], func=AF.Identity), reads=[PSB[pb]], writes=[qTB[qc], sqB[qc // 2]])
            ngr = (T_ + 127) // 128

            def topk_gen(g):
                g0 = g * 128
                gt = min(128, T_ - g0)
                yield
                for i4 in range(4):
                    pb = nbank()
                    insts = []
                    for i in range(4):
                        qc = i4 * 4 + i
                        insts.append(("matmul", dict(out=PS[pb][:gt, i * 128:(i + 1) * 128], lhsT=qT[:, qc, g0:g0 + gt], rhs=kTb[:, l * 2 + (qc % 2), :],
                                                     start=True, stop=True)))
                    K.multi(PE, insts, reads=[qTB[i4 * 4 + i] for i in range(4)] + [kTbB], writes=[PSB[pb]])
                    K.op(ACT, "activation", dict(out=s_sb[:gt, i4 * 4:(i4 + 1) * 4, :], in_=PS[pb][:gt, :].rearrange("p (a b) -> p a b", b=128), func=AF.Identity),
                         reads=[PSB[pb]], writes=[ssbB[i4]])
                yield
                if EMBED:
                    K.op(DVE, "scalar_tensor_tensor", dict(out=s_sbu[:gt], in0=s_sbu[:gt], scalar=cstu[:gt, 3, 3:4], in1=cap(cstu, 4 * 128, [[0, 16], [1, 128]], gt),
                                                           op0=ALU.bitwise_and, op1=ALU.bitwise_or), reads=ssbB + [cstB], writes=ssbB)
                    yield
                    for qc in range(16):
                        K.op(DVE, "max", dict(out=v1[:gt, qc, 0:8], in_=s_sb[:gt, qc, :]), reads=[ssbB[qc // 4]], writes=[v1B[qc]])
                    yield
                    for qc in range(16):
                        K.op(DVE, "match_replace", dict(out=s2[:gt, qc, :], in_to_replace=v1[:gt, qc, 0:8], in_values=s_sb[:gt, qc, :], imm_value=NEG),
                             reads=[ssbB[qc // 4], v1B[qc]], writes=[s2B[qc // 4]])
                    yield
                    for qc in range(16):
                        K.op(DVE, "max", dict(out=v1[:gt, qc, 8:16], in_=s2[:gt, qc, :]), reads=[s2B[qc // 4]], writes=[v1B[qc]])
                    yield
                    K.op(DVE, "tensor_scalar", dict(out=ix[:gt], in0=v1u[:gt], scalar1=127, scalar2=None, op0=ALU.bitwise_and), reads=v1B, writes=ixB)
                else:
                    for qc in range(16):
                        K.op(DVE, "max", dict(out=v1[:gt, qc, 0:8], in_=s_sb[:gt, qc, :]), reads=[ssbB[qc // 4]], writes=[v1B[qc]])
                    yield
                    for qc in range(16):
                        K.op(DVE, "max_index", dict(out=ix[:gt, qc, 0:8], in_max=v1[:gt, qc, 0:8], in_values=s_sb[:gt, qc, :]), reads=[ssbB[qc // 4], v1B[qc]], writes=[ixB[qc]])
                    yield
                    for qc in range(16):
                        K.op(DVE, "match_replace", dict(out=s2[:gt, qc, :], in_to_replace=v1[:gt, qc, 0:8], in_values=s_sb[:gt, qc, :], imm_value=NEG),
                             reads=[ssbB[qc // 4], v1B[qc]], writes=[s2B[qc // 4]])
                    yield
                    for qc in range(16):
                        K.op(DVE, "max", dict(out=v1[:gt, qc, 8:16], in_=s2[:gt, qc, :]), reads=[s2B[qc // 4]], writes=[v1B[qc]])
                    yield
                    for qc in range(16):
                        K.op(DVE, "max_index", dict(out=ix[:gt, qc, 8:16], in_max=v1[:gt, qc, 8:16], in_values=s2[:gt, qc, :]), reads=[s2B[qc // 4], v1B[qc]], writes=[ixB[qc]])
                K.op(DVE, "tensor_copy", dict(out=ixf[:gt], in_=ix[:gt]), reads=ixB, writes=[ixfB])
                K.op(DVE, "tensor_tensor", dict(out=cand[:gt].rearrange("p h (a b) -> p h a b", b=16),
                                                in0=cap(v1, 0, [[32, 8], [1, 16], [0, 16]], gt), in1=cap(v1, 16, [[32, 8], [0, 16], [1, 16]], gt), op=ALU.add),
                     reads=v1B, writes=candB)
                yield
                for h in range(8):
                    K.op(DVE, "max", dict(out=cv[:gt, h, 0:8], in_=cand[:gt, h, :]), reads=[candB[h]], writes=[cvB[h]])
                yield
                for h in range(8):
                    K.op(DVE, "max_index", dict(out=ci[:gt, h, 0:8], in_max=cv[:gt, h, 0:8], in_values=cand[:gt, h, :]), reads=[candB[h], cvB[h]], writes=[ciB[h]])
                yield
                for h in range(8):
                    K.op(DVE, "match_replace", dict(out=cand2[:gt, h, :], in_to_replace=cv[:gt, h, 0:8], in_values=cand[:gt, h, :], imm_value=NEG),
                         reads=[candB[h], cvB[h]], writes=[cand2B[h]])
                yield
                for h in range(8):
                    K.op(DVE, "max", dict(out=cv[:gt, h, 8:16], in_=cand2[:gt, h, :]), reads=[cand2B[h]], writes=[cvB[h]])
                yield
                for h in range(8):
                    K.op(DVE, "max_index", dict(out=ci[:gt, h, 8:16], in_max=cv[:gt, h, 8:16], in_values=cand2[:gt, h, :]), reads=[cand2B[h], cvB[h]], writes=[ciB[h]])
                K.op(DVE, "tensor_tensor", dict(out=ge[:gt], in0=cv[:gt], in1=cap(cv, 0, [[16, 8], [0, 16]], gt), op=ALU.subtract), reads=cvB, writes=[geB])
                K.op(ACT, "activation", dict(out=ge[:gt], in_=ge[:gt], func=AF.Exp), reads=[geB], writes=[geB])
                K.op(DVE, "tensor_reduce", dict(out=gs[:gt], in_=ge[:gt], axis=AX.X, op=ALU.add), reads=[geB], writes=[gsB])
                K.op(DVE, "reciprocal", dict(out=gs2[:gt], in_=gs[:gt]), reads=[gsB], writes=[gs2B])
                K.op(DVE, "tensor_tensor", dict(out=idxg[:gt, 2, :].rearrange("p (h k) -> p h k", k=16), in0=ge[:gt], in1=cap(gs2, 0, [[1, 8], [0, 16]], gt), op=ALU.mult),
                     reads=[geB, gs2B], writes=[idxgB])
                K.op(DVE, "tensor_scalar", dict(out=cab[:gt, 0, :], in0=ci[:gt].rearrange("p h k -> p (h k)"), scalar1=4, scalar2=None, op0=ALU.logical_shift_right), reads=ciB, writes=[cabB])
                K.op(DVE, "tensor_scalar", dict(out=cab[:gt, 1, :], in0=ci[:gt].rearrange("p h k -> p (h k)"), scalar1=15, scalar2=None, op0=ALU.bitwise_and), reads=ciB, writes=[cabB])
                K.op(DVE, "tensor_copy", dict(out=cabf[:gt], in_=cab[:gt]), reads=[cabB], writes=[cabfB])
                yield
                for hf in range(2):
                    K.op(DVE, "tensor_tensor", dict(out=eqt[:gt], in0=cap(cabf, hf * 128, [[16, 8], [1, 16], [0, 16]], gt),
                                                    in1=cap(cst, 2 * 128, [[0, 8], [0, 16], [1, 16]], gt), op=ALU.is_equal),
                         reads=[cabfB, cstB], writes=ssbB)
                    K.op(DVE, "tensor_tensor", dict(out=prod[:gt], in0=eqt[:gt], in1=cap(ixf, hf * 16, [[32, 8], [0, 16], [1, 16]], gt), op=ALU.mult),
                         reads=ssbB + [ixfB], writes=s2B)
                    K.op(DVE, "tensor_reduce", dict(out=idxg[:gt, hf, :], in_=prod[:gt].rearrange("p h a b -> p (h a) b"), axis=AX.X, op=ALU.add),
                         reads=s2B, writes=[idxgB])
                pb = nbank()
                insts = [("transpose", dict(out=PS[pb][:, i * 128:i * 128 + gt], in_=idxg[:gt, i, :], identity=cst[:gt, 0, :gt])) for i in range(3)]
                K.multi(PE, insts, reads=[idxgB, cstB], writes=[PSB[pb]])
                K.op(ACT, "activation", dict(out=idxT[:, :, g0:g0 + gt], in_=PS[pb][:, 0:384].rearrange("p (a b) -> p a b", b=128)[:, :, :gt], func=AF.Identity),
                     reads=[PSB[pb]], writes=[idxTB[g]])
                yield

            def wbuild_gen(g):
                g0 = g * 128
                gt = min(128, T_ - g0)
                if g == 0:
                    fence_bufs([ACT, DVE], arenaB + [yB])
                io = cap(iota_b, 0, [[0, SG], [1, 128]])

                def build(t0):
                    st = nrot("oh", 3)
                    eqab, Aoh = oh[st]
                    eqabB, AohB = ohB[st]
                    K.op(DVE, "tensor_tensor", dict(out=eqab[:], in0=cap(iota_rep, 0, [[0, 2], [SG, 128], [1, SG]]),
                                                    in1=cap(idxT, t0, [[T, 2], [0, 128], [1, SG]]), op=ALU.is_equal),
                         reads=[iotabB, idxTB[g]], writes=[eqabB])
                    K.op(POOL, "tensor_tensor", dict(out=Aoh[:], in0=eqab[:, 0, :, :], in1=cap(idxT, 2 * T + t0, [[0, 128], [1, SG]]), op=ALU.mult),
                         reads=[eqabB, idxTB[g]], writes=[AohB])
                    return st

                def mmev(t0, st):
                    eqab, Aoh = oh[st]
                    eqabB, AohB = ohB[st]
                    for q4 in range(SG // 4):
                        pb = nbank()
                        insts = [("matmul", dict(out=PS[pb][:, i * 128:(i + 1) * 128], lhsT=Aoh[:, :, q4 * 4 + i], rhs=eqab[:, 1, :, q4 * 4 + i], start=True, stop=True))
                                 for i in range(4)]
                        K.multi(PE, insts, reads=[AohB, eqabB], writes=[PSB[pb]])
                        tt = t0 + q4 * 4
                        dst = Wsb[:, tt:tt + 4, :]
                        src = PS[pb][:, :].rearrange("p (a b) -> p a b", b=128)
                        K.op(ACT, "activation", dict(out=dst, in_=src, func=AF.Identity), reads=[PSB[pb]], writes=[WsbB])

                prev = None
                for t0 in range(g0, g0 + gt, SG):
                    st = build(t0)
                    if prev is not None:
                        mmev(*prev)
                    prev = (t0, st)
                    yield
                mmev(*prev)
                yield

            for _ in topk_gen(0):
                pass
            for g in range(ngr):
                gens = [wbuild_gen(g)]
                if g + 1 < ngr:
                    gens.append(topk_gen(g + 1))
                while gens:
                    for gg in list(gens):
                        try:
                            next(gg)
                        except StopIteration:
                            gens.remove(gg)
            nb = min(8, 512 // T_)

            def OUT(dc):
                return PS[4 + dc // nb][:, (dc % nb) * T_:(dc % nb + 1) * T_]

            outB = PSB[4:8]

            def load_u(c, Q):
                st = c % NSET
                off = (l * 128 + c) * 128 * 1024
                K.dma(Q, "ld_u%d" % st, ubuf[st][:], dap(u_b, off, [[1024, 128], [1, 1024]]), reads=[ubB], writes=[ubufB[st]])

            def load_v(c):
                st = c % NSET
                off = (l * 128 + c) * 128 * 1024
                K.dma(SP, "ld_v%d" % st, vbuf[st][:], dap(v_b, off, [[1024, 128], [1, 1024]]), reads=[vbB], writes=[vbufB[st]])

            def zmm(c):
                st = c % NSET
                pb = nbank()
                insts = [("matmul", dict(out=PS[pb][:, :T_], lhsT=ubuf[st][:, k * 128:(k + 1) * 128], rhs=hT[:, k, :T_], start=(k == 0), stop=(k == KC - 1)))
                         for k in range(KC)]
                K.multi(PE, insts, reads=[ubufB[st]] + hTB, writes=[PSB[pb]])
                return pb

            for c in range(NSET):
                load_u(c, SP)
                load_v(c)
            pend = {0: zmm(0), 1: zmm(1)}
            for c in range(128):
                pb = pend.pop(c)
                i = nrot("gz", 3)
                K.op(ACT, "activation", dict(out=gz[i][:, :T_], in_=PS[pb][:, :T_], func=AF.Gelu_apprx_tanh), reads=[PSB[pb]], writes=[gzB[i]])
                if c + NSET < 128:
                    load_u(c + NSET, ACT)
                K.op(DVE, "tensor_tensor", dict(out=wz[i][:, :T_], in0=gz[i][:, :T_], in1=Wsb[:, 0:T_, c], op=ALU.mult), reads=[gzB[i], WsbB], writes=[wzB[i]])
                if c + 2 < 128:
                    pend[c + 2] = zmm(c + 2)
                st = c % NSET
                insts = [("matmul", dict(out=OUT(dc), lhsT=vbuf[st][:, dc * 128:(dc + 1) * 128], rhs=wz[i][:, :T_],
                                         start=(c == 0 and dc % nb == 0), stop=(c == 127))) for dc in range(KC)]
                K.multi(PE, insts, reads=[vbufB[st], wzB[i]], writes=outB)
                if c + NSET < 128:
                    load_v(c + NSET)
            for dc in range(KC):
                K.op(DVE, "scalar_tensor_tensor", dict(out=x[:, dc, :T_], in0=OUT(dc), scalar=MOD(l, 40 + dc, s), in1=x[:, dc, :T_], op0=ALU.mult, op1=ALU.add),
                     reads=outB + [modvB, xB[dc]], writes=[xB[dc]])

        tiles = []
        for s in range(NPS if not ONLY_SAMPLE else min(1, DBG_TILES)):
            nt = SEQ // T
            for j in range(nt if not ONLY_SAMPLE else DBG_TILES):
                tiles.append((s, s * SEQ + j * T, T, j == 0, j == nt - 1))
        for s in range(NSS):
            tiles.append((NPS + s, NPS * SEQ + s * DSEQ, DSEQ, True, True))
        for (s, tok0, T_, first, last) in tiles:
            K.dma(SP, "ld_x", x[:, :, :T_], xin[:, :, tok0:tok0 + T_], writes=xB)
            lru_stage(T_, s, first, last)
            peer_stage(T_, s, 0)
            conf_stage(T_, s, first, last)
            peer_stage(T_, s, 1)
            stats(T_)
            for k in range(KC):
                K.op(DVE, "scalar_tensor_tensor", dict(out=ybuf[:, k, :T_], in0=x[:, k, :T_], scalar=V("nfin", k), in1=rstd[:, :T_], op0=ALU.mult, op1=ALU.mult),
                     reads=[xB[k], vecB, rstdB], writes=[yB] + iiB)
            store_toks.append(K.dma(SP, "st_y", yout[:, :, tok0:tok0 + T_], ybuf[:, :, :T_], reads=[yB]))
        last_tok = {}
        for tk in store_toks:
            last_tok[id(tk.sem)] = tk
        K.fence(SP, list(last_tok.values()))

        with nc.Block() as block:
            @block.tensor
            def _(e):
                K.replay(PE, e)

            @block.scalar
            def _(e):
                K.replay(ACT, e)

            @block.vector
            def _(e):
                K.replay(DVE, e)

            @block.gpsimd
            def _(e):
                K.replay(POOL, e)

            @block.sync
            def _(e):
                K.replay(SP, e)
    return nc


_PROG = {}


def _fm(a):
    a = np.asarray(a, dtype=np.float32)
    lead = a.shape[:-1]
    a = a.reshape(lead + (KC, 128))
    nd = a.ndim
    perm = (nd - 1, nd - 2) + tuple(range(nd - 2))
    return np.ascontiguousarray(a.transpose(perm))


def kernel(**inp):
    f = lambda k: np.asarray(inp[k], dtype=np.float32)
    vec = np.zeros((128, NV), np.float32)

    def put(name, v):
        v = np.asarray(v, np.float32).reshape(-1, 128)
        vec[:, VOFF[name]:VOFF[name] + v.shape[0]] = v.T

    put("nm0", f("norm_mix")[0]); put("nm1", f("norm_mix")[1])
    put("nf0", f("norm_ffn")[0]); put("nf1", f("norm_ffn")[1]); put("nfin", f("norm_final"))
    put("b_in", f("lru_b_in")[0])
    for k in range(4):
        put("cw%d" % k, f("lru_conv_w")[0, k])
    put("cb", f("lru_conv_b")[0]); put("gab", f("lru_gate_a_b")[0]); put("gxb", f("lru_gate_x_b")[0])
    put("lam", f("lru_lambda")[0]); put("bout", f("lru_b_out")[0]); put("bpw1", f("cf_b_pw1")[0])
    for k in range(31):
        put("dw%d" % k, f("cf_dw_w")[0, k])
    put("dwb", f("cf_dw_b")[0]); put("lng", f("cf_ln_g")[0]); put("lnb", f("cf_ln_b")[0]); put("bpw2", f("cf_b_pw2")[0])
    adab = np.ascontiguousarray(f("ada_b").reshape(2, 48, 128).transpose(2, 0, 1))
    adaw = np.ascontiguousarray(f("ada_w").reshape(2, KC, 128, 12, 512).transpose(0, 3, 2, 1, 4)).reshape(2, 12, 128, KC * 512)
    gw = np.stack([f("lru_gate_a_w")[0], f("lru_gate_x_w")[0]], 0)
    gw = np.ascontiguousarray(gw.transpose(2, 0, 1, 3)).reshape(128, 2 * 8 * 128)
    kk = np.stack([f("peer_k1")[0], f("peer_k2")[0], f("peer_k1")[1], f("peer_k2")[1]], 0)
    kT = np.ascontiguousarray(kk.transpose(2, 0, 1)).reshape(128, 4 * 128)
    wcat = np.concatenate([f("lru_w_in")[0], f("lru_w_out")[0], f("cf_w_pw1")[0], f("cf_w_pw2")[0], f("peer_w_q")[0], f("peer_w_q")[1]], axis=1)
    wd = np.ascontiguousarray(wcat.reshape(KC, 128, NPIECE, 256).transpose(2, 1, 0, 3)).reshape(-1, 2048)
    u_arr = np.ascontiguousarray(f("peer_u").reshape(2, 128, 128, KC, 128).transpose(0, 2, 4, 3, 1)).reshape(-1, 2048)
    v_arr = np.ascontiguousarray(f("peer_v").reshape(2, 128, 128, D).transpose(0, 2, 1, 3)).reshape(-1, 2048)
    consts = np.zeros((128, 5, 128), np.float32)
    consts[:, 4, :] = np.arange(128, dtype=np.uint32).view(np.float32)[None, :]
    consts[:, 3, 3] = np.array([0xFFFFFF80], dtype=np.uint32).view(np.float32)[0]
    consts[:, 3, 1] = 1.0
    consts[:, 3, 2] = EPS
    consts[:, 0, :] = np.eye(128, dtype=np.float32)
    consts[:, 1, :] = 1.0
    consts[:, 2, :] = np.arange(128, dtype=np.float32)[None, :]
    xp_, xs_ = f("x_prompt"), f("x_sample")
    cp_, cs_ = f("c_prompt"), f("c_sample")
    sh_, sc_, sd_ = f("state_lru_h"), f("state_lru_conv"), f("state_dwconv")
    in_maps = []
    for c in range(NCORES):
        xt = np.concatenate([xp_[NPS * c:NPS * c + NPS].reshape(-1, D), xs_[NSS * c:NSS * c + NSS].reshape(-1, D)], 0)
        cc = np.concatenate([cp_[NPS * c:NPS * c + NPS], cs_[NSS * c:NSS * c + NSS]], 0)
        in_maps.append(dict(
            xin=_fm(xt), cT=_fm(cc), st_h=np.ascontiguousarray(_fm(sh_[0, NSS * c:NSS * c + NSS]).transpose(0, 2, 1)),
            st_conv=_fm(sc_[0, NSS * c:NSS * c + NSS]), st_dw=_fm(sd_[0, NSS * c:NSS * c + NSS]),
            consts=consts, vec=vec, adab=adab, adaw=adaw, gw=gw, kT=kT, wd=wd, u_arr=u_arr, v_arr=v_arr))
    if "p" not in _PROG:
        _PROG["p"] = build_program()
    res = run_bass_kernel_spmd(_PROG["p"], in_maps, core_ids=list(range(NCORES)))
    B, DB = NPS * NCORES, NSS * NCORES
    y_p = np.zeros((B, SEQ, D), np.float32); y_s = np.zeros((DB, DSEQ, D), np.float32)
    h_p = np.zeros((1, B, D), np.float32); h_s = np.zeros((1, DB, D), np.float32)
    cv_p = np.zeros((1, B, 3, D), np.float32); cv_s = np.zeros((1, DB, 3, D), np.float32)
    dw_p = np.zeros((1, B, 30, D), np.float32); dw_s = np.zeros((1, DB, 30, D), np.float32)

    def unfm(a):
        nd = a.ndim
        perm = tuple(range(2, nd)) + (1, 0)
        a = a.transpose(perm)
        return a.reshape(a.shape[:-2] + (D,))

    for c in range(NCORES):
        r = res.results[c]
        y = unfm(np.asarray(r["yout"]))
        y_p[NPS * c:NPS * c + NPS] = y[:NPS * SEQ].reshape(NPS, SEQ, D)
        y_s[NSS * c:NSS * c + NSS] = y[NPS * SEQ:].reshape(NSS, DSEQ, D)
        h = unfm(np.ascontiguousarray(np.asarray(r["ho"]).transpose(0, 2, 1)))
        h_p[0, NPS * c:NPS * c + NPS] = h[:NPS]; h_s[0, NSS * c:NSS * c + NSS] = h[NPS:]
        cvv = unfm(np.asarray(r["convo"]))
        cv_p[0, NPS * c:NPS * c + NPS] = cvv[:NPS]; cv_s[0, NSS * c:NSS * c + NSS] = cvv[NPS:]
        dww = unfm(np.asarray(r["dwo"]))
        dw_p[0, NPS * c:NPS * c + NPS] = dww[:NPS]; dw_s[0, NSS * c:NSS * c + NSS] = dww[NPS:]
    return (y_p, y_s, h_p, cv_p, dw_p, h_s, cv_s, dw_s)
```
